# Optimizing a Trainium2 kernel written in Bass

```python
import jax, jax.numpy as jnp
from jax import lax
import numpy as np

D_MODEL = 1024
BATCH = 2
SEQ = 8192
DEPTH = 2

HEAD_DIM = 64
D_MIX = D_MODEL
N_HEADS_FOX = D_MIX // (2 * HEAD_DIM)
N_HEADS_DIL = D_MIX // (2 * HEAD_DIM)
D_FOX = N_HEADS_FOX * HEAD_DIM
D_DIL = N_HEADS_DIL * HEAD_DIM
N_IN = 4 * D_FOX + N_HEADS_FOX + 4 * D_DIL
PLE_DIM = 256
ROPE_THETA = 500000.0
ROPE_DIM = HEAD_DIM // 4
DILATED_PATTERNS = ((128, 1), (512, 4), (2048, 16))
BLOCK = 128
EPS = 1e-6
NEG = -1e30
FORGET_BIAS_INIT = 3.0

kernel_name = "fox_dilated_hybrid_heads"


def rms_norm(x, g):
    xf = x.astype(jnp.float32)
    y = xf * lax.rsqrt(jnp.mean(xf * xf, axis=-1, keepdims=True) + EPS)
    return (y * g.astype(jnp.float32)).astype(x.dtype)


def partial_rope(x, positions):
    half = ROPE_DIM // 2
    inv_freq = ROPE_THETA ** (-jnp.arange(half, dtype=jnp.float32) / half)
    ang = positions.astype(jnp.float32)[..., None] * inv_freq
    cos = jnp.cos(ang)[:, :, None, :]
    sin = jnp.sin(ang)[:, :, None, :]
    xr = x[..., :ROPE_DIM].astype(jnp.float32)
    x1, x2 = xr[..., :half], xr[..., half:]
    rot = jnp.concatenate([x1 * cos - x2 * sin, x2 * cos + x1 * sin], axis=-1)
    return jnp.concatenate([rot.astype(x.dtype), x[..., ROPE_DIM:]], axis=-1)


def forgetting_attention(q, k, v, log_f):
    B, S, H, Dh = q.shape
    nb = S // BLOCK
    scale = Dh ** -0.5
    c = jnp.cumsum(log_f.astype(jnp.float32), axis=1).transpose(0, 2, 1)
    qb = q.reshape(B, nb, BLOCK, H, Dh).transpose(1, 0, 2, 3, 4)
    cb = c.reshape(B, H, nb, BLOCK).transpose(2, 0, 1, 3)
    key_pos = jnp.arange(S)

    def one_block(args):
        i, q_i, c_i = args
        s = jnp.einsum('bqhd,bkhd->bhqk', q_i, k, preferred_element_type=jnp.float32) * scale
        s = s + c_i[..., :, None] - c[:, :, None, :]
        q_pos = i * BLOCK + jnp.arange(BLOCK)
        causal = key_pos[None, :] <= q_pos[:, None]
        s = jnp.where(causal, s, NEG)
        prob = jax.nn.softmax(s, axis=-1)
        return jnp.einsum('bhqk,bkhd->bqhd', prob.astype(v.dtype), v)

    out = lax.map(one_block, (jnp.arange(nb), qb, cb))
    return out.transpose(1, 0, 2, 3, 4).reshape(B, S, H, Dh)


def dilated_branch(q, k, v, window, dilation):
    B, S, H, Dh = q.shape
    L = S // dilation
    n_back = window // dilation
    n_prev = -(-n_back // BLOCK)
    nb = -(-L // BLOCK)
    Lp = nb * BLOCK
    N = B * dilation
    KW = (n_prev + 1) * BLOCK

    def to_streams(x):
        x = x.reshape(B, L, dilation, H, Dh).transpose(0, 2, 1, 3, 4).reshape(N, L, H, Dh)
        return jnp.pad(x, ((0, 0), (0, Lp - L), (0, 0), (0, 0)))

    def key_band(x):
        xs = jnp.pad(to_streams(x), ((0, 0), (n_prev * BLOCK, 0), (0, 0), (0, 0)))
        xs = xs.reshape(N, nb + n_prev, BLOCK, H, Dh)
        return jnp.concatenate([xs[:, j:j + nb] for j in range(n_prev + 1)], axis=2)

    qs = to_streams(q).reshape(N, nb, BLOCK, H, Dh)
    kb, vb = key_band(k), key_band(v)
    s = jnp.einsum('nbqhd,nbkhd->nbhqk', qs, kb, preferred_element_type=jnp.float32) * (Dh ** -0.5)
    qq = jnp.arange(BLOCK)[:, None]
    kk = jnp.arange(KW)[None, :]
    dist = qq + n_prev * BLOCK - kk
    key_idx = jnp.arange(nb)[:, None, None] * BLOCK - n_prev * BLOCK + kk[None]
    valid = (dist >= 0) & (dist <= n_back) & (key_idx >= 0)
    s = jnp.where(valid[None, :, None], s, NEG)
    lse = jax.nn.logsumexp(s, axis=-1)
    prob = jnp.exp(s - lse[..., None])
    o = jnp.einsum('nbhqk,nbkhd->nbqhd', prob.astype(v.dtype), vb)

    def from_streams(x):
        tail = x.shape[3:]
        x = x.reshape((N, Lp) + tail)[:, :L]
        x = x.reshape((B, dilation, L) + tail).swapaxes(1, 2)
        return x.reshape((B, S) + tail)

    return from_streams(o), from_streams(lse.transpose(0, 1, 3, 2))


def dilated_attention(q, k, v):
    outs, lses = [], []
    for window, dilation in DILATED_PATTERNS:
        o, l = dilated_branch(q, k, v, window, dilation)
        outs.append(o)
        lses.append(l)
    w = jax.nn.softmax(jnp.stack(lses, axis=0), axis=0)
    o = jnp.stack(outs, axis=0).astype(jnp.float32)
    return jnp.sum(w[..., None] * o, axis=0).astype(q.dtype)


def hybrid_layer(h, p_i, positions, norm_g, w_in, b_f, qk_g, w_out, w_ple, ple_norm_g, w_ple_gate):
    B, S, _ = h.shape
    u = rms_norm(h, norm_g)
    z = u @ w_in
    cuts = np.cumsum([D_FOX, D_FOX, D_FOX, D_FOX, N_HEADS_FOX, D_DIL, D_DIL, D_DIL])
    qa, ka, va, ga, fa, qb, kb, vb, gb = jnp.split(z, cuts.tolist(), axis=-1)
    heads = lambda t, n: t.reshape(B, S, n, HEAD_DIM)

    qa = rms_norm(heads(qa, N_HEADS_FOX), qk_g[0])
    ka = rms_norm(heads(ka, N_HEADS_FOX), qk_g[1])
    log_f = jax.nn.log_sigmoid(fa.astype(jnp.float32) + b_f.astype(jnp.float32))
    oa = forgetting_attention(qa, ka, heads(va, N_HEADS_FOX), log_f).reshape(B, S, D_FOX)
    oa = oa * jax.nn.silu(ga)

    qb = partial_rope(rms_norm(heads(qb, N_HEADS_DIL), qk_g[2]), positions)
    kb = partial_rope(rms_norm(heads(kb, N_HEADS_DIL), qk_g[3]), positions)
    ob = dilated_attention(qb, kb, heads(vb, N_HEADS_DIL)).reshape(B, S, D_DIL)
    ob = ob * jax.nn.silu(gb)

    h = h + jnp.concatenate([oa, ob], axis=-1) @ w_out

    gate = jax.nn.sigmoid(rms_norm(h, ple_norm_g) @ w_ple_gate)
    return h + (p_i @ w_ple) * gate


def setup_inputs(seed: int = 0) -> dict:
    key = jax.random.key(seed)
    ks = jax.random.split(key, 12)
    f32 = jnp.float32
    x = jax.random.normal(ks[0], (BATCH, SEQ, D_MODEL), f32)
    p = jax.random.normal(ks[1], (DEPTH, BATCH, SEQ, PLE_DIM), f32)
    positions = jnp.broadcast_to(jnp.arange(SEQ, dtype=jnp.int32), (BATCH, SEQ))
    norm_g = 1.0 + 0.05 * jax.random.normal(ks[2], (DEPTH, D_MODEL), f32)
    w_in = jax.random.normal(ks[3], (DEPTH, D_MODEL, N_IN), f32) * D_MODEL ** -0.5
    b_f = FORGET_BIAS_INIT + 0.1 * jax.random.normal(ks[4], (DEPTH, N_HEADS_FOX), f32)
    qk_norm_g = 1.0 + 0.05 * jax.random.normal(ks[5], (DEPTH, 4, HEAD_DIM), f32)
    w_out = jax.random.normal(ks[6], (DEPTH, D_MIX, D_MODEL), f32) * D_MIX ** -0.5
    w_ple = jax.random.normal(ks[7], (DEPTH, PLE_DIM, D_MODEL), f32) * PLE_DIM ** -0.5
    ple_norm_g = 1.0 + 0.05 * jax.random.normal(ks[8], (DEPTH, D_MODEL), f32)
    w_ple_gate = jax.random.normal(ks[9], (DEPTH, D_MODEL, D_MODEL), f32) * D_MODEL ** -0.5
    return {"x": x, "p": p, "positions": positions, "norm_g": norm_g, "w_in": w_in,
            "b_f": b_f, "qk_norm_g": qk_norm_g, "w_out": w_out, "w_ple": w_ple,
            "ple_norm_g": ple_norm_g, "w_ple_gate": w_ple_gate}


def reference(x, p, positions, norm_g, w_in, b_f, qk_norm_g, w_out, w_ple, ple_norm_g, w_ple_gate):
    h = x
    for i in range(DEPTH):
        h = hybrid_layer(h, p[i], positions, norm_g[i], w_in[i], b_f[i], qk_norm_g[i],
                         w_out[i], w_ple[i], ple_norm_g[i], w_ple_gate[i])
    return h
```

```python
import numpy as np
from contextlib import ExitStack
import concourse.bass as bass
import concourse.mybir as mybir
from concourse.bass_utils import run_bass_kernel_spmd

F32 = mybir.dt.float32
BF16 = mybir.dt.bfloat16
I32 = mybir.dt.int32
AF = mybir.ActivationFunctionType
ALU = mybir.AluOpType
AX = mybir.AxisListType

NCORES = 8
S = 8192
D = 1024
TSH = 2048
NEG = -30000.0
EPS = 1e-6
TWO_PI = float(2 * np.pi)
LIM = {"nch": 16, "nsb": 4, "attn": True, "stage1": 99, "sub": 99, "gather": True}


class Buf:
    __slots__ = ("name", "writer", "readers", "dsem", "dcount", "persist")

    def __init__(self, name="", persist=False):
        self.name = name
        self.persist = persist
        self.writer = None
        self.readers = {}
        self.dsem = None
        self.dcount = 0


class Eng:
    def __init__(self, ctx, name, raw, is_pe=False):
        self.name = name
        self.raw = raw
        self.sem = ctx.new_sem("e_" + name)
        self.count = 0
        self.seen = {}
        self.is_pe = is_pe


class Ctx:
    def __init__(self, nc, stack):
        self.nc = nc
        self.stack = stack
        self.sems = {}
        self.nsem = 0
        self.pe = Eng(self, "pe", nc.tensor, is_pe=True)
        self.act = Eng(self, "act", nc.scalar)
        self.dve = Eng(self, "dve", nc.vector)
        self.pool = Eng(self, "pool", nc.gpsimd)
        self.sp = Eng(self, "sp", nc.sync)
        self.dbufs = []
        self.free_sems = []
        self.scope_bufs = []

    def new_sem(self, name):
        self.nsem += 1
        s = self.stack.enter_context(self.nc.semaphore(f"{name}_{self.nsem}"))
        self.sems[id(s)] = s
        return s

    def _wait(self, eng, deps):
        for sem, val in deps:
            k = id(sem)
            if eng.seen.get(k, 0) < val:
                eng.raw.wait_ge(sem, val)
                eng.seen[k] = val

    def _deps(self, r, w, waw=True):
        deps = []
        for b in r:
            if b.writer is not None:
                deps.append(b.writer)
        for b in w:
            if b.writer is not None and waw:
                deps.append(b.writer)
            for k, v in b.readers.items():
                deps.append((self.sems[k], v))
        return deps

    def op(self, eng, fn, r=(), w=()):
        own = id(eng.sem)
        if eng.is_pe:
            deps = [d for d in self._deps(r, w) if id(d[0]) != own]
        else:
            deps = self._deps(r, w)
        self._wait(eng, deps)
        ins = fn()
        ins.then_inc(eng.sem, 1)
        eng.count += 1
        for b in r:
            b.readers[own] = eng.count
        for b in w:
            b.writer = (eng.sem, eng.count)
            b.readers = {}
        return ins

    def dma(self, q, out_ap, in_ap, r=(), w=(), waw=True, **kw):
        deps = self._deps(r, w, waw=waw)
        self._wait(q, deps)
        dst = w[0]
        if dst.dsem is None:
            if self.free_sems and q is not self.pool:
                dst.dsem, dst.dcount = self.free_sems.pop()
            else:
                dst.dsem = self.new_sem("d_" + dst.name)
            self.dbufs.append(dst)
            if not dst.persist:
                self.scope_bufs.append(dst)
        ins = q.raw.dma_start(out=out_ap, in_=in_ap, **kw)
        ins.then_inc(dst.dsem, 16)
        dst.dcount += 16
        k = id(dst.dsem)
        for b in r:
            b.readers[k] = dst.dcount
        old_readers = dst.readers if not waw else {}
        dst.writer = (dst.dsem, dst.dcount)
        dst.readers = {}
        return ins

    def allgather(self, src_ap, src_b, dst_ap, dst_b):
        q = self.pool
        self._wait(q, self._deps(src_b, [dst_b]))
        if dst_b.dsem is None:
            dst_b.dsem = self.new_sem("cc_" + dst_b.name)
        ins = self.nc.gpsimd.collective_compute(
            "AllGather", ALU.bypass, replica_groups=[[0, 1, 2, 3], [4, 5, 6, 7]],
            ins=[src_ap.opt()], outs=[dst_ap.opt()])
        ins.then_inc(dst_b.dsem, 1)
        dst_b.dcount += 1
        for sb_ in src_b:
            sb_.readers[id(dst_b.dsem)] = dst_b.dcount
        dst_b.writer = (dst_b.dsem, dst_b.dcount)
        dst_b.readers = {}

    def barrier(self):
        engs = [self.pe, self.act, self.dve, self.pool, self.sp]
        deps = [(e.sem, e.count) for e in engs if e.count > 0]
        deps += [(b.dsem, b.dcount) for b in self.dbufs if b.dcount > 0]
        for e in engs:
            self._wait(e, [d for d in deps if id(d[0]) != id(e.sem) or not e.is_pe])

    def recycle(self):
        for b in self.scope_bufs:
            self.free_sems.append((b.dsem, b.dcount))
            self.dbufs.remove(b)
            b.dsem = None
        self.scope_bufs = []

    def wait_buf(self, eng, b):
        if b.writer is not None:
            self._wait(eng, [b.writer])


class T:
    def __init__(self, t, name, nb=1):
        self.t = t
        self.b = Buf(name)
        self.bs = [Buf(f"{name}{i}") for i in range(nb)]

    def __getitem__(self, k):
        return self.t[k]


_UID = [0]


def _un(name):
    _UID[0] += 1
    return f"{name}_{_UID[0]}"


def sb(st, nc, name, shape, dt, nb=1):
    name = _un(name)
    return T(st.enter_context(nc.sbuf_tensor(name, shape, dt)), name, nb)


def ps(st, nc, name, shape, dt, nb=1):
    name = _un(name)
    return T(st.enter_context(nc.psum_tensor(name, shape, dt)), name, nb)


class Consts:
    pass


def load_consts(c, nc, st, G):
    K = Consts()
    cf = sb(st, nc, "cf", [128, 392], F32)
    c.dma(c.sp, cf[:, :], G["consts"][:, :], r=[], w=[cf.b])
    K.ident = sb(st, nc, "ident", [128, 128], BF16)
    K.mge = sb(st, nc, "mge", [128, 128], BF16)
    K.mle = sb(st, nc, "mle", [128, 128], BF16)
    K.identf = sb(st, nc, "identf", [128, 128], F32)
    K.tri = sb(st, nc, "tri", [128, 128], F32)
    K.l127 = sb(st, nc, "l127", [128, 128], F32)
    K.invf = sb(st, nc, "invf", [128, 8], F32)
    K.m01 = sb(st, nc, "m01", [128, 256], BF16)
    K.eps = sb(st, nc, "epsc", [128, 1], F32)
    K.one = sb(st, nc, "onec", [128, 1], F32)
    K.negpi = sb(st, nc, "negpic", [128, 1], F32)
    c.op(c.dve, lambda: nc.vector.tensor_copy(K.ident[:, :], cf[:, 0:128]), r=[cf.b], w=[K.ident.b])
    c.op(c.dve, lambda: nc.vector.tensor_copy(K.mge[:, :], cf[:, 128:256]), r=[cf.b], w=[K.mge.b])
    c.op(c.dve, lambda: nc.vector.tensor_copy(K.mle[:, :], cf[:, 256:384]), r=[cf.b], w=[K.mle.b])
    c.op(c.dve, lambda: nc.vector.tensor_copy(K.invf[:, :], cf[:, 384:392]), r=[cf.b], w=[K.invf.b])
    c.op(c.dve, lambda: nc.vector.tensor_copy(K.identf[:, :], cf[:, 0:128]), r=[cf.b], w=[K.identf.b])
    c.op(c.dve, lambda: nc.vector.tensor_scalar(K.tri[:, :], cf[:, 128:256], 0.0, None, ALU.is_equal), r=[cf.b], w=[K.tri.b])
    c.op(c.dve, lambda: nc.vector.tensor_scalar(K.m01[:, 0:128], cf[:, 256:384], 0.0, None, ALU.is_equal), r=[cf.b], w=[K.m01.b])
    c.op(c.dve, lambda: nc.vector.tensor_scalar(K.m01[:, 128:256], cf[:, 128:256], 0.0, None, ALU.is_equal), r=[cf.b], w=[K.m01.b])
    c.op(c.dve, lambda: nc.vector.memset(K.l127[:, :], 0.0), w=[K.l127.b])
    c.op(c.dve, lambda: nc.vector.memset(K.l127[96:128, :], 1.0), w=[K.l127.b])
    c.op(c.dve, lambda: nc.vector.tensor_scalar(K.l127[:, :], K.l127[:, :], cf[:, 127:128], None, ALU.mult), r=[K.l127.b, cf.b], w=[K.l127.b])
    c.op(c.dve, lambda: nc.vector.memset(K.eps[:, :], EPS), w=[K.eps.b])
    c.op(c.dve, lambda: nc.vector.memset(K.one[:, :], 1.0), w=[K.one.b])
    c.op(c.dve, lambda: nc.vector.memset(K.negpi[:, :], -float(np.pi)), w=[K.negpi.b])
    return K


def rstd_from_ss(c, nc, ss, rs, n, K, inv_n):
    c.op(c.act, lambda: nc.scalar.activation(rs[:, 0:n], ss[:, 0:n], AF.Ln, bias=K.eps[:, 0:1], scale=inv_n),
         r=[ss.b, K.eps.b], w=[rs.b])
    c.op(c.act, lambda: nc.scalar.activation(rs[:, 0:n], rs[:, 0:n], AF.Exp, scale=-0.5), r=[rs.b], w=[rs.b])


class NormT:
    def __init__(self, c, nc, st, K, name):
        self.c, self.nc, self.K = c, nc, K
        self.sq = sb(st, nc, name + "sq", [128, 1024], BF16)
        self.ss = [sb(st, nc, name + f"ss{i}", [128, 1], F32) for i in range(2)]
        self.rs = [sb(st, nc, name + f"rs{i}", [128, 1], F32) for i in range(2)]
        self.u = [sb(st, nc, name + f"u{i}", [128, 1024], BF16) for i in range(2)]
        self.tp = [ps(st, nc, name + f"tp{i}", [128, 1024], BF16) for i in range(1)]
        self.stg = [sb(st, nc, name + f"stg{i}", [128, 8, 512], BF16) for i in range(2)]
        self.n = 0

    def emit(self, h_ap, h_buf, g_t, t, uT_dram, uT_buf, uT_all=None, uT_all_b=None):
        c, nc, K = self.c, self.nc, self.K
        i = self.n % 2
        self.n += 1
        ss, rs, u, tp = self.ss[i], self.rs[i], self.u[i], self.tp[0]
        stg = self.stg[(t // 4) % 2]
        c.op(c.act, lambda: nc.scalar.activation(self.sq[:, :], h_ap, AF.Square, accum_out=ss[:, 0:1]),
             r=[h_buf], w=[self.sq.b, ss.b])
        rstd_from_ss(c, nc, ss, rs, 1, K, 1.0 / D)
        c.op(c.dve, lambda: nc.vector.scalar_tensor_tensor(u[:, :], h_ap, rs[:, 0:1], g_t[:, :], ALU.mult, ALU.mult),
             r=[h_buf, rs.b, g_t.b], w=[u.b])
        for k in range(8):
            c.op(c.pe, lambda k=k: nc.tensor.transpose(tp[:, k * 128:(k + 1) * 128], u[:, k * 128:(k + 1) * 128], K.ident[:, :]),
                 r=[u.b, K.ident.b], w=[tp.b])
        tt = t % 4
        c.op(c.act, lambda: nc.scalar.copy(stg[:, :, tt * 128:(tt + 1) * 128], tp[:, :].rearrange("p (k t) -> p k t", k=8)),
             r=[tp.b], w=[stg.b])
        if tt == 3:
            ch = t // 4
            dst = uT_dram[ch].ap().rearrange("(k p) t -> p k t", p=128)
            c.dma(c.sp, dst, stg[:, :, :], r=[stg.b], w=[uT_buf[ch]])
            if uT_all is not None and LIM["gather"]:
                c.allgather(uT_dram[ch].ap(), [uT_buf[ch]], uT_all[ch].ap(), uT_all_b[ch])


def phase_norm0(c, nc, K, G):
    with ExitStack() as st:
        g = sb(st, nc, "n0g", [128, 1024], F32)
        c.dma(c.sp, g[:, :], G["smalls"][:, 0:1024], r=[], w=[g.b])
        xin = [sb(st, nc, f"n0x{i}", [128, 1024], F32) for i in range(2)]
        nt = NormT(c, nc, st, K, "n0")
        for t in range(16):
            x = xin[t % 2]
            c.dma(c.sp, x[:, :], G["x_sh"][t * 128:(t + 1) * 128, :], r=[], w=[x.b])
            nt.emit(x[:, :], x.b, g, t, G["uT_loc"][0], G["uT_loc_b"][0], G["uT_all"][0], G["uT_all_b"][0])
        c.barrier()


def qk_norm_tile(c, nc, K, W, zp, ncol, gq, slot):
    sq, ss, rs, qf, qg = W["sq"][slot], W["ss"][slot], W["rs"][slot], W["qf"][slot], W["qg"][slot]
    c.op(c.act, lambda: nc.scalar.activation(sq[:, :], zp[:, 0:256], AF.Square), r=[zp.b], w=[sq.b])
    c.op(c.dve, lambda: nc.vector.tensor_reduce(ss[:, 0:4], sq[:, :].rearrange("p (g d) -> p g d", g=4), AX.X, ALU.add),
         r=[sq.b], w=[ss.b])
    rstd_from_ss(c, nc, ss, rs, 4, K, 1.0 / 64)
    c.op(c.dve, lambda: nc.vector.tensor_tensor(qf[:, :].rearrange("p (g d) -> p g d", g=4),
                                                 zp[:, 0:256].rearrange("p (g d) -> p g d", g=4),
                                                 rs[:, 0:4].unsqueeze(2).to_broadcast([128, 4, 64]), ALU.mult),
         r=[zp.b, rs.b], w=[qf.b])
    c.op(c.dve, lambda: nc.vector.tensor_tensor(qg[:, :], qf[:, :], gq[:, :], ALU.mult), r=[qf.b, gq.b], w=[qg.b])
    return qg


def qk_norm_chunk(c, nc, K, zqk, sq, ss, rs, qf, zin=None):
    if zin is None:
        zin = zqk[:, :, :]
    c.op(c.act, lambda: nc.scalar.activation(sq[:, :].rearrange("p (t n) -> p t n", t=4), zin, AF.Square), r=[zqk.b], w=[sq.b])
    c.op(c.dve, lambda: nc.vector.tensor_reduce(ss[:, 0:16], sq[:, :].rearrange("p (a d) -> p a d", a=16), AX.X, ALU.add),
         r=[sq.b], w=[ss.b])
    rstd_from_ss(c, nc, ss, rs, 16, K, 1.0 / 64)
    c.op(c.dve, lambda: nc.vector.tensor_tensor(qf[:, :].rearrange("p (t g d) -> p t g d", t=4, g=4),
                                                 zin.rearrange("p t (g d) -> p t g d", g=4),
                                                 rs[:, 0:16].rearrange("p (t g) -> p t g", t=4).unsqueeze(3).to_broadcast([128, 4, 4, 64]), ALU.mult),
         r=[zqk.b, rs.b], w=[qf.b])
    return qf


def silu_gate(c, nc, K, gp, e, gdst_ap, gdst_b):
    c.op(c.act, lambda: nc.scalar.activation(e[:, :], gp[:, :], AF.Exp, scale=-1.0), r=[gp.b], w=[e.b])
    c.op(c.act, lambda: nc.scalar.activation(e[:, :], e[:, :], AF.Ln, bias=K.one[:, 0:1], scale=1.0), r=[e.b, K.one.b], w=[e.b])
    c.op(c.act, lambda: nc.scalar.activation(e[:, :], e[:, :], AF.Exp, scale=-1.0), r=[e.b], w=[e.b])
    c.op(c.dve, lambda: nc.vector.tensor_tensor(gdst_ap, gp[:, :], e[:, :], ALU.mult), r=[gp.b, e.b], w=[gdst_b])


def load_uT_chunk(c, nc, G, l, UT, ch):
    ut = UT[ch % 2]
    src = G["uT_all"][l][ch % 4].ap().rearrange("(r k p) t -> p r k t", r=4, k=8, p=128)[:, ch // 4, :, :]
    c.dma(c.sp, ut[:, :, :], src, r=[G["uT_all_b"][l][ch % 4]], w=[ut.b])
    return ut


def finalize_out(c, nc, W, acc, h, ncols, Gt, gcol0, ot_dram_rows, G, l, tok0, slot, bslot):
    oh = slice(0, 64) if h == 0 else slice(64, 128)
    lh = slice(64, 128) if h == 0 else slice(0, 64)
    rl, osb, ot = W["rl"][slot], W["os"][slot], W["ot"][slot]
    c.op(c.dve, lambda: nc.vector.tensor_copy(rl[oh, 0:ncols], acc[lh, 0:ncols]), r=[acc.b], w=[rl.b])
    c.op(c.act, lambda: nc.scalar.activation(rl[oh, 0:ncols], rl[oh, 0:ncols], AF.Ln), r=[rl.b], w=[rl.b])
    c.op(c.act, lambda: nc.scalar.activation(rl[oh, 0:ncols], rl[oh, 0:ncols], AF.Exp, scale=-1.0), r=[rl.b], w=[rl.b])
    c.op(c.dve, lambda: nc.vector.tensor_tensor(osb[oh, 0:ncols], acc[oh, 0:ncols], rl[oh, 0:ncols], ALU.mult),
         r=[acc.b, rl.b], w=[osb.b])
    gf = W["gf"][slot]
    c.op(c.act, lambda: nc.scalar.copy(gf[oh, 0:ncols], Gt[oh, gcol0:gcol0 + ncols]), r=[Gt.b], w=[gf.b])
    c.op(c.dve, lambda: nc.vector.tensor_tensor(ot[oh, 0:ncols], osb[oh, 0:ncols], gf[oh, 0:ncols], ALU.mult),
         r=[osb.b, gf.b], w=[ot.b])
    qd = tok0 // TSH
    r0 = qd * 256 + ot_dram_rows + h * 64
    tq = tok0 % TSH
    c.dma(c.sp, G["oT_loc"][l].ap()[r0:r0 + 64, tq:tq + ncols], ot[oh, 0:ncols], r=[ot.b], w=[G["oT_loc_b"][l][qd][bslot]], waw=False)


def gather_o(c, G, l, q, typ):
    r0 = q * 256 + typ * 128
    d0 = (q * 2 + typ) * 512
    srcb = G["oT_loc_b"][l][q][0:2] if typ == 0 else G["oT_loc_b"][l][q][2:3]
    c.allgather(G["oT_loc"][l].ap()[r0:r0 + 128, :], srcb, G["oT_all"][l].ap()[d0:d0 + 512, :], G["oT_all_b"][l][q][typ])


def phase_fox(c, nc, K, G, l):
    with ExitStack() as st:
        Wa = sb(st, nc, "Wa", [128, 8, 514], BF16)
        c.dma(c.sp, Wa[:, :, :], G["wa16"][l].ap().rearrange("(k p) n -> p k n", p=128), r=[G["wa16_b"][l]], w=[Wa.b])
        gq = sb(st, nc, "gqa", [128, 256], F32)
        c.dma(c.sp, gq[:, :], G["smalls"][:, l * 2560 + 2048:l * 2560 + 2304], r=[], w=[gq.b])
        c.op(c.dve, lambda: nc.vector.tensor_scalar(gq[:, 0:128], gq[:, 0:128], 0.125, None, ALU.mult), r=[gq.b], w=[gq.b])
        bfb = sb(st, nc, "bfb", [128, 2], F32)
        c.dma(c.sp, bfb[:, :], G["bf"][l], r=[], w=[bfb.b])
        QK = sb(st, nc, "QK", [128, 4, S], BF16, nb=16)
        V = sb(st, nc, "V", [128, 64, 256], BF16, nb=16)
        Gt = sb(st, nc, "Gt", [128, S], BF16)
        c.op(c.dve, lambda: nc.vector.memset(V[:, :, :], 1.0), w=V.bs)
        c.op(c.dve, lambda: nc.vector.memset(QK[64:70, :, :], 1.0), w=QK.bs)
        with ExitStack() as s1:
            UT = [sb(s1, nc, f"UT{i}", [128, 8, 512], BF16) for i in range(2)]
            sq = sb(s1, nc, "asq", [128, 1024], F32)
            qf = sb(s1, nc, "aqf", [128, 1024], F32)
            ss = sb(s1, nc, "ass", [128, 16], F32)
            rs = sb(s1, nc, "ars", [128, 16], F32)
            F = {k: [sb(s1, nc, f"f{k}", [2, 512], F32)] * 2 for k in ("fb", "y", "sc", "cy", "r1", "r2")}
            SPLT = [sb(s1, nc, "SPLT", [2, 6, 512], BF16)] * 2
            CAR = [sb(s1, nc, f"CAR{i}", [2, 1], F32) for i in range(2)]
            onesf = sb(s1, nc, "onesf", [2, 512], F32)
            bfb2 = sb(s1, nc, "bfb2", [2, 1], F32)
            c.dma(c.sp, bfb2[:, :], G["bf"][l][0, :].rearrange("(p o) -> p o", o=1), r=[], w=[bfb2.b])
            c.op(c.dve, lambda: nc.vector.memset(onesf[:, :], 1.0), w=[onesf.b])
            QN = [sb(s1, nc, f"QN{i}", [128, 4, 4, 64], BF16) for i in range(2)]
            E = [sb(s1, nc, "Eg", [128, 512], F32)] * 2
            gp = [ps(s1, nc, "gp", [128, 512], F32)] * 2
            fp_ = [ps(s1, nc, "fp", [32, 512], F32)] * 2
            zall = ps(s1, nc, "zall", [128, 4, 512], F32)
            tp = ps(s1, nc, "tpa", [128, 4, 4, 128], BF16)
            LV = LIM["stage1"]
            augb = Buf("augb")
            nchs = LIM["nch"] if LV >= 1 else 0
            uts = {}
            if nchs:
                uts[0] = load_uT_chunk(c, nc, G, l, UT, 0)
            for ch in range(nchs):
                if ch + 1 < nchs:
                    uts[ch + 1] = load_uT_chunk(c, nc, G, l, UT, ch + 1)
                ut = uts.pop(ch)
                g_p = gp[ch % 2]
                for k in range(8):
                    c.op(c.pe, lambda k=k: nc.tensor.matmul(g_p[:, :], Wa[:, k, 386:514], ut[:, k, :], start=(k == 0), stop=(k == 7)),
                         r=[Wa.b, ut.b], w=[g_p.b])
                silu_gate(c, nc, K, g_p, E[ch % 2], Gt[:, ch * 512:(ch + 1) * 512], Gt.b)
                for tt in range(4):
                    for k in range(8):
                        c.op(c.pe, lambda k=k, tt=tt: nc.tensor.matmul(zall[:, tt, 0:384], ut[:, k, tt * 128:(tt + 1) * 128], Wa[:, k, 0:384],
                                                                       start=(k == 0), stop=(k == 7)),
                             r=[Wa.b, ut.b], w=[zall.b])
                if LV >= 3:
                    sl = ch % 2
                    f_p = fp_[sl]
                    for k in range(8):
                        c.op(c.pe, lambda k=k: nc.tensor.matmul(f_p[:, :], Wa[:, k, 384:416], ut[:, k, :], start=(k == 0), stop=(k == 7)),
                             r=[Wa.b, ut.b], w=[f_p.b])
                c.op(c.act, lambda: nc.scalar.copy(V[:, 4 * ch:4 * ch + 4, 0:64], zall[:, :, 256:320]), r=[zall.b], w=[V.bs[ch]])
                c.op(c.act, lambda: nc.scalar.copy(V[:, 4 * ch:4 * ch + 4, 192:256], zall[:, :, 320:384]), r=[zall.b], w=[V.bs[ch]])
                qn = QN[ch % 2]
                qk_norm_chunk(c, nc, K, zall, sq, ss, rs, qf, zin=zall[:, :, 0:256])
                c.op(c.dve, lambda: nc.vector.tensor_tensor(qn[:, :, :, :].rearrange("p t g d -> p t (g d)"),
                                                             qf[:, :].rearrange("p (t n) -> p t n", t=4),
                                                             gq[:, :].unsqueeze(1).to_broadcast([128, 4, 256]), ALU.mult),
                     r=[qf.b, gq.b], w=[qn.b])
                for tt in range(4):
                    for g4 in range(4):
                        c.op(c.pe, lambda g4=g4, tt=tt: nc.tensor.transpose(tp[0:64, tt, g4, :], qn[:, tt, g4, :], K.ident[:, :]),
                             r=[qn.b, K.ident.b], w=[tp.b])
                c.op(c.act, lambda: nc.scalar.copy(QK[0:64, :, ch * 512:(ch + 1) * 512].rearrange("p g (t c) -> p g t c", t=4),
                                                   tp[0:64, :, :, :].rearrange("p t g c -> p g t c")), r=[tp.b], w=[QK.bs[ch]])
                if LV >= 3:
                    sl = ch % 2
                    f_p = fp_[sl]
                    fb, y, sc, cy, r1, r2, spl = [F[k][sl] for k in ("fb", "y", "sc", "cy", "r1", "r2")] + [SPLT[sl]]
                    c.op(c.dve, lambda: nc.vector.tensor_scalar(fb[:, :], f_p[0:2, :], bfb2[:, 0:1], None, ALU.add), r=[f_p.b, bfb2.b], w=[fb.b])
                    if LIM["sub"] < 2:
                        continue
                    c.op(c.act, lambda: nc.scalar.activation(y[:, :], fb[:, :], AF.Exp, scale=-1.0), r=[fb.b], w=[y.b])
                    c.op(c.act, lambda: nc.scalar.activation(y[:, :], y[:, :], AF.Ln, bias=K.one[0:2, 0:1], scale=1.0), r=[y.b, K.one.b], w=[y.b])
                    if LIM["sub"] < 3:
                        continue
                    c.op(c.dve, lambda: nc.vector.tensor_tensor_scan(sc[:, :], onesf[:, :], y[:, :], 0.0, ALU.mult, ALU.add), r=[onesf.b, y.b], w=[sc.b])
                    if ch == 0:
                        c.op(c.dve, lambda: nc.vector.tensor_copy(cy[:, :], sc[:, :]), r=[sc.b], w=[cy.b])
                    else:
                        car = CAR[(ch - 1) % 2]
                        c.op(c.dve, lambda: nc.vector.tensor_scalar(cy[:, :], sc[:, :], car[:, 0:1], None, ALU.add), r=[sc.b, car.b], w=[cy.b])
                    c.op(c.dve, lambda: nc.vector.tensor_copy(CAR[ch % 2][:, :], cy[:, 511:512]), r=[cy.b], w=[CAR[ch % 2].b])
                    if LIM["sub"] < 4:
                        continue
                    c.op(c.dve, lambda: nc.vector.tensor_copy(spl[:, 3, :], cy[:, :]), r=[cy.b], w=[spl.b])
                    c.op(c.dve, lambda: nc.vector.tensor_copy(sc[:, :], spl[:, 3, :]), r=[spl.b], w=[sc.b])
                    c.op(c.dve, lambda: nc.vector.tensor_tensor(r1[:, :], cy[:, :], sc[:, :], ALU.subtract), r=[cy.b, sc.b], w=[r1.b])
                    c.op(c.dve, lambda: nc.vector.tensor_copy(spl[:, 4, :], r1[:, :]), r=[r1.b], w=[spl.b])
                    c.op(c.dve, lambda: nc.vector.tensor_copy(sc[:, :], spl[:, 4, :]), r=[spl.b], w=[sc.b])
                    c.op(c.dve, lambda: nc.vector.tensor_tensor(r2[:, :], r1[:, :], sc[:, :], ALU.subtract), r=[r1.b, sc.b], w=[r2.b])
                    c.op(c.dve, lambda: nc.vector.tensor_copy(spl[:, 5, :], r2[:, :]), r=[r2.b], w=[spl.b])
                    c.op(c.dve, lambda: nc.vector.tensor_scalar(spl[:, 0:3, :], spl[:, 3:6, :], -1.0, None, ALU.mult), r=[spl.b], w=[spl.b])
                    cs_ = slice(ch * 512, (ch + 1) * 512)
                    c.wait_buf(c.sp, QK.bs[ch])
                    for hh in range(2 if LIM["sub"] >= 5 else 0):
                        c.dma(c.sp, QK[64:67, hh, cs_], spl[hh:hh + 1, 0:3, :], r=[spl.b], w=[augb], waw=False)
                        c.dma(c.sp, QK[67:70, 2 + hh, cs_], spl[hh:hh + 1, 3:6, :], r=[spl.b], w=[augb], waw=False)
            c.barrier()
        if not LIM["attn"]:
            return
        with ExitStack() as s2:
            NS, NP = 3, 4
            Sb = [ps(s2, nc, f"Sb{i}", [128, 2, 512], F32) for i in range(NS)]
            Ob = [ps(s2, nc, f"Ob{i}", [128, 512], F32) for i in range(2)]
            P = [sb(s2, nc, f"P{i}", [128, 2, 512], BF16) for i in range(NP)]
            W2 = {k: [sb(s2, nc, f"f{k}{i}", [128, 512], dt) for i in range(2)] for k, dt in
                  [("rl", F32), ("os", F32), ("gf", F32), ("ot", BF16)]}
            units = []
            for h in range(2):
                for I in range(LIM["nch"]):
                    for j in range(0, 4 * I, 2):
                        units.append((h, I, j, 2))
                    for j in range(4 * I, 4 * I + 4):
                        units.append((h, I, j, 1))
            LA = 2

            def emit_S(n):
                h, I, j, cnt_ = units[n]
                Sx, Px = Sb[n % NS], P[n % NP]
                q0 = I * 512
                if cnt_ == 2:
                    for u in range(2):
                        jj = j + u
                        c.op(c.pe, lambda jj=jj, u=u: nc.tensor.matmul(Sx[:, u, :], QK[0:70, 2 + h, jj * 128:(jj + 1) * 128],
                                                                       QK[0:70, h, q0:q0 + 512], start=True, stop=True),
                             r=[QK.bs[jj // 4], QK.bs[I]], w=[Sx.b])
                    c.op(c.act, lambda: nc.scalar.activation(Px[:, :, :], Sx[:, :, :], AF.Exp), r=[Sx.b], w=[Px.b])
                    return
                r_ = j - 4 * I
                lo = r_ * 128
                kT = QK[0:70, 2 + h, j * 128:(j + 1) * 128]
                rdeps = [QK.bs[j // 4], QK.bs[I]]
                c.op(c.pe, lambda: nc.tensor.matmul(Sx[:, 0, lo:lo + 128], kT, QK[0:70, h, q0 + lo:q0 + lo + 128], start=True, stop=False),
                     r=rdeps, w=[Sx.b])
                c.op(c.pe, lambda: nc.tensor.matmul(Sx[:, 0, lo:lo + 128], K.ident[:, :], K.mge[:, :], start=False, stop=True),
                     r=[K.ident.b, K.mge.b], w=[Sx.b])
                if lo + 128 < 512:
                    c.op(c.pe, lambda: nc.tensor.matmul(Sx[:, 0, lo + 128:512], kT, QK[0:70, h, q0 + lo + 128:q0 + 512], start=True, stop=True),
                         r=rdeps, w=[Sx.b])
                c.op(c.act, lambda: nc.scalar.activation(Px[:, 0, lo:512], Sx[:, 0, lo:512], AF.Exp), r=[Sx.b], w=[Px.b])

            def emit_PV(n):
                h, I, j, cnt_ = units[n]
                nj = 4 * I + 4
                it = h * LIM["nch"] + I
                O, Px = Ob[it % 2], P[n % NP]
                if cnt_ == 2:
                    for u in range(2):
                        jj = j + u
                        c.op(c.pe, lambda jj=jj, u=u: nc.tensor.matmul(O[:, 0:512], V[:, jj, h * 128:(h + 1) * 128], Px[:, u, :],
                                                                       start=(jj == 0), stop=False, skip_group_check=True),
                             r=[V.bs[jj // 4], Px.b], w=[O.b])
                    return
                lo = (j - 4 * I) * 128
                c.op(c.pe, lambda: nc.tensor.matmul(O[:, lo:512], V[:, j, h * 128:(h + 1) * 128], Px[:, 0, lo:512],
                                                    start=(j == 0), stop=(j == nj - 1), skip_group_check=True),
                     r=[V.bs[j // 4], Px.b], w=[O.b])
                if j == nj - 1:
                    finalize_out(c, nc, W2, O, h, 512, Gt, I * 512, 0, G, l, I * 512, it % 2, it % 2)

            for n in range(len(units) + LA):
                if n < len(units):
                    emit_S(n)
                if n - LA >= 0:
                    emit_PV(n - LA)
            c.barrier()


def phase_dil(c, nc, K, G, l, ROPE):
    with ExitStack() as st:
        Wb = sb(st, nc, "Wb", [128, 8, 512], BF16)
        c.dma(c.sp, Wb[:, :, :], G["wb16"][l].ap().rearrange("(k p) n -> p k n", p=128), r=[G["wb16_b"][l]], w=[Wb.b])
        gq = sb(st, nc, "gqb", [128, 256], F32)
        c.dma(c.sp, gq[:, :], G["smalls"][:, l * 2560 + 2304:l * 2560 + 2560], r=[], w=[gq.b])
        c.op(c.dve, lambda: nc.vector.tensor_scalar(gq[:, 0:128], gq[:, 0:128], 0.125, None, ALU.mult), r=[gq.b], w=[gq.b])
        QK = sb(st, nc, "QKd", [128, 4, S], BF16, nb=16)
        VT = sb(st, nc, "VTd", [128, S], BF16, nb=16)
        Gt = sb(st, nc, "Gtd", [128, S], BF16)
        with ExitStack() as s1:
            UT = [sb(s1, nc, f"UTd{i}", [128, 8, 512], BF16) for i in range(2)]
            sq = sb(s1, nc, "bsq", [128, 1024], F32)
            qf = sb(s1, nc, "bqf", [128, 1024], F32)
            qg = sb(s1, nc, "bqg", [128, 4, 4, 64], F32)
            ra = sb(s1, nc, "bra", [128, 4, 4, 16], F32)
            rb = sb(s1, nc, "brb", [128, 4, 4, 16], F32)
            ss = sb(s1, nc, "bss", [128, 16], F32)
            rs = sb(s1, nc, "brs", [128, 16], F32)
            QN = [sb(s1, nc, f"QNd{i}", [128, 4, 4, 64], BF16) for i in range(2)]
            E = sb(s1, nc, "Egd", [128, 512], F32)
            gp = ps(s1, nc, "gpd", [128, 512], F32)
            vp = ps(s1, nc, "vpd", [128, 512], F32)
            zqk = ps(s1, nc, "zqkd", [128, 4, 256], F32)
            tp = ps(s1, nc, "tpd", [128, 4, 4, 128], BF16)
            uts = {0: load_uT_chunk(c, nc, G, l, UT, 0)}
            for ch in range(LIM["nch"]):
                if ch + 1 < LIM["nch"]:
                    uts[ch + 1] = load_uT_chunk(c, nc, G, l, UT, ch + 1)
                ut = uts.pop(ch)
                for k in range(8):
                    c.op(c.pe, lambda k=k: nc.tensor.matmul(vp[:, :], Wb[:, k, 256:384], ut[:, k, :], start=(k == 0), stop=(k == 7)),
                         r=[Wb.b, ut.b], w=[vp.b])
                c.op(c.act, lambda: nc.scalar.copy(VT[:, ch * 512:(ch + 1) * 512], vp[:, :]), r=[vp.b], w=[VT.bs[ch]])
                for k in range(8):
                    c.op(c.pe, lambda k=k: nc.tensor.matmul(gp[:, :], Wb[:, k, 384:512], ut[:, k, :], start=(k == 0), stop=(k == 7)),
                         r=[Wb.b, ut.b], w=[gp.b])
                silu_gate(c, nc, K, gp, E, Gt[:, ch * 512:(ch + 1) * 512], Gt.b)
                for tt in range(4):
                    for k in range(8):
                        c.op(c.pe, lambda k=k, tt=tt: nc.tensor.matmul(zqk[:, tt, :], ut[:, k, tt * 128:(tt + 1) * 128], Wb[:, k, 0:256],
                                                                       start=(k == 0), stop=(k == 7)),
                             r=[Wb.b, ut.b], w=[zqk.b])
                qn = QN[ch % 2]
                qk_norm_chunk(c, nc, K, zqk, sq, ss, rs, qf)
                c.op(c.dve, lambda: nc.vector.tensor_tensor(qg[:, :, :, :].rearrange("p t g d -> p t (g d)"),
                                                             qf[:, :].rearrange("p (t n) -> p t n", t=4),
                                                             gq[:, :].unsqueeze(1).to_broadcast([128, 4, 256]), ALU.mult),
                     r=[qf.b, gq.b], w=[qg.b])
                c.op(c.act, lambda: nc.scalar.copy(qn[:, :, :, 16:64], qg[:, :, :, 16:64]), r=[qg.b], w=[qn.b])
                t0 = 4 * ch
                cs = ROPE["cs2"][:, t0:t0 + 4, :].unsqueeze(2).to_broadcast([128, 4, 4, 16])
                sn_lo = ROPE["sn2"][:, t0:t0 + 4, 0:8].unsqueeze(2).to_broadcast([128, 4, 4, 8])
                sn_hi = ROPE["sn2"][:, t0:t0 + 4, 8:16].unsqueeze(2).to_broadcast([128, 4, 4, 8])
                c.op(c.dve, lambda: nc.vector.tensor_tensor(ra[:, :, :, :], qg[:, :, :, 0:16], cs, ALU.mult), r=[qg.b, ROPE["b"]], w=[ra.b])
                c.op(c.dve, lambda: nc.vector.tensor_tensor(rb[:, :, :, 0:8], qg[:, :, :, 8:16], sn_lo, ALU.mult), r=[qg.b, ROPE["b"]], w=[rb.b])
                c.op(c.dve, lambda: nc.vector.tensor_tensor(rb[:, :, :, 8:16], qg[:, :, :, 0:8], sn_hi, ALU.mult), r=[qg.b, ROPE["b"]], w=[rb.b])
                c.op(c.dve, lambda: nc.vector.tensor_tensor(qn[:, :, :, 0:16], ra[:, :, :, :], rb[:, :, :, :], ALU.add), r=[ra.b, rb.b], w=[qn.b])
                for tt in range(4):
                    for g4 in range(4):
                        c.op(c.pe, lambda g4=g4, tt=tt: nc.tensor.transpose(tp[0:64, tt, g4, :], qn[:, tt, g4, :], K.ident[:, :]),
                             r=[qn.b, K.ident.b], w=[tp.b])
                c.op(c.act, lambda: nc.scalar.copy(QK[0:64, :, ch * 512:(ch + 1) * 512].rearrange("p g (t c) -> p g t c", t=4),
                                                   tp[0:64, :, :, :].rearrange("p t g c -> p g t c")), r=[tp.b], w=[QK.bs[ch]])
            c.barrier()
        with ExitStack() as s2:
            NS, NP, NV, NVT = 3, 5, 32, 3
            Sb = [ps(s2, nc, f"Sd{i}", [128, 256], F32) for i in range(NS)]
            Op = [ps(s2, nc, f"Od{i}", [128, 128], F32) for i in range(2)]
            vtp = [ps(s2, nc, f"vtp{i}", [128, 64], BF16) for i in range(NVT)]
            P = [sb(s2, nc, f"Pd{i}", [128, 256], BF16) for i in range(NP)]
            Vd = [sb(s2, nc, f"Vd{i}", [128, 128], BF16) for i in range(NV)]
            ACC = [sb(s2, nc, f"ACC{i}", [128, 2048], F32) for i in range(2)]
            W2 = {k: [sb(s2, nc, f"g{k}{i}", [128, 2048], dt) for i in range(1)] for k, dt in
                  [("rl", F32), ("os", F32), ("gf", F32), ("ot", BF16)]}
            cnt = {"nv": [0, 0], "nvt": 0}
            LA = 3
            NVH = NV // 2
            for k_, v in enumerate(Vd):
                c.op(c.dve, lambda v=v: nc.vector.memset(v[:, :], 1.0), w=[v.b])

            def make_vd(h, start, d):
                hr = slice(h * 64, (h + 1) * 64)
                vcol = slice(0, 64) if h == 0 else slice(64, 128)
                vd = Vd[h * NVH + cnt["nv"][h] % NVH]
                cnt["nv"][h] += 1
                vt = vtp[cnt["nvt"] % NVT]
                cnt["nvt"] += 1
                chs = sorted(set([start // 512, (start + 127 * d) // 512]))
                c.op(c.pe, lambda: nc.tensor.transpose(vt[:, :], VT[hr, start:start + 127 * d + 1:d], K.ident[hr, hr]),
                     r=[VT.bs[x] for x in range(chs[0], chs[-1] + 1)] + [K.ident.b], w=[vt.b])
                c.op(c.dve, lambda: nc.vector.tensor_copy(vd[:, vcol], vt[:, :]), r=[vt.b], w=[vd.b])
                return vd

            blocks = []
            for SBk in range(LIM["nsb"]):
                for h in range(2):
                    for d in (1, 4, 16):
                        nbq = 16 // d
                        for r_ in range(d):
                            for ii in range(nbq):
                                blocks.append((SBk, d, r_, ii, nbq, h))
            state = {"vd_prev": None}
            info = {}

            def emit_A(n):
                SBk, d, r_, ii, nbq, h = blocks[n]
                if ii == 0:
                    state["vd_prev"] = None
                i = nbq * SBk + ii
                start = 128 * i * d + r_
                pstart = start - 128 * d
                Sx, Px = Sb[n % NS], P[n % NP]
                c0, c1 = start // 512, (start + 127 * d) // 512
                rd = [QK.bs[x] for x in range(c0, c1 + 1)]
                qT = QK[0:64, h, start:start + 127 * d + 1:d]
                c.op(c.pe, lambda: nc.tensor.matmul(Sx[:, 128:256], QK[0:64, 2 + h, start:start + 127 * d + 1:d], qT, start=True, stop=True),
                     r=rd, w=[Sx.b])
                lo = 128
                vd_prev = state["vd_prev"]
                if i > 0:
                    lo = 0
                    p0, p1 = pstart // 512, (pstart + 127 * d) // 512
                    rdp = rd + [QK.bs[x] for x in range(p0, p1 + 1)]
                    c.op(c.pe, lambda: nc.tensor.matmul(Sx[:, 0:128], QK[0:64, 2 + h, pstart:pstart + 127 * d + 1:d], qT, start=True, stop=True),
                         r=rdp, w=[Sx.b])
                    if vd_prev is None:
                        vd_prev = make_vd(h, pstart, d)
                vd_own = make_vd(h, start, d)
                c.op(c.act, lambda: nc.scalar.activation(Px[:, lo:256], Sx[:, lo:256], AF.Exp), r=[Sx.b], w=[Px.b])
                c.op(c.dve, lambda: nc.vector.tensor_tensor(Px[:, lo:256], Px[:, lo:256], K.m01[:, lo:256], ALU.mult), r=[Px.b, K.m01.b], w=[Px.b])
                info[n] = (i, start, vd_prev, vd_own)
                state["vd_prev"] = vd_own

            def emit_B(n):
                SBk, d, r_, ii, nbq, h = blocks[n]
                i, start, vd_prev, vd_own = info.pop(n)
                Px = P[n % NP]
                O = Op[n % 2]
                acc = ACC[h]
                if i > 0:
                    c.op(c.pe, lambda: nc.tensor.matmul(O[:, :], vd_prev[:, :], Px[:, 0:128], start=True, stop=False),
                         r=[vd_prev.b, Px.b], w=[O.b])
                c.op(c.pe, lambda: nc.tensor.matmul(O[:, :], vd_own[:, :], Px[:, 128:256], start=(i == 0), stop=True),
                     r=[vd_own.b, Px.b], w=[O.b])
                off = start - 2048 * SBk
                av = acc[:, off:off + 127 * d + 1:d]
                if d == 1:
                    c.op(c.dve, lambda: nc.vector.tensor_copy(av, O[:, :]), r=[O.b], w=[acc.b])
                else:
                    c.op(c.dve, lambda: nc.vector.tensor_tensor(av, O[:, :], av, ALU.add), r=[O.b, acc.b], w=[acc.b])
                last = (n + 1 == len(blocks)) or (blocks[n + 1][0] != SBk) or (blocks[n + 1][5] != h)
                if last:
                    finalize_out(c, nc, W2, acc, h, 2048, Gt, SBk * 2048, 128, G, l, SBk * 2048, 0, 2)
                    if h == 1 and LIM["gather"]:
                        gather_o(c, G, l, SBk, 1)

            for n in range(len(blocks) + LA):
                if n < len(blocks):
                    emit_A(n)
                if n - LA >= 0:
                    emit_B(n - LA)
            c.barrier()


def build_rope(c, nc, st, K, G):
    cs2 = sb(st, nc, "cs2", [128, 64, 16], F32)
    sn2 = sb(st, nc, "sn2", [128, 64, 16], F32)
    rb = Buf("rope")
    with ExitStack() as s1:
        pi_ = sb(s1, nc, "posi", [128, 64], I32)
        pf = sb(s1, nc, "posf", [128, 64], F32)
        ang = sb(s1, nc, "ang", [128, 64, 8], F32)
        a2 = sb(s1, nc, "ang2", [128, 64, 8], F32)
        kf = sb(s1, nc, "kf", [128, 64, 8], F32)
        ki = sb(s1, nc, "ki", [128, 64, 8], I32)
        c.dma(c.sp, pi_[:, :], G["pos"][:, :], r=[], w=[pi_.b])
        c.op(c.dve, lambda: nc.vector.tensor_copy(pf[:, :], pi_[:, :]), r=[pi_.b], w=[pf.b])
        c.op(c.dve, lambda: nc.vector.tensor_tensor(ang[:, :, :], pf[:, :].unsqueeze(2).to_broadcast([128, 64, 8]),
                                                     K.invf[:, :].unsqueeze(1).to_broadcast([128, 64, 8]), ALU.mult),
             r=[pf.b, K.invf.b], w=[ang.b])

        def sin_of(dst_ap, shift, scale):
            c.op(c.dve, lambda: nc.vector.tensor_scalar(a2[:, :, :], ang[:, :, :], shift + float(np.pi), None, ALU.add), r=[ang.b], w=[a2.b])
            c.op(c.dve, lambda: nc.vector.tensor_scalar(kf[:, :, :], a2[:, :, :], 1.0 / TWO_PI, None, ALU.mult), r=[a2.b], w=[kf.b])
            c.op(c.dve, lambda: nc.vector.tensor_copy(ki[:, :, :], kf[:, :, :]), r=[kf.b], w=[ki.b])
            c.op(c.dve, lambda: nc.vector.tensor_copy(kf[:, :, :], ki[:, :, :]), r=[ki.b], w=[kf.b])
            c.op(c.dve, lambda: nc.vector.scalar_tensor_tensor(a2[:, :, :], kf[:, :, :], -TWO_PI, a2[:, :, :], ALU.mult, ALU.add), r=[kf.b, a2.b], w=[a2.b])
            c.op(c.dve, lambda: nc.vector.tensor_scalar(kf[:, :, :], a2[:, :, :], 0.0, TWO_PI, ALU.is_lt, ALU.mult), r=[a2.b], w=[kf.b])
            c.op(c.dve, lambda: nc.vector.tensor_tensor(a2[:, :, :], a2[:, :, :], kf[:, :, :], ALU.add), r=[a2.b, kf.b], w=[a2.b])
            c.op(c.dve, lambda: nc.vector.tensor_scalar(kf[:, :, :], a2[:, :, :], TWO_PI, -TWO_PI, ALU.is_ge, ALU.mult), r=[a2.b], w=[kf.b])
            c.op(c.dve, lambda: nc.vector.tensor_tensor(a2[:, :, :], a2[:, :, :], kf[:, :, :], ALU.add), r=[a2.b, kf.b], w=[a2.b])
            c.op(c.act, lambda: nc.scalar.activation(a2[:, :, :], a2[:, :, :], AF.Sin, bias=K.negpi[:, 0:1], scale=1.0), r=[a2.b, K.negpi.b], w=[a2.b])
            c.op(c.dve, lambda: nc.vector.tensor_scalar(dst_ap, a2[:, :, :], scale, None, ALU.mult), r=[a2.b], w=[rb])

        sin_of(cs2[:, :, 0:8], float(np.pi / 2), 1.0)
        sin_of(cs2[:, :, 8:16], float(np.pi / 2), 1.0)
        sin_of(sn2[:, :, 0:8], 0.0, -1.0)
        sin_of(sn2[:, :, 8:16], 0.0, 1.0)
        c.barrier()
    return {"cs2": cs2, "sn2": sn2, "b": rb}


def phase_b(c, nc, K, G, l, last):
    with ExitStack() as st:
        Wo = sb(st, nc, "Wo", [128, 8, 1024], BF16)
        Wg = sb(st, nc, "Wg", [128, 8, 1024], BF16)
        Wp = sb(st, nc, "Wp", [128, 2, 1024], BF16)
        PT = sb(st, nc, "PT", [128, 2, TSH], BF16)
        c.dma(c.sp, Wo[:, :, :], G["wo16"][l].ap().rearrange("(k p) n -> p k n", p=128), r=[G["wo16_b"][l]], w=[Wo.b])
        c.dma(c.sp, Wg[:, :, :], G["wg16"][l].ap().rearrange("(k p) n -> p k n", p=128), r=[G["wg16_b"][l]], w=[Wg.b])
        c.dma(c.sp, Wp[:, :, :], G["wp16"][l].ap().rearrange("(k p) n -> p k n", p=128), r=[G["wp16_b"][l]], w=[Wp.b])
        c.dma(c.sp, PT[:, :, :], G["pT16"][l].ap().rearrange("(k p) t -> p k t", p=128), r=[G["pT16_b"][l]], w=[PT.b])
        gple = sb(st, nc, "gple", [128, 1024], F32)
        c.dma(c.sp, gple[:, :], G["smalls"][:, l * 2560 + 1024:l * 2560 + 2048], r=[], w=[gple.b])
        gnext = None
        if not last:
            gnext = sb(st, nc, "gnext", [128, 1024], F32)
            c.dma(c.sp, gnext[:, :], G["smalls"][:, (l + 1) * 2560:(l + 1) * 2560 + 1024], r=[], w=[gnext.b])
            nt = NormT(c, nc, st, K, "nb")
        OT = [sb(st, nc, f"OTi{i}", [128, 8, 512], BF16) for i in range(2)]
        H = [sb(st, nc, f"H{i}", [128, 1024], F32) for i in range(2)]
        H2 = [sb(st, nc, f"H2{i}", [128, 1024], F32) for i in range(2)]
        sq = sb(st, nc, "bsq", [128, 1024], BF16)
        ss = [sb(st, nc, f"bss{i}", [128, 1], F32) for i in range(2)]
        rs = [sb(st, nc, f"brs{i}", [128, 1], F32) for i in range(2)]
        Vn = [sb(st, nc, f"Vn{i}", [128, 1024], BF16) for i in range(2)]
        VTs = [sb(st, nc, f"VTs{i}", [128, 1024], BF16) for i in range(2)]
        Eg = [sb(st, nc, f"Egb{i}", [128, 1024], F32) for i in range(2)]
        TM = [sb(st, nc, f"TM{i}", [128, 1024], F32) for i in range(2)]
        X = [[ps(st, nc, f"X{i}{hf}", [128, 512], F32) for hf in range(2)] for i in range(2)]
        Gp = [ps(st, nc, f"Gp{i}", [128, 512], F32) for i in range(2)]
        vtp = ps(st, nc, "vtpb", [128, 1024], BF16)
        hsrc = G["x_sh"] if l == 0 else G["h_loc"].ap()
        hsrc_b = [Buf("xin")] if l == 0 else G["h_loc_b"]

        def load_ot(ch):
            o_ = OT[ch % 2]
            src = G["oT_mine"][l].ap().rearrange("(q p) t -> p q t", p=128)[:, :, ch * 512:(ch + 1) * 512]
            c.dma(c.sp, o_[:, :, :], src, r=[G["oT_mine_b"][l]], w=[o_.b])
            return o_

        def load_h(t_):
            h_ = H[t_ % 2]
            c.dma(c.sp, h_[:, :], hsrc[t_ * 128:(t_ + 1) * 128, :], r=hsrc_b, w=[h_.b])
            return h_

        ots = {0: load_ot(0)}

        def stage_a(t):
            i = t % 2
            ot = ots[t // 4]
            if t % 4 == 1 and t // 4 + 1 < 4:
                ots[t // 4 + 1] = load_ot(t // 4 + 1)
            h = load_h(t)
            tt = t % 4
            for half in range(2):
                a = X[i][half]
                for kc in range(8):
                    c.op(c.pe, lambda kc=kc: nc.tensor.matmul(a[:, :], ot[:, kc, tt * 128:(tt + 1) * 128],
                                                              Wo[:, kc, half * 512:(half + 1) * 512],
                                                              start=(kc == 0), stop=(kc == 7)),
                         r=[ot.b, Wo.b], w=[a.b])
                c.op(c.dve, lambda a=a, half=half: nc.vector.tensor_tensor(h[:, half * 512:(half + 1) * 512], a[:, :], h[:, half * 512:(half + 1) * 512], ALU.add),
                     r=[a.b, h.b], w=[h.b])
            c.op(c.act, lambda: nc.scalar.activation(sq[:, :], h[:, :], AF.Square, accum_out=ss[i][:, 0:1]), r=[h.b], w=[sq.b, ss[i].b])
            rstd_from_ss(c, nc, ss[i], rs[i], 1, K, 1.0 / D)
            vn, vts = Vn[i], VTs[i]
            c.op(c.dve, lambda: nc.vector.scalar_tensor_tensor(vn[:, :], h[:, :], rs[i][:, 0:1], gple[:, :], ALU.mult, ALU.mult),
                 r=[h.b, rs[i].b, gple.b], w=[vn.b])
            for k in range(8):
                c.op(c.pe, lambda k=k: nc.tensor.transpose(vtp[:, k * 128:(k + 1) * 128], vn[:, k * 128:(k + 1) * 128], K.ident[:, :]),
                     r=[vn.b, K.ident.b], w=[vtp.b])
            c.op(c.act, lambda: nc.scalar.copy(vts[:, :], vtp[:, :]), r=[vtp.b], w=[vts.b])

        def stage_b(t):
            i = t % 2
            h, vts, eg, tm = H[i], VTs[i], Eg[i], TM[i]
            for half in range(2):
                g_ = Gp[half]
                for kc in range(8):
                    c.op(c.pe, lambda kc=kc: nc.tensor.matmul(g_[:, :], vts[:, kc * 128:(kc + 1) * 128], Wg[:, kc, half * 512:(half + 1) * 512],
                                                              start=(kc == 0), stop=(kc == 7)),
                         r=[vts.b, Wg.b], w=[g_.b])
                c.op(c.act, lambda g_=g_, half=half: nc.scalar.activation(eg[:, half * 512:(half + 1) * 512], g_[:, :], AF.Exp, scale=-1.0),
                     r=[g_.b], w=[eg.b])
            c.op(c.act, lambda: nc.scalar.activation(eg[:, :], eg[:, :], AF.Ln, bias=K.one[:, 0:1], scale=1.0), r=[eg.b, K.one.b], w=[eg.b])
            c.op(c.act, lambda: nc.scalar.activation(eg[:, :], eg[:, :], AF.Exp, scale=-1.0), r=[eg.b], w=[eg.b])
            for half in range(2):
                p_ = X[i][half]
                for kc in range(2):
                    c.op(c.pe, lambda kc=kc: nc.tensor.matmul(p_[:, :], PT[:, kc, t * 128:(t + 1) * 128], Wp[:, kc, half * 512:(half + 1) * 512],
                                                              start=(kc == 0), stop=(kc == 1)),
                         r=[PT.b, Wp.b], w=[p_.b])
                c.op(c.dve, lambda p_=p_, half=half: nc.vector.tensor_tensor(tm[:, half * 512:(half + 1) * 512], p_[:, :], eg[:, half * 512:(half + 1) * 512], ALU.mult),
                     r=[p_.b, eg.b], w=[tm.b])
            h2 = H2[i]
            c.op(c.dve, lambda: nc.vector.tensor_tensor(h2[:, :], h[:, :], tm[:, :], ALU.add), r=[h.b, tm.b], w=[h2.b])
            if last:
                c.dma(c.sp, G["out"][t * 128:(t + 1) * 128, :], h2[:, :], r=[h2.b], w=[G["out_b"][i]], waw=False)
            else:
                c.dma(c.sp, G["h_loc"].ap()[t * 128:(t + 1) * 128, :], h2[:, :], r=[h2.b], w=[G["h_loc_b"][i]], waw=False)
                nt.emit(h2[:, :], h2.b, gnext, t, G["uT_loc"][l + 1], G["uT_loc_b"][l + 1], G["uT_all"][l + 1], G["uT_all_b"][l + 1])

        stage_a(0)
        for t in range(16):
            if t + 1 < 16:
                stage_a(t + 1)
            stage_b(t)
        c.barrier()


def build_program(n_layers=2, debug=False, stop=None):
    _UID[0] = 0
    nc = bass.Bass("TRN2", target_bir_lowering=False)
    G = {}
    ei = lambda n, s_, d: nc.dram_tensor(n, s_, d, kind="ExternalInput").ap()
    G["x_sh"] = ei("x_sh", [TSH, D], F32)
    G["pT"] = ei("pT", [2, 256, TSH], F32)
    G["pos"] = ei("pos", [128, 64], I32)
    G["w_a"] = ei("w_a", [2, D, 514], F32)
    G["w_b"] = ei("w_b", [2, D, 512], F32)
    G["w_out"] = ei("w_out", [2, D, D], F32)
    G["w_gate"] = ei("w_gate", [2, D, D], F32)
    G["w_ple"] = ei("w_ple", [2, 256, D], F32)
    G["smalls"] = ei("smalls", [128, 5120], F32)
    G["consts"] = ei("consts", [128, 392], F32)
    G["bf"] = ei("bf", [2, 128, 2], F32)
    G["out"] = nc.dram_tensor("out", [TSH, D], F32, kind="ExternalOutput").ap()
    G["out_b"] = [Buf("out0", True), Buf("out1", True)]
    if debug:
        G["dbg_o"] = nc.dram_tensor("dbg_o", [256, S], BF16, kind="ExternalOutput").ap()
        G["dbg_u"] = nc.dram_tensor("dbg_u", [8 * 128, TSH], BF16, kind="ExternalOutput").ap()
    for nm, shp, src in (("wa16", [D, 514], "w_a"), ("wb16", [D, 512], "w_b"), ("wo16", [D, D], "w_out"),
                         ("wg16", [D, D], "w_gate"), ("wp16", [256, D], "w_ple"), ("pT16", [256, TSH], "pT")):
        G[nm] = [nc.dram_tensor(f"{nm}_{l}", shp, BF16) for l in range(2)]
        G[nm + "_b"] = [Buf(f"{nm}{l}", True) for l in range(2)]
        G[nm + "_src"] = src
    uT_loc = [nc.dram_tensor(f"uT_loc_{ch}", [8 * 128, 512], BF16) for ch in range(4)]
    uT_all = [nc.dram_tensor(f"uT_all_{ch}", [4 * 8 * 128, 512], BF16) for ch in range(4)]
    oT_loc = nc.dram_tensor("oT_loc", [4 * 256, TSH], BF16)
    oT_all = nc.dram_tensor("oT_all", [4 * 4 * 256, TSH], BF16)
    oT_mine = nc.dram_tensor("oT_mine", [4 * 256, TSH], BF16)
    G["uT_loc"] = [uT_loc, uT_loc]
    G["uT_all"] = [uT_all, uT_all]
    G["oT_loc"] = [oT_loc, oT_loc]
    G["oT_all"] = [oT_all, oT_all]
    G["oT_mine"] = [oT_mine, oT_mine]
    G["h_loc"] = nc.dram_tensor("h_loc", [TSH, D], F32)
    G["h_loc_b"] = [Buf("h_loc0", True), Buf("h_loc1", True)]
    b1 = [Buf(f"uTl{i}", True) for i in range(4)]
    G["uT_loc_b"] = [b1, b1]
    b2 = [Buf(f"uTa{i}", True) for i in range(4)]
    G["uT_all_b"] = [b2, b2]
    b3 = [[Buf(f"oTl{q}_{i}", True) for i in range(3)] for q in range(4)]
    G["oT_loc_b"] = [b3, b3]
    b4 = [[Buf(f"oTa{q}_{y}", True) for y in range(2)] for q in range(4)]
    G["oT_all_b"] = [b4, b4]
    G["oT_mine_b"] = [Buf("oTmine0", True), Buf("oTmine1", True)]
    with ExitStack() as st:
        c = Ctx(nc, st)
        def precast(names, l):
            for nm in names:
                c.dma(c.pool, G[nm][l].ap()[:, :], G[G[nm + "_src"]][l], r=[], w=[G[nm + "_b"][l]])

        precast(["wa16"], 0)
        K = load_consts(c, nc, st, G)
        ROPE = build_rope(c, nc, st, K, G)
        if stop not in ("foxsim", "dilsim", "bsim"):
            phase_norm0(c, nc, K, G)
            c.recycle()
        for l in range(n_layers):
            if stop == "norm0":
                break
            if stop in ("foxsim", "dilsim", "bsim"):
                precast(["wb16", "wo16", "wg16", "wp16", "pT16"], 0)
            if stop == "foxsim":
                phase_fox(c, nc, K, G, l)
                break
            if stop == "dilsim":
                phase_dil(c, nc, K, G, l, ROPE)
                break
            if stop == "bsim":
                phase_b(c, nc, K, G, l, last=False)
                break
            if stop == "ag":
                break
            if l == 0:
                precast(["wb16", "wo16", "wg16", "wp16", "pT16"], 0)
                if n_layers > 1:
                    precast(["wa16", "wb16", "wo16", "wg16", "wp16", "pT16"], 1)
            phase_fox(c, nc, K, G, l)
            c.recycle()
            for q in range(4):
                gather_o(c, G, l, q, 0)
            if stop == "fox":
                break
            phase_dil(c, nc, K, G, l, ROPE)
            c.recycle()
            if stop == "dil":
                break
            rank = nc.gpsimd.partition_id() % 4
            src = G["oT_all"][l].ap()[bass.ds(rank * 1024, 1024), :]
            c.dma(c.pool, G["oT_mine"][l].ap()[:, :], src, r=[b_ for qq in G["oT_all_b"][l] for b_ in qq], w=[G["oT_mine_b"][l]])
            phase_b(c, nc, K, G, l, last=(l == 1))
            c.recycle()
        if n_layers == 1 and stop is None:
            c.dma(c.pool, G["out"][:, :], G["h_loc"].ap()[:, :], r=G["h_loc_b"], w=[G["out_b"][0]])
        if debug:
            db = Buf("dbg", True)
            if stop == "dil":
                for q in range(4):
                    c.dma(c.pool, G["dbg_o"][:, q * TSH:(q + 1) * TSH], G["oT_loc"][0].ap()[q * 256:(q + 1) * 256, :], r=G["oT_loc_b"][0][q], w=[db])
            if stop is not None:
                for ch in range(4):
                    c.dma(c.pool, G["dbg_u"][:, ch * 512:(ch + 1) * 512], G["uT_loc"][0][ch].ap()[:, :], r=[G["uT_loc_b"][0][ch]], w=[db])
            c.wait_buf(c.sp, db)
        G["nsem"] = c.nsem
        for ob in G["out_b"]:
            c.wait_buf(c.sp, ob)
            c.wait_buf(c.pool, ob)
    return nc


def make_consts():
    cst = np.zeros((128, 392), np.float32)
    s_ = np.arange(128)[:, None]
    t_ = np.arange(128)[None, :]
    cst[:, 0:128] = (s_ == t_)
    cst[:, 128:256] = np.where(t_ >= s_, 0.0, NEG)
    cst[:, 256:384] = np.where(t_ <= s_, 0.0, NEG)
    cst[:, 384:392] = (500000.0 ** (-np.arange(8, dtype=np.float32) / 8.0))[None, :]
    return cst


def make_in_maps(x, p, positions, norm_g, w_in, b_f, qk_norm_g, w_out, w_ple, ple_norm_g, w_ple_gate):
    f = lambda a: np.ascontiguousarray(np.asarray(a, dtype=np.float32))
    x, p, norm_g, w_in, b_f, qk_norm_g, w_out, w_ple, ple_norm_g, w_ple_gate = map(
        f, (x, p, norm_g, w_in, b_f, qk_norm_g, w_out, w_ple, ple_norm_g, w_ple_gate))
    positions = np.asarray(positions).astype(np.int32)
    cst = make_consts()
    smalls = np.zeros((128, 5120), np.float32)
    for l in range(2):
        o = l * 2560
        smalls[:, o:o + 1024] = norm_g[l][None]
        smalls[:, o + 1024:o + 2048] = ple_norm_g[l][None]
        g = qk_norm_g[l]
        smalls[:, o + 2048:o + 2304] = np.concatenate([g[0], g[0], g[1], g[1]])[None]
        smalls[:, o + 2304:o + 2560] = np.concatenate([g[2], g[2], g[3], g[3]])[None]
    maps = []
    for core in range(NCORES):
        b, r = core // 4, core % 4
        hs = slice(128 * r, 128 * r + 128)
        cols_a = np.concatenate([np.arange(0, 512)[hs], np.arange(512, 1024)[hs], np.arange(1024, 1536)[hs],
                                 np.array([2048 + 2 * r, 2048 + 2 * r + 1]), np.arange(1536, 2048)[hs]])
        base = 2056
        cols_b = np.concatenate([base + np.arange(0, 512)[hs], base + np.arange(512, 1024)[hs],
                                 base + np.arange(1024, 1536)[hs], base + np.arange(1536, 2048)[hs]])
        m = {
            "x_sh": np.ascontiguousarray(x[b, r * TSH:(r + 1) * TSH]),
            "pT": np.ascontiguousarray(p[:, b, r * TSH:(r + 1) * TSH, :].transpose(0, 2, 1)),
            "pos": np.ascontiguousarray(positions[b].reshape(64, 128).T),
            "w_a": np.ascontiguousarray(w_in[:, :, cols_a]),
            "w_b": np.ascontiguousarray(w_in[:, :, cols_b]),
            "w_out": w_out, "w_gate": w_ple_gate, "w_ple": w_ple,
            "smalls": smalls, "consts": cst,
            "bf": np.ascontiguousarray(np.broadcast_to(b_f[:, None, 2 * r:2 * r + 2], (2, 128, 2))),
        }
        maps.append(m)
    return maps


_NC_CACHE = {}


def kernel(x, p, positions, norm_g, w_in, b_f, qk_norm_g, w_out, w_ple, ple_norm_g, w_ple_gate):
    maps = make_in_maps(x, p, positions, norm_g, w_in, b_f, qk_norm_g, w_out, w_ple, ple_norm_g, w_ple_gate)
    nc = build_program()
    res = run_bass_kernel_spmd(nc, maps, core_ids=list(range(NCORES)))
    out = np.empty((2, S, D), np.float32)
    for core in range(NCORES):
        b, r = core // 4, core % 4
        out[b, r * TSH:(r + 1) * TSH] = res.results[core]["out"]
    return out
```

```python
import numpy as np
from contextlib import ExitStack
import concourse.bass as bass
import concourse.mybir as mybir
from concourse.bass_utils import run_bass_kernel_spmd

F32 = mybir.dt.float32
BF16 = mybir.dt.bfloat16
I32 = mybir.dt.int32
AF = mybir.ActivationFunctionType
ALU = mybir.AluOpType
AX = mybir.AxisListType

NCORES = 8
S = 8192
D = 1024
TSH = 2048
NEG = -30000.0
EPS = 1e-6
TWO_PI = float(2 * np.pi)
LIM = {"nch": 16, "nsb": 4, "attn": True, "stage1": 99, "sub": 99, "gather": True}


class Buf:
    __slots__ = ("name", "writer", "readers", "dsem", "dcount", "persist")

    def __init__(self, name="", persist=False):
        self.name = name
        self.persist = persist
        self.writer = None
        self.readers = {}
        self.dsem = None
        self.dcount = 0


class Eng:
    def __init__(self, ctx, name, raw, is_pe=False):
        self.name = name
        self.raw = raw
        self.sem = ctx.new_sem("e_" + name)
        self.count = 0
        self.seen = {}
        self.is_pe = is_pe


class Ctx:
    def __init__(self, nc, stack):
        self.nc = nc
        self.stack = stack
        self.sems = {}
        self.nsem = 0
        self.pe = Eng(self, "pe", nc.tensor, is_pe=True)
        self.act = Eng(self, "act", nc.scalar)
        self.dve = Eng(self, "dve", nc.vector)
        self.pool = Eng(self, "pool", nc.gpsimd)
        self.sp = Eng(self, "sp", nc.sync)
        self.dbufs = []
        self.free_sems = []
        self.scope_bufs = []

    def new_sem(self, name):
        self.nsem += 1
        s = self.stack.enter_context(self.nc.semaphore(f"{name}_{self.nsem}"))
        self.sems[id(s)] = s
        return s

    def _wait(self, eng, deps):
        for sem, val in deps:
            k = id(sem)
            if eng.seen.get(k, 0) < val:
                eng.raw.wait_ge(sem, val)
                eng.seen[k] = val

    def _deps(self, r, w, waw=True):
        deps = []
        for b in r:
            if b.writer is not None:
                deps.append(b.writer)
        for b in w:
            if b.writer is not None and waw:
                deps.append(b.writer)
            for k, v in b.readers.items():
                deps.append((self.sems[k], v))
        return deps

    def op(self, eng, fn, r=(), w=()):
        own = id(eng.sem)
        if eng.is_pe:
            deps = [d for d in self._deps(r, w) if id(d[0]) != own]
        else:
            deps = self._deps(r, w)
        self._wait(eng, deps)
        ins = fn()
        ins.then_inc(eng.sem, 1)
        eng.count += 1
        for b in r:
            b.readers[own] = eng.count
        for b in w:
            b.writer = (eng.sem, eng.count)
            b.readers = {}
        return ins

    def dma(self, q, out_ap, in_ap, r=(), w=(), waw=True, **kw):
        deps = self._deps(r, w, waw=waw)
        self._wait(q, deps)
        dst = w[0]
        if dst.dsem is None:
            if self.free_sems and q is not self.pool:
                dst.dsem, dst.dcount = self.free_sems.pop()
            else:
                dst.dsem = self.new_sem("d_" + dst.name)
            self.dbufs.append(dst)
            if not dst.persist:
                self.scope_bufs.append(dst)
        ins = q.raw.dma_start(out=out_ap, in_=in_ap, **kw)
        ins.then_inc(dst.dsem, 16)
        dst.dcount += 16
        k = id(dst.dsem)
        for b in r:
            b.readers[k] = dst.dcount
        old_readers = dst.readers if not waw else {}
        dst.writer = (dst.dsem, dst.dcount)
        dst.readers = {}
        return ins

    def allgather(self, src_ap, src_b, dst_ap, dst_b):
        q = self.pool
        self._wait(q, self._deps(src_b, [dst_b]))
        if dst_b.dsem is None:
            dst_b.dsem = self.new_sem("cc_" + dst_b.name)
        ins = self.nc.gpsimd.collective_compute(
            "AllGather", ALU.bypass, replica_groups=[[0, 1, 2, 3], [4, 5, 6, 7]],
            ins=[src_ap.opt()], outs=[dst_ap.opt()])
        ins.then_inc(dst_b.dsem, 1)
        dst_b.dcount += 1
        for sb_ in src_b:
            sb_.readers[id(dst_b.dsem)] = dst_b.dcount
        dst_b.writer = (dst_b.dsem, dst_b.dcount)
        dst_b.readers = {}

    def barrier(self):
        engs = [self.pe, self.act, self.dve, self.pool, self.sp]
        deps = [(e.sem, e.count) for e in engs if e.count > 0]
        deps += [(b.dsem, b.dcount) for b in self.dbufs if b.dcount > 0]
        for e in engs:
            self._wait(e, [d for d in deps if id(d[0]) != id(e.sem) or not e.is_pe])

    def recycle(self):
        for b in self.scope_bufs:
            self.free_sems.append((b.dsem, b.dcount))
            self.dbufs.remove(b)
            b.dsem = None
        self.scope_bufs = []

    def wait_buf(self, eng, b):
        if b.writer is not None:
            self._wait(eng, [b.writer])


class T:
    def __init__(self, t, name, nb=1):
        self.t = t
        self.b = Buf(name)
        self.bs = [Buf(f"{name}{i}") for i in range(nb)]

    def __getitem__(self, k):
        return self.t[k]


_UID = [0]


def _un(name):
    _UID[0] += 1
    return f"{name}_{_UID[0]}"


def sb(st, nc, name, shape, dt, nb=1):
    name = _un(name)
    return T(st.enter_context(nc.sbuf_tensor(name, shape, dt)), name, nb)


def ps(st, nc, name, shape, dt, nb=1):
    name = _un(name)
    return T(st.enter_context(nc.psum_tensor(name, shape, dt)), name, nb)


class Consts:
    pass


def load_consts(c, nc, st, G):
    K = Consts()
    cf = sb(st, nc, "cf", [128, 392], F32)
    c.dma(c.sp, cf[:, :], G["consts"][:, :], r=[], w=[cf.b])
    K.ident = sb(st, nc, "ident", [128, 128], BF16)
    K.mge = sb(st, nc, "mge", [128, 128], BF16)
    K.mle = sb(st, nc, "mle", [128, 128], BF16)
    K.identf = sb(st, nc, "identf", [128, 128], F32)
    K.tri = sb(st, nc, "tri", [128, 128], F32)
    K.l127 = sb(st, nc, "l127", [128, 128], F32)
    K.invf = sb(st, nc, "invf", [128, 8], F32)
    K.m01 = sb(st, nc, "m01", [128, 256], BF16)
    K.eps = sb(st, nc, "epsc", [128, 1], F32)
    K.one = sb(st, nc, "onec", [128, 1], F32)
    K.negpi = sb(st, nc, "negpic", [128, 1], F32)
    c.op(c.dve, lambda: nc.vector.tensor_copy(K.ident[:, :], cf[:, 0:128]), r=[cf.b], w=[K.ident.b])
    c.op(c.dve, lambda: nc.vector.tensor_copy(K.mge[:, :], cf[:, 128:256]), r=[cf.b], w=[K.mge.b])
    c.op(c.dve, lambda: nc.vector.tensor_copy(K.mle[:, :], cf[:, 256:384]), r=[cf.b], w=[K.mle.b])
    c.op(c.dve, lambda: nc.vector.tensor_copy(K.invf[:, :], cf[:, 384:392]), r=[cf.b], w=[K.invf.b])
    c.op(c.dve, lambda: nc.vector.tensor_copy(K.identf[:, :], cf[:, 0:128]), r=[cf.b], w=[K.identf.b])
    c.op(c.dve, lambda: nc.vector.tensor_scalar(K.tri[:, :], cf[:, 128:256], 0.0, None, ALU.is_equal), r=[cf.b], w=[K.tri.b])
    c.op(c.dve, lambda: nc.vector.tensor_scalar(K.m01[:, 0:128], cf[:, 256:384], 0.0, None, ALU.is_equal), r=[cf.b], w=[K.m01.b])
    c.op(c.dve, lambda: nc.vector.tensor_scalar(K.m01[:, 128:256], cf[:, 128:256], 0.0, None, ALU.is_equal), r=[cf.b], w=[K.m01.b])
    c.op(c.dve, lambda: nc.vector.memset(K.l127[:, :], 0.0), w=[K.l127.b])
    c.op(c.dve, lambda: nc.vector.memset(K.l127[96:128, :], 1.0), w=[K.l127.b])
    c.op(c.dve, lambda: nc.vector.tensor_scalar(K.l127[:, :], K.l127[:, :], cf[:, 127:128], None, ALU.mult), r=[K.l127.b, cf.b], w=[K.l127.b])
    c.op(c.dve, lambda: nc.vector.memset(K.eps[:, :], EPS), w=[K.eps.b])
    c.op(c.dve, lambda: nc.vector.memset(K.one[:, :], 1.0), w=[K.one.b])
    c.op(c.dve, lambda: nc.vector.memset(K.negpi[:, :], -float(np.pi)), w=[K.negpi.b])
    return K


def rstd_from_ss(c, nc, ss, rs, n, K, inv_n):
    c.op(c.act, lambda: nc.scalar.activation(rs[:, 0:n], ss[:, 0:n], AF.Ln, bias=K.eps[:, 0:1], scale=inv_n),
         r=[ss.b, K.eps.b], w=[rs.b])
    c.op(c.act, lambda: nc.scalar.activation(rs[:, 0:n], rs[:, 0:n], AF.Exp, scale=-0.5), r=[rs.b], w=[rs.b])


class NormT:
    def __init__(self, c, nc, st, K, name):
        self.c, self.nc, self.K = c, nc, K
        self.sq = sb(st, nc, name + "sq", [128, 1024], BF16)
        self.ss = [sb(st, nc, name + f"ss{i}", [128, 1], F32) for i in range(2)]
        self.rs = [sb(st, nc, name + f"rs{i}", [128, 1], F32) for i in range(2)]
        self.u = [sb(st, nc, name + f"u{i}", [128, 1024], BF16) for i in range(2)]
        self.tp = [ps(st, nc, name + f"tp{i}", [128, 1024], BF16) for i in range(1)]
        self.stg = [sb(st, nc, name + f"stg{i}", [128, 8, 512], BF16) for i in range(2)]
        self.n = 0

    def emit(self, h_ap, h_buf, g_t, t, uT_dram, uT_buf, uT_all=None, uT_all_b=None):
        c, nc, K = self.c, self.nc, self.K
        i = self.n % 2
        self.n += 1
        ss, rs, u, tp = self.ss[i], self.rs[i], self.u[i], self.tp[0]
        stg = self.stg[(t // 4) % 2]
        c.op(c.act, lambda: nc.scalar.activation(self.sq[:, :], h_ap, AF.Square, accum_out=ss[:, 0:1]),
             r=[h_buf], w=[self.sq.b, ss.b])
        rstd_from_ss(c, nc, ss, rs, 1, K, 1.0 / D)
        c.op(c.dve, lambda: nc.vector.scalar_tensor_tensor(u[:, :], h_ap, rs[:, 0:1], g_t[:, :], ALU.mult, ALU.mult),
             r=[h_buf, rs.b, g_t.b], w=[u.b])
        for k in range(8):
            c.op(c.pe, lambda k=k: nc.tensor.transpose(tp[:, k * 128:(k + 1) * 128], u[:, k * 128:(k + 1) * 128], K.ident[:, :]),
                 r=[u.b, K.ident.b], w=[tp.b])
        tt = t % 4
        c.op(c.act, lambda: nc.scalar.copy(stg[:, :, tt * 128:(tt + 1) * 128], tp[:, :].rearrange("p (k t) -> p k t", k=8)),
             r=[tp.b], w=[stg.b])
        if tt == 3:
            ch = t // 4
            dst = uT_dram[ch].ap().rearrange("(k p) t -> p k t", p=128)
            c.dma(c.sp, dst, stg[:, :, :], r=[stg.b], w=[uT_buf[ch]])
            if uT_all is not None and LIM["gather"]:
                c.allgather(uT_dram[ch].ap(), [uT_buf[ch]], uT_all[ch].ap(), uT_all_b[ch])


def phase_norm0(c, nc, K, G):
    with ExitStack() as st:
        g = sb(st, nc, "n0g", [128, 1024], F32)
        c.dma(c.sp, g[:, :], G["smalls"][:, 0:1024], r=[], w=[g.b])
        xin = [sb(st, nc, f"n0x{i}", [128, 1024], F32) for i in range(2)]
        nt = NormT(c, nc, st, K, "n0")
        for t in range(16):
            x = xin[t % 2]
            c.dma(c.sp, x[:, :], G["x_sh"][t * 128:(t + 1) * 128, :], r=[], w=[x.b])
            nt.emit(x[:, :], x.b, g, t, G["uT_loc"][0], G["uT_loc_b"][0], G["uT_all"][0], G["uT_all_b"][0])
        c.barrier()


def qk_norm_tile(c, nc, K, W, zp, ncol, gq, slot):
    sq, ss, rs, qf, qg = W["sq"][slot], W["ss"][slot], W["rs"][slot], W["qf"][slot], W["qg"][slot]
    c.op(c.act, lambda: nc.scalar.activation(sq[:, :], zp[:, 0:256], AF.Square), r=[zp.b], w=[sq.b])
    c.op(c.dve, lambda: nc.vector.tensor_reduce(ss[:, 0:4], sq[:, :].rearrange("p (g d) -> p g d", g=4), AX.X, ALU.add),
         r=[sq.b], w=[ss.b])
    rstd_from_ss(c, nc, ss, rs, 4, K, 1.0 / 64)
    c.op(c.dve, lambda: nc.vector.tensor_tensor(qf[:, :].rearrange("p (g d) -> p g d", g=4),
                                                 zp[:, 0:256].rearrange("p (g d) -> p g d", g=4),
                                                 rs[:, 0:4].unsqueeze(2).to_broadcast([128, 4, 64]), ALU.mult),
         r=[zp.b, rs.b], w=[qf.b])
    c.op(c.dve, lambda: nc.vector.tensor_tensor(qg[:, :], qf[:, :], gq[:, :], ALU.mult), r=[qf.b, gq.b], w=[qg.b])
    return qg


def qk_norm_chunk(c, nc, K, zqk, sq, ss, rs, qf):
    c.op(c.act, lambda: nc.scalar.activation(sq[:, :], zqk[:, :, :].rearrange("p t n -> p (t n)"), AF.Square), r=[zqk.b], w=[sq.b])
    c.op(c.dve, lambda: nc.vector.tensor_reduce(ss[:, 0:16], sq[:, :].rearrange("p (a d) -> p a d", a=16), AX.X, ALU.add),
         r=[sq.b], w=[ss.b])
    rstd_from_ss(c, nc, ss, rs, 16, K, 1.0 / 64)
    c.op(c.dve, lambda: nc.vector.tensor_tensor(qf[:, :].rearrange("p (a d) -> p a d", a=16),
                                                 zqk[:, :, :].rearrange("p t (g d) -> p (t g) d", g=4),
                                                 rs[:, 0:16].unsqueeze(2).to_broadcast([128, 16, 64]), ALU.mult),
         r=[zqk.b, rs.b], w=[qf.b])
    return qf


def silu_gate(c, nc, K, gp, e, gdst_ap, gdst_b):
    c.op(c.act, lambda: nc.scalar.activation(e[:, :], gp[:, :], AF.Exp, scale=-1.0), r=[gp.b], w=[e.b])
    c.op(c.act, lambda: nc.scalar.activation(e[:, :], e[:, :], AF.Ln, bias=K.one[:, 0:1], scale=1.0), r=[e.b, K.one.b], w=[e.b])
    c.op(c.act, lambda: nc.scalar.activation(e[:, :], e[:, :], AF.Exp, scale=-1.0), r=[e.b], w=[e.b])
    c.op(c.dve, lambda: nc.vector.tensor_tensor(gdst_ap, gp[:, :], e[:, :], ALU.mult), r=[gp.b, e.b], w=[gdst_b])


def load_uT_chunk(c, nc, G, l, UT, ch):
    ut = UT[ch % 2]
    src = G["uT_all"][l][ch % 4].ap().rearrange("(r k p) t -> p r k t", r=4, k=8, p=128)[:, ch // 4, :, :]
    c.dma(c.sp, ut[:, :, :], src, r=[G["uT_all_b"][l][ch % 4]], w=[ut.b])
    return ut


def finalize_out(c, nc, W, acc, h, ncols, Gt, gcol0, ot_dram_rows, G, l, tok0, slot, bslot):
    oh = slice(0, 64) if h == 0 else slice(64, 128)
    lh = slice(64, 128) if h == 0 else slice(0, 64)
    rl, osb, ot = W["rl"][slot], W["os"][slot], W["ot"][slot]
    c.op(c.dve, lambda: nc.vector.tensor_copy(rl[oh, 0:ncols], acc[lh, 0:ncols]), r=[acc.b], w=[rl.b])
    c.op(c.act, lambda: nc.scalar.activation(rl[oh, 0:ncols], rl[oh, 0:ncols], AF.Ln), r=[rl.b], w=[rl.b])
    c.op(c.act, lambda: nc.scalar.activation(rl[oh, 0:ncols], rl[oh, 0:ncols], AF.Exp, scale=-1.0), r=[rl.b], w=[rl.b])
    c.op(c.dve, lambda: nc.vector.tensor_tensor(osb[oh, 0:ncols], acc[oh, 0:ncols], rl[oh, 0:ncols], ALU.mult),
         r=[acc.b, rl.b], w=[osb.b])
    gf = W["gf"][slot]
    if ncols <= 512:
        c.op(c.dve, lambda: nc.vector.tensor_copy(gf[oh, 0:ncols], Gt[oh, gcol0:gcol0 + ncols]), r=[Gt.b], w=[gf.b])
    else:
        c.op(c.act, lambda: nc.scalar.copy(gf[oh, 0:ncols], Gt[oh, gcol0:gcol0 + ncols]), r=[Gt.b], w=[gf.b])
    c.op(c.dve, lambda: nc.vector.tensor_tensor(ot[oh, 0:ncols], osb[oh, 0:ncols], gf[oh, 0:ncols], ALU.mult),
         r=[osb.b, gf.b], w=[ot.b])
    qd = tok0 // TSH
    r0 = qd * 256 + ot_dram_rows + h * 64
    tq = tok0 % TSH
    c.dma(c.sp, G["oT_loc"][l].ap()[r0:r0 + 64, tq:tq + ncols], ot[oh, 0:ncols], r=[ot.b], w=[G["oT_loc_b"][l][qd][bslot]], waw=False)


def gather_o(c, G, l, q, typ):
    r0 = q * 256 + typ * 128
    d0 = (q * 2 + typ) * 512
    srcb = G["oT_loc_b"][l][q][0:2] if typ == 0 else G["oT_loc_b"][l][q][2:3]
    c.allgather(G["oT_loc"][l].ap()[r0:r0 + 128, :], srcb, G["oT_all"][l].ap()[d0:d0 + 512, :], G["oT_all_b"][l][q][typ])


def phase_fox(c, nc, K, G, l):
    with ExitStack() as st:
        Wa = sb(st, nc, "Wa", [128, 8, 514], BF16)
        c.dma(c.sp, Wa[:, :, :], G["wa16"][l].ap().rearrange("(k p) n -> p k n", p=128), r=[G["wa16_b"][l]], w=[Wa.b])
        gq = sb(st, nc, "gqa", [128, 256], F32)
        c.dma(c.sp, gq[:, :], G["smalls"][:, l * 2560 + 2048:l * 2560 + 2304], r=[], w=[gq.b])
        c.op(c.dve, lambda: nc.vector.tensor_scalar(gq[:, 0:128], gq[:, 0:128], 0.125, None, ALU.mult), r=[gq.b], w=[gq.b])
        bfb = sb(st, nc, "bfb", [128, 2], F32)
        c.dma(c.sp, bfb[:, :], G["bf"][l], r=[], w=[bfb.b])
        QK = sb(st, nc, "QK", [128, 4, S], BF16, nb=16)
        V = sb(st, nc, "V", [128, 64, 256], BF16, nb=16)
        Gt = sb(st, nc, "Gt", [128, S], BF16)
        c.op(c.dve, lambda: nc.vector.memset(V[:, :, 64:192], 1.0), w=V.bs)
        c.op(c.dve, lambda: nc.vector.memset(QK[64:70, :, :], 1.0), w=QK.bs)
        with ExitStack() as s1:
            UT = [sb(s1, nc, f"UT{i}", [128, 8, 512], BF16) for i in range(2)]
            sq = sb(s1, nc, "asq", [128, 1024], F32)
            qf = sb(s1, nc, "aqf", [128, 1024], F32)
            ss = sb(s1, nc, "ass", [128, 16], F32)
            rs = sb(s1, nc, "ars", [128, 16], F32)
            F = {k: [sb(s1, nc, f"f{k}", [2, 512], F32)] * 2 for k in ("fb", "y", "sc", "cy", "r1", "r2")}
            SPLT = [sb(s1, nc, "SPLT", [2, 6, 512], BF16)] * 2
            CAR = [sb(s1, nc, f"CAR{i}", [2, 1], F32) for i in range(2)]
            onesf = sb(s1, nc, "onesf", [2, 512], F32)
            bfb2 = sb(s1, nc, "bfb2", [2, 1], F32)
            c.dma(c.sp, bfb2[:, :], G["bf"][l][0, :].rearrange("(p o) -> p o", o=1), r=[], w=[bfb2.b])
            c.op(c.dve, lambda: nc.vector.memset(onesf[:, :], 1.0), w=[onesf.b])
            QN = [sb(s1, nc, f"QN{i}", [128, 4, 4, 64], BF16) for i in range(2)]
            E = [sb(s1, nc, "Eg", [128, 512], F32)] * 2
            gp = [ps(s1, nc, "gp", [128, 512], F32)] * 2
            fp_ = [ps(s1, nc, "fp", [32, 512], F32)] * 2
            zqk = ps(s1, nc, "zqk", [128, 4, 256], F32)
            zv = ps(s1, nc, "zv", [128, 4, 128], F32)
            tp = ps(s1, nc, "tpa", [128, 4, 4, 128], BF16)
            LV = LIM["stage1"]
            augb = Buf("augb")
            nchs = LIM["nch"] if LV >= 1 else 0
            uts = {}
            if nchs:
                uts[0] = load_uT_chunk(c, nc, G, l, UT, 0)
            for ch in range(nchs):
                if ch + 1 < nchs:
                    uts[ch + 1] = load_uT_chunk(c, nc, G, l, UT, ch + 1)
                ut = uts.pop(ch)
                g_p = gp[ch % 2]
                for k in range(8):
                    c.op(c.pe, lambda k=k: nc.tensor.matmul(g_p[:, :], Wa[:, k, 386:514], ut[:, k, :], start=(k == 0), stop=(k == 7)),
                         r=[Wa.b, ut.b], w=[g_p.b])
                silu_gate(c, nc, K, g_p, E[ch % 2], Gt[:, ch * 512:(ch + 1) * 512], Gt.b)
                for tt in range(4):
                    for k in range(8):
                        c.op(c.pe, lambda k=k, tt=tt: nc.tensor.matmul(zqk[:, tt, :], ut[:, k, tt * 128:(tt + 1) * 128], Wa[:, k, 0:256],
                                                                       start=(k == 0), stop=(k == 7)),
                             r=[Wa.b, ut.b], w=[zqk.b])
                    for k in range(8):
                        c.op(c.pe, lambda k=k, tt=tt: nc.tensor.matmul(zv[:, tt, :], ut[:, k, tt * 128:(tt + 1) * 128], Wa[:, k, 256:384],
                                                                       start=(k == 0), stop=(k == 7)),
                             r=[Wa.b, ut.b], w=[zv.b])
                if LV >= 3:
                    sl = ch % 2
                    f_p = fp_[sl]
                    for k in range(8):
                        c.op(c.pe, lambda k=k: nc.tensor.matmul(f_p[:, :], Wa[:, k, 384:416], ut[:, k, :], start=(k == 0), stop=(k == 7)),
                             r=[Wa.b, ut.b], w=[f_p.b])
                c.op(c.act, lambda: nc.scalar.copy(V[:, 4 * ch:4 * ch + 4, 0:64], zv[:, :, 0:64]), r=[zv.b], w=[V.bs[ch]])
                c.op(c.act, lambda: nc.scalar.copy(V[:, 4 * ch:4 * ch + 4, 192:256], zv[:, :, 64:128]), r=[zv.b], w=[V.bs[ch]])
                qn = QN[ch % 2]
                qk_norm_chunk(c, nc, K, zqk, sq, ss, rs, qf)
                c.op(c.dve, lambda: nc.vector.tensor_tensor(qn[:, :, :, :].rearrange("p t g d -> p t (g d)"),
                                                             qf[:, :].rearrange("p (t n) -> p t n", t=4),
                                                             gq[:, :].unsqueeze(1).to_broadcast([128, 4, 256]), ALU.mult),
                     r=[qf.b, gq.b], w=[qn.b])
                for tt in range(4):
                    for g4 in range(4):
                        c.op(c.pe, lambda g4=g4, tt=tt: nc.tensor.transpose(tp[0:64, tt, g4, :], qn[:, tt, g4, :], K.ident[:, :]),
                             r=[qn.b, K.ident.b], w=[tp.b])
                c.op(c.act, lambda: nc.scalar.copy(QK[0:64, :, ch * 512:(ch + 1) * 512].rearrange("p g (t c) -> p g t c", t=4),
                                                   tp[0:64, :, :, :].rearrange("p t g c -> p g t c")), r=[tp.b], w=[QK.bs[ch]])
                if LV >= 3:
                    sl = ch % 2
                    f_p = fp_[sl]
                    fb, y, sc, cy, r1, r2, spl = [F[k][sl] for k in ("fb", "y", "sc", "cy", "r1", "r2")] + [SPLT[sl]]
                    c.op(c.dve, lambda: nc.vector.tensor_scalar(fb[:, :], f_p[0:2, :], bfb2[:, 0:1], None, ALU.add), r=[f_p.b, bfb2.b], w=[fb.b])
                    if LIM["sub"] < 2:
                        continue
                    c.op(c.act, lambda: nc.scalar.activation(y[:, :], fb[:, :], AF.Exp, scale=-1.0), r=[fb.b], w=[y.b])
                    c.op(c.act, lambda: nc.scalar.activation(y[:, :], y[:, :], AF.Ln, bias=K.one[0:2, 0:1], scale=1.0), r=[y.b, K.one.b], w=[y.b])
                    if LIM["sub"] < 3:
                        continue
                    c.op(c.dve, lambda: nc.vector.tensor_tensor_scan(sc[:, :], onesf[:, :], y[:, :], 0.0, ALU.mult, ALU.add), r=[onesf.b, y.b], w=[sc.b])
                    if ch == 0:
                        c.op(c.dve, lambda: nc.vector.tensor_copy(cy[:, :], sc[:, :]), r=[sc.b], w=[cy.b])
                    else:
                        car = CAR[(ch - 1) % 2]
                        c.op(c.dve, lambda: nc.vector.tensor_scalar(cy[:, :], sc[:, :], car[:, 0:1], None, ALU.add), r=[sc.b, car.b], w=[cy.b])
                    c.op(c.dve, lambda: nc.vector.tensor_copy(CAR[ch % 2][:, :], cy[:, 511:512]), r=[cy.b], w=[CAR[ch % 2].b])
                    if LIM["sub"] < 4:
                        continue
                    c.op(c.dve, lambda: nc.vector.tensor_copy(spl[:, 3, :], cy[:, :]), r=[cy.b], w=[spl.b])
                    c.op(c.dve, lambda: nc.vector.tensor_copy(sc[:, :], spl[:, 3, :]), r=[spl.b], w=[sc.b])
                    c.op(c.dve, lambda: nc.vector.tensor_tensor(r1[:, :], cy[:, :], sc[:, :], ALU.subtract), r=[cy.b, sc.b], w=[r1.b])
                    c.op(c.dve, lambda: nc.vector.tensor_copy(spl[:, 4, :], r1[:, :]), r=[r1.b], w=[spl.b])
                    c.op(c.dve, lambda: nc.vector.tensor_copy(sc[:, :], spl[:, 4, :]), r=[spl.b], w=[sc.b])
                    c.op(c.dve, lambda: nc.vector.tensor_tensor(r2[:, :], r1[:, :], sc[:, :], ALU.subtract), r=[r1.b, sc.b], w=[r2.b])
                    c.op(c.dve, lambda: nc.vector.tensor_copy(spl[:, 5, :], r2[:, :]), r=[r2.b], w=[spl.b])
                    c.op(c.dve, lambda: nc.vector.tensor_scalar(spl[:, 0:3, :], spl[:, 3:6, :], -1.0, None, ALU.mult), r=[spl.b], w=[spl.b])
                    cs_ = slice(ch * 512, (ch + 1) * 512)
                    c.wait_buf(c.sp, QK.bs[ch])
                    for hh in range(2 if LIM["sub"] >= 5 else 0):
                        c.dma(c.sp, QK[64:67, hh, cs_], spl[hh:hh + 1, 0:3, :], r=[spl.b], w=[augb], waw=False)
                        c.dma(c.sp, QK[67:70, 2 + hh, cs_], spl[hh:hh + 1, 3:6, :], r=[spl.b], w=[augb], waw=False)
            c.barrier()
        if not LIM["attn"]:
            return
        with ExitStack() as s2:
            NS, NP = 3, 4
            Sb = [ps(s2, nc, f"Sb{i}", [128, 2, 512], F32) for i in range(NS)]
            Ob = [ps(s2, nc, f"Ob{i}", [128, 512], F32) for i in range(2)]
            P = [sb(s2, nc, f"P{i}", [128, 2, 512], BF16) for i in range(NP)]
            W2 = {k: [sb(s2, nc, f"f{k}{i}", [128, 512], dt) for i in range(2)] for k, dt in
                  [("rl", F32), ("os", F32), ("gf", F32), ("ot", BF16)]}
            units = []
            for h in range(2):
                for I in range(LIM["nch"]):
                    for j in range(0, 4 * I, 2):
                        units.append((h, I, j, 2))
                    for j in range(4 * I, 4 * I + 4):
                        units.append((h, I, j, 1))
            LA = 3

            def emit_S(n):
                h, I, j, cnt_ = units[n]
                Sx, Px = Sb[n % NS], P[n % NP]
                q0 = I * 512
                if cnt_ == 2:
                    for u in range(2):
                        jj = j + u
                        c.op(c.pe, lambda jj=jj, u=u: nc.tensor.matmul(Sx[:, u, :], QK[0:70, 2 + h, jj * 128:(jj + 1) * 128],
                                                                       QK[0:70, h, q0:q0 + 512], start=True, stop=True),
                             r=[QK.bs[jj // 4], QK.bs[I]], w=[Sx.b])
                    c.op(c.act, lambda: nc.scalar.activation(Px[:, :, :], Sx[:, :, :], AF.Exp), r=[Sx.b], w=[Px.b])
                    return
                r_ = j - 4 * I
                lo = r_ * 128
                kT = QK[0:70, 2 + h, j * 128:(j + 1) * 128]
                rdeps = [QK.bs[j // 4], QK.bs[I]]
                c.op(c.pe, lambda: nc.tensor.matmul(Sx[:, 0, lo:lo + 128], kT, QK[0:70, h, q0 + lo:q0 + lo + 128], start=True, stop=False),
                     r=rdeps, w=[Sx.b])
                c.op(c.pe, lambda: nc.tensor.matmul(Sx[:, 0, lo:lo + 128], K.ident[:, :], K.mge[:, :], start=False, stop=True),
                     r=[K.ident.b, K.mge.b], w=[Sx.b])
                if lo + 128 < 512:
                    c.op(c.pe, lambda: nc.tensor.matmul(Sx[:, 0, lo + 128:512], kT, QK[0:70, h, q0 + lo + 128:q0 + 512], start=True, stop=True),
                         r=rdeps, w=[Sx.b])
                c.op(c.act, lambda: nc.scalar.activation(Px[:, 0, lo:512], Sx[:, 0, lo:512], AF.Exp), r=[Sx.b], w=[Px.b])

            def emit_PV(n):
                h, I, j, cnt_ = units[n]
                nj = 4 * I + 4
                it = h * LIM["nch"] + I
                O, Px = Ob[it % 2], P[n % NP]
                if cnt_ == 2:
                    for u in range(2):
                        jj = j + u
                        c.op(c.pe, lambda jj=jj, u=u: nc.tensor.matmul(O[:, 0:512], V[:, jj, h * 128:(h + 1) * 128], Px[:, u, :],
                                                                       start=(jj == 0), stop=False, skip_group_check=True),
                             r=[V.bs[jj // 4], Px.b], w=[O.b])
                    return
                lo = (j - 4 * I) * 128
                c.op(c.pe, lambda: nc.tensor.matmul(O[:, lo:512], V[:, j, h * 128:(h + 1) * 128], Px[:, 0, lo:512],
                                                    start=(j == 0), stop=(j == nj - 1), skip_group_check=True),
                     r=[V.bs[j // 4], Px.b], w=[O.b])
                if j == nj - 1:
                    finalize_out(c, nc, W2, O, h, 512, Gt, I * 512, 0, G, l, I * 512, it % 2, it % 2)

            for n in range(len(units) + LA):
                if n < len(units):
                    emit_S(n)
                if n - LA >= 0:
                    emit_PV(n - LA)
            c.barrier()


def phase_dil(c, nc, K, G, l, ROPE):
    with ExitStack() as st:
        Wb = sb(st, nc, "Wb", [128, 8, 512], BF16)
        c.dma(c.sp, Wb[:, :, :], G["wb16"][l].ap().rearrange("(k p) n -> p k n", p=128), r=[G["wb16_b"][l]], w=[Wb.b])
        gq = sb(st, nc, "gqb", [128, 256], F32)
        c.dma(c.sp, gq[:, :], G["smalls"][:, l * 2560 + 2304:l * 2560 + 2560], r=[], w=[gq.b])
        c.op(c.dve, lambda: nc.vector.tensor_scalar(gq[:, 0:128], gq[:, 0:128], 0.125, None, ALU.mult), r=[gq.b], w=[gq.b])
        QK = sb(st, nc, "QKd", [128, 4, S], BF16, nb=16)
        VT = sb(st, nc, "VTd", [128, S], BF16, nb=16)
        Gt = sb(st, nc, "Gtd", [128, S], BF16)
        with ExitStack() as s1:
            UT = [sb(s1, nc, f"UTd{i}", [128, 8, 512], BF16) for i in range(2)]
            sq = sb(s1, nc, "bsq", [128, 1024], F32)
            qf = sb(s1, nc, "bqf", [128, 1024], F32)
            qg = sb(s1, nc, "bqg", [128, 4, 4, 64], F32)
            ra = sb(s1, nc, "bra", [128, 4, 4, 16], F32)
            rb = sb(s1, nc, "brb", [128, 4, 4, 16], F32)
            ss = sb(s1, nc, "bss", [128, 16], F32)
            rs = sb(s1, nc, "brs", [128, 16], F32)
            QN = [sb(s1, nc, f"QNd{i}", [128, 4, 4, 64], BF16) for i in range(2)]
            E = sb(s1, nc, "Egd", [128, 512], F32)
            gp = ps(s1, nc, "gpd", [128, 512], F32)
            vp = ps(s1, nc, "vpd", [128, 512], F32)
            zqk = ps(s1, nc, "zqkd", [128, 4, 256], F32)
            tp = ps(s1, nc, "tpd", [128, 4, 4, 128], BF16)
            uts = {0: load_uT_chunk(c, nc, G, l, UT, 0)}
            for ch in range(LIM["nch"]):
                if ch + 1 < LIM["nch"]:
                    uts[ch + 1] = load_uT_chunk(c, nc, G, l, UT, ch + 1)
                ut = uts.pop(ch)
                for k in range(8):
                    c.op(c.pe, lambda k=k: nc.tensor.matmul(vp[:, :], Wb[:, k, 256:384], ut[:, k, :], start=(k == 0), stop=(k == 7)),
                         r=[Wb.b, ut.b], w=[vp.b])
                c.op(c.act, lambda: nc.scalar.copy(VT[:, ch * 512:(ch + 1) * 512], vp[:, :]), r=[vp.b], w=[VT.bs[ch]])
                for k in range(8):
                    c.op(c.pe, lambda k=k: nc.tensor.matmul(gp[:, :], Wb[:, k, 384:512], ut[:, k, :], start=(k == 0), stop=(k == 7)),
                         r=[Wb.b, ut.b], w=[gp.b])
                silu_gate(c, nc, K, gp, E, Gt[:, ch * 512:(ch + 1) * 512], Gt.b)
                for tt in range(4):
                    for k in range(8):
                        c.op(c.pe, lambda k=k, tt=tt: nc.tensor.matmul(zqk[:, tt, :], ut[:, k, tt * 128:(tt + 1) * 128], Wb[:, k, 0:256],
                                                                       start=(k == 0), stop=(k == 7)),
                             r=[Wb.b, ut.b], w=[zqk.b])
                qn = QN[ch % 2]
                qk_norm_chunk(c, nc, K, zqk, sq, ss, rs, qf)
                c.op(c.dve, lambda: nc.vector.tensor_tensor(qg[:, :, :, :].rearrange("p t g d -> p t (g d)"),
                                                             qf[:, :].rearrange("p (t n) -> p t n", t=4),
                                                             gq[:, :].unsqueeze(1).to_broadcast([128, 4, 256]), ALU.mult),
                     r=[qf.b, gq.b], w=[qg.b])
                c.op(c.act, lambda: nc.scalar.copy(qn[:, :, :, 16:64], qg[:, :, :, 16:64]), r=[qg.b], w=[qn.b])
                t0 = 4 * ch
                cs = ROPE["cs2"][:, t0:t0 + 4, :].unsqueeze(2).to_broadcast([128, 4, 4, 16])
                sn_lo = ROPE["sn2"][:, t0:t0 + 4, 0:8].unsqueeze(2).to_broadcast([128, 4, 4, 8])
                sn_hi = ROPE["sn2"][:, t0:t0 + 4, 8:16].unsqueeze(2).to_broadcast([128, 4, 4, 8])
                c.op(c.dve, lambda: nc.vector.tensor_tensor(ra[:, :, :, :], qg[:, :, :, 0:16], cs, ALU.mult), r=[qg.b, ROPE["b"]], w=[ra.b])
                c.op(c.dve, lambda: nc.vector.tensor_tensor(rb[:, :, :, 0:8], qg[:, :, :, 8:16], sn_lo, ALU.mult), r=[qg.b, ROPE["b"]], w=[rb.b])
                c.op(c.dve, lambda: nc.vector.tensor_tensor(rb[:, :, :, 8:16], qg[:, :, :, 0:8], sn_hi, ALU.mult), r=[qg.b, ROPE["b"]], w=[rb.b])
                c.op(c.dve, lambda: nc.vector.tensor_tensor(qn[:, :, :, 0:16], ra[:, :, :, :], rb[:, :, :, :], ALU.add), r=[ra.b, rb.b], w=[qn.b])
                for tt in range(4):
                    for g4 in range(4):
                        c.op(c.pe, lambda g4=g4, tt=tt: nc.tensor.transpose(tp[0:64, tt, g4, :], qn[:, tt, g4, :], K.ident[:, :]),
                             r=[qn.b, K.ident.b], w=[tp.b])
                c.op(c.act, lambda: nc.scalar.copy(QK[0:64, :, ch * 512:(ch + 1) * 512].rearrange("p g (t c) -> p g t c", t=4),
                                                   tp[0:64, :, :, :].rearrange("p t g c -> p g t c")), r=[tp.b], w=[QK.bs[ch]])
            c.barrier()
        with ExitStack() as s2:
            NS, NP, NV, NVT = 3, 5, 32, 3
            Sb = [ps(s2, nc, f"Sd{i}", [128, 256], F32) for i in range(NS)]
            Op = [ps(s2, nc, f"Od{i}", [128, 128], F32) for i in range(2)]
            vtp = [ps(s2, nc, f"vtp{i}", [128, 64], BF16) for i in range(NVT)]
            P = [sb(s2, nc, f"Pd{i}", [128, 256], BF16) for i in range(NP)]
            Vd = [sb(s2, nc, f"Vd{i}", [128, 128], BF16) for i in range(NV)]
            ACC = [sb(s2, nc, f"ACC{i}", [128, 2048], F32) for i in range(2)]
            W2 = {k: [sb(s2, nc, f"g{k}{i}", [128, 2048], dt) for i in range(1)] for k, dt in
                  [("rl", F32), ("os", F32), ("gf", F32), ("ot", BF16)]}
            cnt = {"nv": [0, 0], "nvt": 0}
            LA = 3
            NVH = NV // 2
            for k_, v in enumerate(Vd):
                c.op(c.dve, lambda v=v: nc.vector.memset(v[:, :], 1.0), w=[v.b])

            def make_vd(h, start, d):
                hr = slice(h * 64, (h + 1) * 64)
                vcol = slice(0, 64) if h == 0 else slice(64, 128)
                vd = Vd[h * NVH + cnt["nv"][h] % NVH]
                cnt["nv"][h] += 1
                vt = vtp[cnt["nvt"] % NVT]
                cnt["nvt"] += 1
                chs = sorted(set([start // 512, (start + 127 * d) // 512]))
                c.op(c.pe, lambda: nc.tensor.transpose(vt[:, :], VT[hr, start:start + 127 * d + 1:d], K.ident[hr, hr]),
                     r=[VT.bs[x] for x in range(chs[0], chs[-1] + 1)] + [K.ident.b], w=[vt.b])
                c.op(c.dve, lambda: nc.vector.tensor_copy(vd[:, vcol], vt[:, :]), r=[vt.b], w=[vd.b])
                return vd

            blocks = []
            for SBk in range(LIM["nsb"]):
                for h in range(2):
                    for d in (1, 4, 16):
                        nbq = 16 // d
                        for r_ in range(d):
                            for ii in range(nbq):
                                blocks.append((SBk, d, r_, ii, nbq, h))
            state = {"vd_prev": None}
            info = {}

            def emit_A(n):
                SBk, d, r_, ii, nbq, h = blocks[n]
                if ii == 0:
                    state["vd_prev"] = None
                i = nbq * SBk + ii
                start = 128 * i * d + r_
                pstart = start - 128 * d
                Sx, Px = Sb[n % NS], P[n % NP]
                c0, c1 = start // 512, (start + 127 * d) // 512
                rd = [QK.bs[x] for x in range(c0, c1 + 1)]
                qT = QK[0:64, h, start:start + 127 * d + 1:d]
                c.op(c.pe, lambda: nc.tensor.matmul(Sx[:, 128:256], QK[0:64, 2 + h, start:start + 127 * d + 1:d], qT, start=True, stop=True),
                     r=rd, w=[Sx.b])
                lo = 128
                vd_prev = state["vd_prev"]
                if i > 0:
                    lo = 0
                    p0, p1 = pstart // 512, (pstart + 127 * d) // 512
                    rdp = rd + [QK.bs[x] for x in range(p0, p1 + 1)]
                    c.op(c.pe, lambda: nc.tensor.matmul(Sx[:, 0:128], QK[0:64, 2 + h, pstart:pstart + 127 * d + 1:d], qT, start=True, stop=True),
                         r=rdp, w=[Sx.b])
                    if vd_prev is None:
                        vd_prev = make_vd(h, pstart, d)
                vd_own = make_vd(h, start, d)
                c.op(c.act, lambda: nc.scalar.activation(Px[:, lo:256], Sx[:, lo:256], AF.Exp), r=[Sx.b], w=[Px.b])
                c.op(c.dve, lambda: nc.vector.tensor_tensor(Px[:, lo:256], Px[:, lo:256], K.m01[:, lo:256], ALU.mult), r=[Px.b, K.m01.b], w=[Px.b])
                info[n] = (i, start, vd_prev, vd_own)
                state["vd_prev"] = vd_own

            def emit_B(n):
                SBk, d, r_, ii, nbq, h = blocks[n]
                i, start, vd_prev, vd_own = info.pop(n)
                Px = P[n % NP]
                O = Op[n % 2]
                acc = ACC[h]
                if i > 0:
                    c.op(c.pe, lambda: nc.tensor.matmul(O[:, :], vd_prev[:, :], Px[:, 0:128], start=True, stop=False),
                         r=[vd_prev.b, Px.b], w=[O.b])
                c.op(c.pe, lambda: nc.tensor.matmul(O[:, :], vd_own[:, :], Px[:, 128:256], start=(i == 0), stop=True),
                     r=[vd_own.b, Px.b], w=[O.b])
                off = start - 2048 * SBk
                av = acc[:, off:off + 127 * d + 1:d]
                if d == 1:
                    c.op(c.dve, lambda: nc.vector.tensor_copy(av, O[:, :]), r=[O.b], w=[acc.b])
                else:
                    c.op(c.dve, lambda: nc.vector.tensor_tensor(av, O[:, :], av, ALU.add), r=[O.b, acc.b], w=[acc.b])
                last = (n + 1 == len(blocks)) or (blocks[n + 1][0] != SBk) or (blocks[n + 1][5] != h)
                if last:
                    finalize_out(c, nc, W2, acc, h, 2048, Gt, SBk * 2048, 128, G, l, SBk * 2048, 0, 2)
                    if h == 1 and LIM["gather"]:
                        gather_o(c, G, l, SBk, 1)

            for n in range(len(blocks) + LA):
                if n < len(blocks):
                    emit_A(n)
                if n - LA >= 0:
                    emit_B(n - LA)
            c.barrier()


def build_rope(c, nc, st, K, G):
    cs2 = sb(st, nc, "cs2", [128, 64, 16], F32)
    sn2 = sb(st, nc, "sn2", [128, 64, 16], F32)
    rb = Buf("rope")
    with ExitStack() as s1:
        pi_ = sb(s1, nc, "posi", [128, 64], I32)
        pf = sb(s1, nc, "posf", [128, 64], F32)
        ang = sb(s1, nc, "ang", [128, 64, 8], F32)
        a2 = sb(s1, nc, "ang2", [128, 64, 8], F32)
        kf = sb(s1, nc, "kf", [128, 64, 8], F32)
        ki = sb(s1, nc, "ki", [128, 64, 8], I32)
        c.dma(c.sp, pi_[:, :], G["pos"][:, :], r=[], w=[pi_.b])
        c.op(c.dve, lambda: nc.vector.tensor_copy(pf[:, :], pi_[:, :]), r=[pi_.b], w=[pf.b])
        c.op(c.dve, lambda: nc.vector.tensor_tensor(ang[:, :, :], pf[:, :].unsqueeze(2).to_broadcast([128, 64, 8]),
                                                     K.invf[:, :].unsqueeze(1).to_broadcast([128, 64, 8]), ALU.mult),
             r=[pf.b, K.invf.b], w=[ang.b])

        def sin_of(dst_ap, shift, scale):
            c.op(c.dve, lambda: nc.vector.tensor_scalar(a2[:, :, :], ang[:, :, :], shift + float(np.pi), None, ALU.add), r=[ang.b], w=[a2.b])
            c.op(c.dve, lambda: nc.vector.tensor_scalar(kf[:, :, :], a2[:, :, :], 1.0 / TWO_PI, None, ALU.mult), r=[a2.b], w=[kf.b])
            c.op(c.dve, lambda: nc.vector.tensor_copy(ki[:, :, :], kf[:, :, :]), r=[kf.b], w=[ki.b])
            c.op(c.dve, lambda: nc.vector.tensor_copy(kf[:, :, :], ki[:, :, :]), r=[ki.b], w=[kf.b])
            c.op(c.dve, lambda: nc.vector.scalar_tensor_tensor(a2[:, :, :], kf[:, :, :], -TWO_PI, a2[:, :, :], ALU.mult, ALU.add), r=[kf.b, a2.b], w=[a2.b])
            c.op(c.dve, lambda: nc.vector.tensor_scalar(kf[:, :, :], a2[:, :, :], 0.0, TWO_PI, ALU.is_lt, ALU.mult), r=[a2.b], w=[kf.b])
            c.op(c.dve, lambda: nc.vector.tensor_tensor(a2[:, :, :], a2[:, :, :], kf[:, :, :], ALU.add), r=[a2.b, kf.b], w=[a2.b])
            c.op(c.dve, lambda: nc.vector.tensor_scalar(kf[:, :, :], a2[:, :, :], TWO_PI, -TWO_PI, ALU.is_ge, ALU.mult), r=[a2.b], w=[kf.b])
            c.op(c.dve, lambda: nc.vector.tensor_tensor(a2[:, :, :], a2[:, :, :], kf[:, :, :], ALU.add), r=[a2.b, kf.b], w=[a2.b])
            c.op(c.act, lambda: nc.scalar.activation(a2[:, :, :], a2[:, :, :], AF.Sin, bias=K.negpi[:, 0:1], scale=1.0), r=[a2.b, K.negpi.b], w=[a2.b])
            c.op(c.dve, lambda: nc.vector.tensor_scalar(dst_ap, a2[:, :, :], scale, None, ALU.mult), r=[a2.b], w=[rb])

        sin_of(cs2[:, :, 0:8], float(np.pi / 2), 1.0)
        sin_of(cs2[:, :, 8:16], float(np.pi / 2), 1.0)
        sin_of(sn2[:, :, 0:8], 0.0, -1.0)
        sin_of(sn2[:, :, 8:16], 0.0, 1.0)
        c.barrier()
    return {"cs2": cs2, "sn2": sn2, "b": rb}


def phase_b(c, nc, K, G, l, last):
    with ExitStack() as st:
        Wo = sb(st, nc, "Wo", [128, 8, 1024], BF16)
        Wg = sb(st, nc, "Wg", [128, 8, 1024], BF16)
        Wp = sb(st, nc, "Wp", [128, 2, 1024], BF16)
        PT = sb(st, nc, "PT", [128, 2, TSH], BF16)
        c.dma(c.sp, Wo[:, :, :], G["wo16"][l].ap().rearrange("(k p) n -> p k n", p=128), r=[G["wo16_b"][l]], w=[Wo.b])
        c.dma(c.sp, Wg[:, :, :], G["wg16"][l].ap().rearrange("(k p) n -> p k n", p=128), r=[G["wg16_b"][l]], w=[Wg.b])
        c.dma(c.sp, Wp[:, :, :], G["wp16"][l].ap().rearrange("(k p) n -> p k n", p=128), r=[G["wp16_b"][l]], w=[Wp.b])
        c.dma(c.sp, PT[:, :, :], G["pT16"][l].ap().rearrange("(k p) t -> p k t", p=128), r=[G["pT16_b"][l]], w=[PT.b])
        gple = sb(st, nc, "gple", [128, 1024], F32)
        c.dma(c.sp, gple[:, :], G["smalls"][:, l * 2560 + 1024:l * 2560 + 2048], r=[], w=[gple.b])
        gnext = None
        if not last:
            gnext = sb(st, nc, "gnext", [128, 1024], F32)
            c.dma(c.sp, gnext[:, :], G["smalls"][:, (l + 1) * 2560:(l + 1) * 2560 + 1024], r=[], w=[gnext.b])
            nt = NormT(c, nc, st, K, "nb")
        OT = [sb(st, nc, f"OTi{i}", [128, 8, 512], BF16) for i in range(2)]
        H = [sb(st, nc, f"H{i}", [128, 1024], F32) for i in range(2)]
        H2 = [sb(st, nc, f"H2{i}", [128, 1024], F32) for i in range(2)]
        sq = sb(st, nc, "bsq", [128, 1024], BF16)
        ss = [sb(st, nc, f"bss{i}", [128, 1], F32) for i in range(2)]
        rs = [sb(st, nc, f"brs{i}", [128, 1], F32) for i in range(2)]
        Vn = [sb(st, nc, f"Vn{i}", [128, 1024], BF16) for i in range(2)]
        VTs = [sb(st, nc, f"VTs{i}", [128, 1024], BF16) for i in range(2)]
        Eg = [sb(st, nc, f"Egb{i}", [128, 1024], F32) for i in range(2)]
        TM = [sb(st, nc, f"TM{i}", [128, 1024], F32) for i in range(2)]
        X = [[ps(st, nc, f"X{i}{hf}", [128, 512], F32) for hf in range(2)] for i in range(2)]
        Gp = [ps(st, nc, f"Gp{i}", [128, 512], F32) for i in range(2)]
        vtp = ps(st, nc, "vtpb", [128, 1024], BF16)
        hsrc = G["x_sh"] if l == 0 else G["h_loc"].ap()
        hsrc_b = [Buf("xin")] if l == 0 else G["h_loc_b"]

        def load_ot(ch):
            o_ = OT[ch % 2]
            src = G["oT_mine"][l].ap().rearrange("(q p) t -> p q t", p=128)[:, :, ch * 512:(ch + 1) * 512]
            c.dma(c.sp, o_[:, :, :], src, r=[G["oT_mine_b"][l]], w=[o_.b])
            return o_

        def load_h(t_):
            h_ = H[t_ % 2]
            c.dma(c.sp, h_[:, :], hsrc[t_ * 128:(t_ + 1) * 128, :], r=hsrc_b, w=[h_.b])
            return h_

        ots = {0: load_ot(0)}

        def stage_a(t):
            i = t % 2
            ot = ots[t // 4]
            if t % 4 == 1 and t // 4 + 1 < 4:
                ots[t // 4 + 1] = load_ot(t // 4 + 1)
            h = load_h(t)
            tt = t % 4
            for half in range(2):
                a = X[i][half]
                for kc in range(8):
                    c.op(c.pe, lambda kc=kc: nc.tensor.matmul(a[:, :], ot[:, kc, tt * 128:(tt + 1) * 128],
                                                              Wo[:, kc, half * 512:(half + 1) * 512],
                                                              start=(kc == 0), stop=(kc == 7)),
                         r=[ot.b, Wo.b], w=[a.b])
                c.op(c.dve, lambda a=a, half=half: nc.vector.tensor_tensor(h[:, half * 512:(half + 1) * 512], a[:, :], h[:, half * 512:(half + 1) * 512], ALU.add),
                     r=[a.b, h.b], w=[h.b])
            c.op(c.act, lambda: nc.scalar.activation(sq[:, :], h[:, :], AF.Square, accum_out=ss[i][:, 0:1]), r=[h.b], w=[sq.b, ss[i].b])
            rstd_from_ss(c, nc, ss[i], rs[i], 1, K, 1.0 / D)
            vn, vts = Vn[i], VTs[i]
            c.op(c.dve, lambda: nc.vector.scalar_tensor_tensor(vn[:, :], h[:, :], rs[i][:, 0:1], gple[:, :], ALU.mult, ALU.mult),
                 r=[h.b, rs[i].b, gple.b], w=[vn.b])
            for k in range(8):
                c.op(c.pe, lambda k=k: nc.tensor.transpose(vtp[:, k * 128:(k + 1) * 128], vn[:, k * 128:(k + 1) * 128], K.ident[:, :]),
                     r=[vn.b, K.ident.b], w=[vtp.b])
            c.op(c.act, lambda: nc.scalar.copy(vts[:, :], vtp[:, :]), r=[vtp.b], w=[vts.b])

        def stage_b(t):
            i = t % 2
            h, vts, eg, tm = H[i], VTs[i], Eg[i], TM[i]
            for half in range(2):
                g_ = Gp[half]
                for kc in range(8):
                    c.op(c.pe, lambda kc=kc: nc.tensor.matmul(g_[:, :], vts[:, kc * 128:(kc + 1) * 128], Wg[:, kc, half * 512:(half + 1) * 512],
                                                              start=(kc == 0), stop=(kc == 7)),
                         r=[vts.b, Wg.b], w=[g_.b])
                c.op(c.act, lambda g_=g_, half=half: nc.scalar.activation(eg[:, half * 512:(half + 1) * 512], g_[:, :], AF.Exp, scale=-1.0),
                     r=[g_.b], w=[eg.b])
            c.op(c.act, lambda: nc.scalar.activation(eg[:, :], eg[:, :], AF.Ln, bias=K.one[:, 0:1], scale=1.0), r=[eg.b, K.one.b], w=[eg.b])
            c.op(c.act, lambda: nc.scalar.activation(eg[:, :], eg[:, :], AF.Exp, scale=-1.0), r=[eg.b], w=[eg.b])
            for half in range(2):
                p_ = X[i][half]
                for kc in range(2):
                    c.op(c.pe, lambda kc=kc: nc.tensor.matmul(p_[:, :], PT[:, kc, t * 128:(t + 1) * 128], Wp[:, kc, half * 512:(half + 1) * 512],
                                                              start=(kc == 0), stop=(kc == 1)),
                         r=[PT.b, Wp.b], w=[p_.b])
                c.op(c.dve, lambda p_=p_, half=half: nc.vector.tensor_tensor(tm[:, half * 512:(half + 1) * 512], p_[:, :], eg[:, half * 512:(half + 1) * 512], ALU.mult),
                     r=[p_.b, eg.b], w=[tm.b])
            h2 = H2[i]
            c.op(c.dve, lambda: nc.vector.tensor_tensor(h2[:, :], h[:, :], tm[:, :], ALU.add), r=[h.b, tm.b], w=[h2.b])
            if last:
                c.dma(c.sp, G["out"][t * 128:(t + 1) * 128, :], h2[:, :], r=[h2.b], w=[G["out_b"][i]], waw=False)
            else:
                c.dma(c.sp, G["h_loc"].ap()[t * 128:(t + 1) * 128, :], h2[:, :], r=[h2.b], w=[G["h_loc_b"][i]], waw=False)
                nt.emit(h2[:, :], h2.b, gnext, t, G["uT_loc"][l + 1], G["uT_loc_b"][l + 1], G["uT_all"][l + 1], G["uT_all_b"][l + 1])

        stage_a(0)
        for t in range(16):
            if t + 1 < 16:
                stage_a(t + 1)
            stage_b(t)
        c.barrier()


def build_program(n_layers=2, debug=False, stop=None):
    _UID[0] = 0
    nc = bass.Bass("TRN2", target_bir_lowering=False)
    G = {}
    ei = lambda n, s_, d: nc.dram_tensor(n, s_, d, kind="ExternalInput").ap()
    G["x_sh"] = ei("x_sh", [TSH, D], F32)
    G["pT"] = ei("pT", [2, 256, TSH], F32)
    G["pos"] = ei("pos", [128, 64], I32)
    G["w_a"] = ei("w_a", [2, D, 514], F32)
    G["w_b"] = ei("w_b", [2, D, 512], F32)
    G["w_out"] = ei("w_out", [2, D, D], F32)
    G["w_gate"] = ei("w_gate", [2, D, D], F32)
    G["w_ple"] = ei("w_ple", [2, 256, D], F32)
    G["smalls"] = ei("smalls", [128, 5120], F32)
    G["consts"] = ei("consts", [128, 392], F32)
    G["bf"] = ei("bf", [2, 128, 2], F32)
    G["out"] = nc.dram_tensor("out", [TSH, D], F32, kind="ExternalOutput").ap()
    G["out_b"] = [Buf("out0", True), Buf("out1", True)]
    if debug:
        G["dbg_o"] = nc.dram_tensor("dbg_o", [256, S], BF16, kind="ExternalOutput").ap()
        G["dbg_u"] = nc.dram_tensor("dbg_u", [8 * 128, TSH], BF16, kind="ExternalOutput").ap()
    for nm, shp, src in (("wa16", [D, 514], "w_a"), ("wb16", [D, 512], "w_b"), ("wo16", [D, D], "w_out"),
                         ("wg16", [D, D], "w_gate"), ("wp16", [256, D], "w_ple"), ("pT16", [256, TSH], "pT")):
        G[nm] = [nc.dram_tensor(f"{nm}_{l}", shp, BF16) for l in range(2)]
        G[nm + "_b"] = [Buf(f"{nm}{l}", True) for l in range(2)]
        G[nm + "_src"] = src
    uT_loc = [nc.dram_tensor(f"uT_loc_{ch}", [8 * 128, 512], BF16) for ch in range(4)]
    uT_all = [nc.dram_tensor(f"uT_all_{ch}", [4 * 8 * 128, 512], BF16) for ch in range(4)]
    oT_loc = nc.dram_tensor("oT_loc", [4 * 256, TSH], BF16)
    oT_all = nc.dram_tensor("oT_all", [4 * 4 * 256, TSH], BF16)
    oT_mine = nc.dram_tensor("oT_mine", [4 * 256, TSH], BF16)
    G["uT_loc"] = [uT_loc, uT_loc]
    G["uT_all"] = [uT_all, uT_all]
    G["oT_loc"] = [oT_loc, oT_loc]
    G["oT_all"] = [oT_all, oT_all]
    G["oT_mine"] = [oT_mine, oT_mine]
    G["h_loc"] = nc.dram_tensor("h_loc", [TSH, D], F32)
    G["h_loc_b"] = [Buf("h_loc0", True), Buf("h_loc1", True)]
    b1 = [Buf(f"uTl{i}", True) for i in range(4)]
    G["uT_loc_b"] = [b1, b1]
    b2 = [Buf(f"uTa{i}", True) for i in range(4)]
    G["uT_all_b"] = [b2, b2]
    b3 = [[Buf(f"oTl{q}_{i}", True) for i in range(3)] for q in range(4)]
    G["oT_loc_b"] = [b3, b3]
    b4 = [[Buf(f"oTa{q}_{y}", True) for y in range(2)] for q in range(4)]
    G["oT_all_b"] = [b4, b4]
    G["oT_mine_b"] = [Buf("oTmine0", True), Buf("oTmine1", True)]
    with ExitStack() as st:
        c = Ctx(nc, st)
        def precast(names, l):
            for nm in names:
                c.dma(c.pool, G[nm][l].ap()[:, :], G[G[nm + "_src"]][l], r=[], w=[G[nm + "_b"][l]])

        precast(["wa16"], 0)
        K = load_consts(c, nc, st, G)
        ROPE = build_rope(c, nc, st, K, G)
        if stop not in ("foxsim", "dilsim", "bsim"):
            phase_norm0(c, nc, K, G)
            c.recycle()
        for l in range(n_layers):
            if stop == "norm0":
                break
            if stop in ("foxsim", "dilsim", "bsim"):
                precast(["wb16", "wo16", "wg16", "wp16", "pT16"], 0)
            if stop == "foxsim":
                phase_fox(c, nc, K, G, l)
                break
            if stop == "dilsim":
                phase_dil(c, nc, K, G, l, ROPE)
                break
            if stop == "bsim":
                phase_b(c, nc, K, G, l, last=False)
                break
            if stop == "ag":
                break
            if l == 0:
                precast(["wb16", "wo16", "wg16", "wp16", "pT16"], 0)
                if n_layers > 1:
                    precast(["wa16", "wb16", "wo16", "wg16", "wp16", "pT16"], 1)
            phase_fox(c, nc, K, G, l)
            c.recycle()
            for q in range(4):
                gather_o(c, G, l, q, 0)
            if stop == "fox":
                break
            phase_dil(c, nc, K, G, l, ROPE)
            c.recycle()
            if stop == "dil":
                break
            rank = nc.gpsimd.partition_id() % 4
            src = G["oT_all"][l].ap()[bass.ds(rank * 1024, 1024), :]
            c.dma(c.pool, G["oT_mine"][l].ap()[:, :], src, r=[b_ for qq in G["oT_all_b"][l] for b_ in qq], w=[G["oT_mine_b"][l]])
            phase_b(c, nc, K, G, l, last=(l == 1))
            c.recycle()
        if n_layers == 1 and stop is None:
            c.dma(c.pool, G["out"][:, :], G["h_loc"].ap()[:, :], r=G["h_loc_b"], w=[G["out_b"][0]])
        if debug:
            db = Buf("dbg", True)
            if stop == "dil":
                for q in range(4):
                    c.dma(c.pool, G["dbg_o"][:, q * TSH:(q + 1) * TSH], G["oT_loc"][0].ap()[q * 256:(q + 1) * 256, :], r=G["oT_loc_b"][0][q], w=[db])
            if stop is not None:
                for ch in range(4):
                    c.dma(c.pool, G["dbg_u"][:, ch * 512:(ch + 1) * 512], G["uT_loc"][0][ch].ap()[:, :], r=[G["uT_loc_b"][0][ch]], w=[db])
            c.wait_buf(c.sp, db)
        G["nsem"] = c.nsem
        for ob in G["out_b"]:
            c.wait_buf(c.sp, ob)
            c.wait_buf(c.pool, ob)
    return nc


def make_consts():
    cst = np.zeros((128, 392), np.float32)
    s_ = np.arange(128)[:, None]
    t_ = np.arange(128)[None, :]
    cst[:, 0:128] = (s_ == t_)
    cst[:, 128:256] = np.where(t_ >= s_, 0.0, NEG)
    cst[:, 256:384] = np.where(t_ <= s_, 0.0, NEG)
    cst[:, 384:392] = (500000.0 ** (-np.arange(8, dtype=np.float32) / 8.0))[None, :]
    return cst


def make_in_maps(x, p, positions, norm_g, w_in, b_f, qk_norm_g, w_out, w_ple, ple_norm_g, w_ple_gate):
    f = lambda a: np.ascontiguousarray(np.asarray(a, dtype=np.float32))
    x, p, norm_g, w_in, b_f, qk_norm_g, w_out, w_ple, ple_norm_g, w_ple_gate = map(
        f, (x, p, norm_g, w_in, b_f, qk_norm_g, w_out, w_ple, ple_norm_g, w_ple_gate))
    positions = np.asarray(positions).astype(np.int32)
    cst = make_consts()
    smalls = np.zeros((128, 5120), np.float32)
    for l in range(2):
        o = l * 2560
        smalls[:, o:o + 1024] = norm_g[l][None]
        smalls[:, o + 1024:o + 2048] = ple_norm_g[l][None]
        g = qk_norm_g[l]
        smalls[:, o + 2048:o + 2304] = np.concatenate([g[0], g[0], g[1], g[1]])[None]
        smalls[:, o + 2304:o + 2560] = np.concatenate([g[2], g[2], g[3], g[3]])[None]
    maps = []
    for core in range(NCORES):
        b, r = core // 4, core % 4
        hs = slice(128 * r, 128 * r + 128)
        cols_a = np.concatenate([np.arange(0, 512)[hs], np.arange(512, 1024)[hs], np.arange(1024, 1536)[hs],
                                 np.array([2048 + 2 * r, 2048 + 2 * r + 1]), np.arange(1536, 2048)[hs]])
        base = 2056
        cols_b = np.concatenate([base + np.arange(0, 512)[hs], base + np.arange(512, 1024)[hs],
                                 base + np.arange(1024, 1536)[hs], base + np.arange(1536, 2048)[hs]])
        m = {
            "x_sh": np.ascontiguousarray(x[b, r * TSH:(r + 1) * TSH]),
            "pT": np.ascontiguousarray(p[:, b, r * TSH:(r + 1) * TSH, :].transpose(0, 2, 1)),
            "pos": np.ascontiguousarray(positions[b].reshape(64, 128).T),
            "w_a": np.ascontiguousarray(w_in[:, :, cols_a]),
            "w_b": np.ascontiguousarray(w_in[:, :, cols_b]),
            "w_out": w_out, "w_gate": w_ple_gate, "w_ple": w_ple,
            "smalls": smalls, "consts": cst,
            "bf": np.ascontiguousarray(np.broadcast_to(b_f[:, None, 2 * r:2 * r + 2], (2, 128, 2))),
        }
        maps.append(m)
    return maps


_NC_CACHE = {}


def kernel(x, p, positions, norm_g, w_in, b_f, qk_norm_g, w_out, w_ple, ple_norm_g, w_ple_gate):
    maps = make_in_maps(x, p, positions, norm_g, w_in, b_f, qk_norm_g, w_out, w_ple, ple_norm_g, w_ple_gate)
    nc = build_program()
    res = run_bass_kernel_spmd(nc, maps, core_ids=list(range(NCORES)))
    out = np.empty((2, S, D), np.float32)
    for core in range(NCORES):
        b, r = core // 4, core % 4
        out[b, r * TSH:(r + 1) * TSH] = res.results[core]["out"]
    return out
```

```python
import numpy as np
from contextlib import ExitStack
import concourse.bass as bass
import concourse.mybir as mybir
from concourse.bass_utils import run_bass_kernel_spmd

F32 = mybir.dt.float32
BF16 = mybir.dt.bfloat16
I32 = mybir.dt.int32
AF = mybir.ActivationFunctionType
ALU = mybir.AluOpType
AX = mybir.AxisListType

NCORES = 8
S = 8192
D = 1024
TSH = 2048
NEG = -30000.0
EPS = 1e-6
TWO_PI = float(2 * np.pi)
LIM = {"nch": 16, "nsb": 4, "attn": True, "stage1": 99, "sub": 99, "gather": True}


class Buf:
    __slots__ = ("name", "writer", "readers", "dsem", "dcount", "persist")

    def __init__(self, name="", persist=False):
        self.name = name
        self.persist = persist
        self.writer = None
        self.readers = {}
        self.dsem = None
        self.dcount = 0


class Eng:
    def __init__(self, ctx, name, raw, is_pe=False):
        self.name = name
        self.raw = raw
        self.sem = ctx.new_sem("e_" + name)
        self.count = 0
        self.seen = {}
        self.is_pe = is_pe


class Ctx:
    def __init__(self, nc, stack):
        self.nc = nc
        self.stack = stack
        self.sems = {}
        self.nsem = 0
        self.pe = Eng(self, "pe", nc.tensor, is_pe=True)
        self.act = Eng(self, "act", nc.scalar)
        self.dve = Eng(self, "dve", nc.vector)
        self.pool = Eng(self, "pool", nc.gpsimd)
        self.sp = Eng(self, "sp", nc.sync)
        self.dbufs = []
        self.free_sems = []
        self.scope_bufs = []

    def new_sem(self, name):
        self.nsem += 1
        s = self.stack.enter_context(self.nc.semaphore(f"{name}_{self.nsem}"))
        self.sems[id(s)] = s
        return s

    def _wait(self, eng, deps):
        for sem, val in deps:
            k = id(sem)
            if eng.seen.get(k, 0) < val:
                eng.raw.wait_ge(sem, val)
                eng.seen[k] = val

    def _deps(self, r, w, waw=True):
        deps = []
        for b in r:
            if b.writer is not None:
                deps.append(b.writer)
        for b in w:
            if b.writer is not None and waw:
                deps.append(b.writer)
            for k, v in b.readers.items():
                deps.append((self.sems[k], v))
        return deps

    def op(self, eng, fn, r=(), w=()):
        own = id(eng.sem)
        if eng.is_pe:
            deps = [d for d in self._deps(r, w) if id(d[0]) != own]
        else:
            deps = self._deps(r, w)
        self._wait(eng, deps)
        ins = fn()
        ins.then_inc(eng.sem, 1)
        eng.count += 1
        for b in r:
            b.readers[own] = eng.count
        for b in w:
            b.writer = (eng.sem, eng.count)
            b.readers = {}
        return ins

    def dma(self, q, out_ap, in_ap, r=(), w=(), waw=True, **kw):
        deps = self._deps(r, w, waw=waw)
        self._wait(q, deps)
        dst = w[0]
        if dst.dsem is None:
            if self.free_sems and q is not self.pool:
                dst.dsem, dst.dcount = self.free_sems.pop()
            else:
                dst.dsem = self.new_sem("d_" + dst.name)
            self.dbufs.append(dst)
            if not dst.persist:
                self.scope_bufs.append(dst)
        ins = q.raw.dma_start(out=out_ap, in_=in_ap, **kw)
        ins.then_inc(dst.dsem, 16)
        dst.dcount += 16
        k = id(dst.dsem)
        for b in r:
            b.readers[k] = dst.dcount
        old_readers = dst.readers if not waw else {}
        dst.writer = (dst.dsem, dst.dcount)
        dst.readers = {}
        return ins

    def allgather(self, src_ap, src_b, dst_ap, dst_b):
        q = self.pool
        self._wait(q, self._deps(src_b, [dst_b]))
        if dst_b.dsem is None:
            dst_b.dsem = self.new_sem("cc_" + dst_b.name)
        ins = self.nc.gpsimd.collective_compute(
            "AllGather", ALU.bypass, replica_groups=[[0, 1, 2, 3], [4, 5, 6, 7]],
            ins=[src_ap.opt()], outs=[dst_ap.opt()])
        ins.then_inc(dst_b.dsem, 1)
        dst_b.dcount += 1
        for sb_ in src_b:
            sb_.readers[id(dst_b.dsem)] = dst_b.dcount
        dst_b.writer = (dst_b.dsem, dst_b.dcount)
        dst_b.readers = {}

    def barrier(self):
        engs = [self.pe, self.act, self.dve, self.pool, self.sp]
        deps = [(e.sem, e.count) for e in engs if e.count > 0]
        deps += [(b.dsem, b.dcount) for b in self.dbufs if b.dcount > 0]
        for e in engs:
            self._wait(e, [d for d in deps if id(d[0]) != id(e.sem) or not e.is_pe])

    def recycle(self):
        for b in self.scope_bufs:
            self.free_sems.append((b.dsem, b.dcount))
            self.dbufs.remove(b)
            b.dsem = None
        self.scope_bufs = []

    def wait_buf(self, eng, b):
        if b.writer is not None:
            self._wait(eng, [b.writer])


class T:
    def __init__(self, t, name, nb=1):
        self.t = t
        self.b = Buf(name)
        self.bs = [Buf(f"{name}{i}") for i in range(nb)]

    def __getitem__(self, k):
        return self.t[k]


_UID = [0]


def _un(name):
    _UID[0] += 1
    return f"{name}_{_UID[0]}"


def sb(st, nc, name, shape, dt, nb=1):
    name = _un(name)
    return T(st.enter_context(nc.sbuf_tensor(name, shape, dt)), name, nb)


def ps(st, nc, name, shape, dt, nb=1):
    name = _un(name)
    return T(st.enter_context(nc.psum_tensor(name, shape, dt)), name, nb)


class Consts:
    pass


def load_consts(c, nc, st, G):
    K = Consts()
    cf = sb(st, nc, "cf", [128, 392], F32)
    c.dma(c.sp, cf[:, :], G["consts"][:, :], r=[], w=[cf.b])
    K.ident = sb(st, nc, "ident", [128, 128], BF16)
    K.mge = sb(st, nc, "mge", [128, 128], BF16)
    K.mle = sb(st, nc, "mle", [128, 128], BF16)
    K.identf = sb(st, nc, "identf", [128, 128], F32)
    K.tri = sb(st, nc, "tri", [128, 128], F32)
    K.l127 = sb(st, nc, "l127", [128, 128], F32)
    K.invf = sb(st, nc, "invf", [128, 8], F32)
    K.m01 = sb(st, nc, "m01", [128, 256], BF16)
    K.eps = sb(st, nc, "epsc", [128, 1], F32)
    K.one = sb(st, nc, "onec", [128, 1], F32)
    K.negpi = sb(st, nc, "negpic", [128, 1], F32)
    c.op(c.dve, lambda: nc.vector.tensor_copy(K.ident[:, :], cf[:, 0:128]), r=[cf.b], w=[K.ident.b])
    c.op(c.dve, lambda: nc.vector.tensor_copy(K.mge[:, :], cf[:, 128:256]), r=[cf.b], w=[K.mge.b])
    c.op(c.dve, lambda: nc.vector.tensor_copy(K.mle[:, :], cf[:, 256:384]), r=[cf.b], w=[K.mle.b])
    c.op(c.dve, lambda: nc.vector.tensor_copy(K.invf[:, :], cf[:, 384:392]), r=[cf.b], w=[K.invf.b])
    c.op(c.dve, lambda: nc.vector.tensor_copy(K.identf[:, :], cf[:, 0:128]), r=[cf.b], w=[K.identf.b])
    c.op(c.dve, lambda: nc.vector.tensor_scalar(K.tri[:, :], cf[:, 128:256], 0.0, None, ALU.is_equal), r=[cf.b], w=[K.tri.b])
    c.op(c.dve, lambda: nc.vector.tensor_scalar(K.m01[:, 0:128], cf[:, 256:384], 0.0, None, ALU.is_equal), r=[cf.b], w=[K.m01.b])
    c.op(c.dve, lambda: nc.vector.tensor_scalar(K.m01[:, 128:256], cf[:, 128:256], 0.0, None, ALU.is_equal), r=[cf.b], w=[K.m01.b])
    c.op(c.dve, lambda: nc.vector.memset(K.l127[:, :], 0.0), w=[K.l127.b])
    c.op(c.dve, lambda: nc.vector.memset(K.l127[96:128, :], 1.0), w=[K.l127.b])
    c.op(c.dve, lambda: nc.vector.tensor_scalar(K.l127[:, :], K.l127[:, :], cf[:, 127:128], None, ALU.mult), r=[K.l127.b, cf.b], w=[K.l127.b])
    c.op(c.dve, lambda: nc.vector.memset(K.eps[:, :], EPS), w=[K.eps.b])
    c.op(c.dve, lambda: nc.vector.memset(K.one[:, :], 1.0), w=[K.one.b])
    c.op(c.dve, lambda: nc.vector.memset(K.negpi[:, :], -float(np.pi)), w=[K.negpi.b])
    return K


def rstd_from_ss(c, nc, ss, rs, n, K, inv_n):
    c.op(c.act, lambda: nc.scalar.activation(rs[:, 0:n], ss[:, 0:n], AF.Ln, bias=K.eps[:, 0:1], scale=inv_n),
         r=[ss.b, K.eps.b], w=[rs.b])
    c.op(c.act, lambda: nc.scalar.activation(rs[:, 0:n], rs[:, 0:n], AF.Exp, scale=-0.5), r=[rs.b], w=[rs.b])


class NormT:
    def __init__(self, c, nc, st, K, name):
        self.c, self.nc, self.K = c, nc, K
        self.sq = sb(st, nc, name + "sq", [128, 1024], BF16)
        self.ss = [sb(st, nc, name + f"ss{i}", [128, 1], F32) for i in range(2)]
        self.rs = [sb(st, nc, name + f"rs{i}", [128, 1], F32) for i in range(2)]
        self.u = [sb(st, nc, name + f"u{i}", [128, 1024], BF16) for i in range(2)]
        self.tp = [ps(st, nc, name + f"tp{i}", [128, 1024], BF16) for i in range(1)]
        self.stg = [sb(st, nc, name + f"stg{i}", [128, 8, 512], BF16) for i in range(2)]
        self.n = 0

    def emit(self, h_ap, h_buf, g_t, t, uT_dram, uT_buf, uT_all=None, uT_all_b=None):
        c, nc, K = self.c, self.nc, self.K
        i = self.n % 2
        self.n += 1
        ss, rs, u, tp = self.ss[i], self.rs[i], self.u[i], self.tp[0]
        stg = self.stg[(t // 4) % 2]
        c.op(c.act, lambda: nc.scalar.activation(self.sq[:, :], h_ap, AF.Square, accum_out=ss[:, 0:1]),
             r=[h_buf], w=[self.sq.b, ss.b])
        rstd_from_ss(c, nc, ss, rs, 1, K, 1.0 / D)
        c.op(c.dve, lambda: nc.vector.scalar_tensor_tensor(u[:, :], h_ap, rs[:, 0:1], g_t[:, :], ALU.mult, ALU.mult),
             r=[h_buf, rs.b, g_t.b], w=[u.b])
        for k in range(8):
            c.op(c.pe, lambda k=k: nc.tensor.transpose(tp[:, k * 128:(k + 1) * 128], u[:, k * 128:(k + 1) * 128], K.ident[:, :]),
                 r=[u.b, K.ident.b], w=[tp.b])
        tt = t % 4
        c.op(c.act, lambda: nc.scalar.copy(stg[:, :, tt * 128:(tt + 1) * 128], tp[:, :].rearrange("p (k t) -> p k t", k=8)),
             r=[tp.b], w=[stg.b])
        if tt == 3:
            ch = t // 4
            dst = uT_dram[ch].ap().rearrange("(k p) t -> p k t", p=128)
            c.dma(c.sp, dst, stg[:, :, :], r=[stg.b], w=[uT_buf[ch]])
            if uT_all is not None and LIM["gather"]:
                c.allgather(uT_dram[ch].ap(), [uT_buf[ch]], uT_all[ch].ap(), uT_all_b[ch])


def phase_norm0(c, nc, K, G):
    with ExitStack() as st:
        g = sb(st, nc, "n0g", [128, 1024], F32)
        c.dma(c.sp, g[:, :], G["smalls"][:, 0:1024], r=[], w=[g.b])
        xin = [sb(st, nc, f"n0x{i}", [128, 1024], F32) for i in range(2)]
        nt = NormT(c, nc, st, K, "n0")
        for t in range(16):
            x = xin[t % 2]
            c.dma(c.sp, x[:, :], G["x_sh"][t * 128:(t + 1) * 128, :], r=[], w=[x.b])
            nt.emit(x[:, :], x.b, g, t, G["uT_loc"][0], G["uT_loc_b"][0], G["uT_all"][0], G["uT_all_b"][0])
        c.barrier()


def qk_norm_tile(c, nc, K, W, zp, ncol, gq, slot):
    sq, ss, rs, qf, qg = W["sq"][slot], W["ss"][slot], W["rs"][slot], W["qf"][slot], W["qg"][slot]
    c.op(c.act, lambda: nc.scalar.activation(sq[:, :], zp[:, 0:256], AF.Square), r=[zp.b], w=[sq.b])
    c.op(c.dve, lambda: nc.vector.tensor_reduce(ss[:, 0:4], sq[:, :].rearrange("p (g d) -> p g d", g=4), AX.X, ALU.add),
         r=[sq.b], w=[ss.b])
    rstd_from_ss(c, nc, ss, rs, 4, K, 1.0 / 64)
    c.op(c.dve, lambda: nc.vector.tensor_tensor(qf[:, :].rearrange("p (g d) -> p g d", g=4),
                                                 zp[:, 0:256].rearrange("p (g d) -> p g d", g=4),
                                                 rs[:, 0:4].unsqueeze(2).to_broadcast([128, 4, 64]), ALU.mult),
         r=[zp.b, rs.b], w=[qf.b])
    c.op(c.dve, lambda: nc.vector.tensor_tensor(qg[:, :], qf[:, :], gq[:, :], ALU.mult), r=[qf.b, gq.b], w=[qg.b])
    return qg


def qk_norm_chunk(c, nc, K, zqk, sq, ss, rs, qf):
    c.op(c.act, lambda: nc.scalar.activation(sq[:, :], zqk[:, :, :].rearrange("p t n -> p (t n)"), AF.Square), r=[zqk.b], w=[sq.b])
    c.op(c.dve, lambda: nc.vector.tensor_reduce(ss[:, 0:16], sq[:, :].rearrange("p (a d) -> p a d", a=16), AX.X, ALU.add),
         r=[sq.b], w=[ss.b])
    rstd_from_ss(c, nc, ss, rs, 16, K, 1.0 / 64)
    c.op(c.dve, lambda: nc.vector.tensor_tensor(qf[:, :].rearrange("p (a d) -> p a d", a=16),
                                                 zqk[:, :, :].rearrange("p t (g d) -> p (t g) d", g=4),
                                                 rs[:, 0:16].unsqueeze(2).to_broadcast([128, 16, 64]), ALU.mult),
         r=[zqk.b, rs.b], w=[qf.b])
    return qf


def silu_gate(c, nc, K, gp, e, gdst_ap, gdst_b):
    c.op(c.act, lambda: nc.scalar.activation(e[:, :], gp[:, :], AF.Exp, scale=-1.0), r=[gp.b], w=[e.b])
    c.op(c.act, lambda: nc.scalar.activation(e[:, :], e[:, :], AF.Ln, bias=K.one[:, 0:1], scale=1.0), r=[e.b, K.one.b], w=[e.b])
    c.op(c.act, lambda: nc.scalar.activation(e[:, :], e[:, :], AF.Exp, scale=-1.0), r=[e.b], w=[e.b])
    c.op(c.dve, lambda: nc.vector.tensor_tensor(gdst_ap, gp[:, :], e[:, :], ALU.mult), r=[gp.b, e.b], w=[gdst_b])


def load_uT_chunk(c, nc, G, l, UT, ch):
    ut = UT[ch % 2]
    src = G["uT_all"][l][ch % 4].ap().rearrange("(r k p) t -> p r k t", r=4, k=8, p=128)[:, ch // 4, :, :]
    c.dma(c.sp, ut[:, :, :], src, r=[G["uT_all_b"][l][ch % 4]], w=[ut.b])
    return ut


def finalize_out(c, nc, W, acc, h, ncols, Gt, gcol0, ot_dram_rows, G, l, tok0, slot, bslot):
    oh = slice(0, 64) if h == 0 else slice(64, 128)
    lh = slice(64, 128) if h == 0 else slice(0, 64)
    rl, osb, ot = W["rl"][slot], W["os"][slot], W["ot"][slot]
    if ncols <= 512:
        c.op(c.dve, lambda: nc.vector.reciprocal(rl[oh, 0:ncols], acc[lh, 0:ncols]), r=[acc.b], w=[rl.b])
    else:
        c.op(c.dve, lambda: nc.vector.tensor_copy(rl[oh, 0:ncols], acc[lh, 0:ncols]), r=[acc.b], w=[rl.b])
        c.op(c.act, lambda: nc.scalar.activation(rl[oh, 0:ncols], rl[oh, 0:ncols], AF.Ln), r=[rl.b], w=[rl.b])
        c.op(c.act, lambda: nc.scalar.activation(rl[oh, 0:ncols], rl[oh, 0:ncols], AF.Exp, scale=-1.0), r=[rl.b], w=[rl.b])
    c.op(c.dve, lambda: nc.vector.tensor_tensor(osb[oh, 0:ncols], acc[oh, 0:ncols], rl[oh, 0:ncols], ALU.mult),
         r=[acc.b, rl.b], w=[osb.b])
    gf = W["gf"][slot]
    if ncols <= 512:
        c.op(c.dve, lambda: nc.vector.tensor_copy(gf[oh, 0:ncols], Gt[oh, gcol0:gcol0 + ncols]), r=[Gt.b], w=[gf.b])
    else:
        c.op(c.act, lambda: nc.scalar.copy(gf[oh, 0:ncols], Gt[oh, gcol0:gcol0 + ncols]), r=[Gt.b], w=[gf.b])
    c.op(c.dve, lambda: nc.vector.tensor_tensor(ot[oh, 0:ncols], osb[oh, 0:ncols], gf[oh, 0:ncols], ALU.mult),
         r=[osb.b, gf.b], w=[ot.b])
    qd = tok0 // TSH
    r0 = qd * 256 + ot_dram_rows + h * 64
    tq = tok0 % TSH
    c.dma(c.sp, G["oT_loc"][l].ap()[r0:r0 + 64, tq:tq + ncols], ot[oh, 0:ncols], r=[ot.b], w=[G["oT_loc_b"][l][qd][bslot]], waw=False)


def gather_o(c, G, l, q, typ):
    r0 = q * 256 + typ * 128
    d0 = (q * 2 + typ) * 512
    srcb = G["oT_loc_b"][l][q][0:2] if typ == 0 else G["oT_loc_b"][l][q][2:3]
    c.allgather(G["oT_loc"][l].ap()[r0:r0 + 128, :], srcb, G["oT_all"][l].ap()[d0:d0 + 512, :], G["oT_all_b"][l][q][typ])


def phase_fox(c, nc, K, G, l):
    with ExitStack() as st:
        Wa = sb(st, nc, "Wa", [128, 8, 514], BF16)
        c.dma(c.sp, Wa[:, :, :], G["wa16"][l].ap().rearrange("(k p) n -> p k n", p=128), r=[G["wa16_b"][l]], w=[Wa.b])
        gq = sb(st, nc, "gqa", [128, 256], F32)
        c.dma(c.sp, gq[:, :], G["smalls"][:, l * 2560 + 2048:l * 2560 + 2304], r=[], w=[gq.b])
        c.op(c.dve, lambda: nc.vector.tensor_scalar(gq[:, 0:128], gq[:, 0:128], 0.125, None, ALU.mult), r=[gq.b], w=[gq.b])
        bfb = sb(st, nc, "bfb", [128, 2], F32)
        c.dma(c.sp, bfb[:, :], G["bf"][l], r=[], w=[bfb.b])
        QK = sb(st, nc, "QK", [128, 4, S], BF16, nb=16)
        V = sb(st, nc, "V", [128, 64, 256], BF16, nb=16)
        Gt = sb(st, nc, "Gt", [128, S], BF16)
        c.op(c.dve, lambda: nc.vector.memset(V[:, :, 64:192], 1.0), w=V.bs)
        c.op(c.dve, lambda: nc.vector.memset(QK[64:70, :, :], 1.0), w=QK.bs)
        with ExitStack() as s1:
            UT = [sb(s1, nc, f"UT{i}", [128, 8, 512], BF16) for i in range(2)]
            sq = sb(s1, nc, "asq", [128, 1024], F32)
            qf = sb(s1, nc, "aqf", [128, 1024], F32)
            ss = sb(s1, nc, "ass", [128, 16], F32)
            rs = sb(s1, nc, "ars", [128, 16], F32)
            F = {k: [sb(s1, nc, f"f{k}", [2, 512], F32)] * 2 for k in ("fb", "y", "sc", "cy", "r1", "r2")}
            SPLT = [sb(s1, nc, "SPLT", [2, 6, 512], BF16)] * 2
            CAR = [sb(s1, nc, f"CAR{i}", [2, 1], F32) for i in range(2)]
            onesf = sb(s1, nc, "onesf", [2, 512], F32)
            bfb2 = sb(s1, nc, "bfb2", [2, 1], F32)
            c.dma(c.sp, bfb2[:, :], G["bf"][l][0, :].rearrange("(p o) -> p o", o=1), r=[], w=[bfb2.b])
            c.op(c.dve, lambda: nc.vector.memset(onesf[:, :], 1.0), w=[onesf.b])
            QN = [sb(s1, nc, f"QN{i}", [128, 4, 4, 64], BF16) for i in range(2)]
            E = [sb(s1, nc, "Eg", [128, 512], F32)] * 2
            gp = [ps(s1, nc, "gp", [128, 512], F32)] * 2
            fp_ = [ps(s1, nc, "fp", [32, 512], F32)] * 2
            zqk = ps(s1, nc, "zqk", [128, 4, 256], F32)
            zv = ps(s1, nc, "zv", [128, 4, 128], F32)
            tp = ps(s1, nc, "tpa", [128, 4, 4, 128], BF16)
            LV = LIM["stage1"]
            augb = Buf("augb")
            nchs = LIM["nch"] if LV >= 1 else 0
            uts = {}
            if nchs:
                uts[0] = load_uT_chunk(c, nc, G, l, UT, 0)
            for ch in range(nchs):
                if ch + 1 < nchs:
                    uts[ch + 1] = load_uT_chunk(c, nc, G, l, UT, ch + 1)
                ut = uts.pop(ch)
                g_p = gp[ch % 2]
                for k in range(8):
                    c.op(c.pe, lambda k=k: nc.tensor.matmul(g_p[:, :], Wa[:, k, 386:514], ut[:, k, :], start=(k == 0), stop=(k == 7)),
                         r=[Wa.b, ut.b], w=[g_p.b])
                silu_gate(c, nc, K, g_p, E[ch % 2], Gt[:, ch * 512:(ch + 1) * 512], Gt.b)
                for tt in range(4):
                    for k in range(8):
                        c.op(c.pe, lambda k=k, tt=tt: nc.tensor.matmul(zqk[:, tt, :], ut[:, k, tt * 128:(tt + 1) * 128], Wa[:, k, 0:256],
                                                                       start=(k == 0), stop=(k == 7)),
                             r=[Wa.b, ut.b], w=[zqk.b])
                    for k in range(8):
                        c.op(c.pe, lambda k=k, tt=tt: nc.tensor.matmul(zv[:, tt, :], ut[:, k, tt * 128:(tt + 1) * 128], Wa[:, k, 256:384],
                                                                       start=(k == 0), stop=(k == 7)),
                             r=[Wa.b, ut.b], w=[zv.b])
                if LV >= 3:
                    sl = ch % 2
                    f_p = fp_[sl]
                    for k in range(8):
                        c.op(c.pe, lambda k=k: nc.tensor.matmul(f_p[:, :], Wa[:, k, 384:416], ut[:, k, :], start=(k == 0), stop=(k == 7)),
                             r=[Wa.b, ut.b], w=[f_p.b])
                c.op(c.act, lambda: nc.scalar.copy(V[:, 4 * ch:4 * ch + 4, 0:64], zv[:, :, 0:64]), r=[zv.b], w=[V.bs[ch]])
                c.op(c.act, lambda: nc.scalar.copy(V[:, 4 * ch:4 * ch + 4, 192:256], zv[:, :, 64:128]), r=[zv.b], w=[V.bs[ch]])
                qn = QN[ch % 2]
                qk_norm_chunk(c, nc, K, zqk, sq, ss, rs, qf)
                c.op(c.dve, lambda: nc.vector.tensor_tensor(qn[:, :, :, :].rearrange("p t g d -> p t (g d)"),
                                                             qf[:, :].rearrange("p (t n) -> p t n", t=4),
                                                             gq[:, :].unsqueeze(1).to_broadcast([128, 4, 256]), ALU.mult),
                     r=[qf.b, gq.b], w=[qn.b])
                for tt in range(4):
                    for g4 in range(4):
                        c.op(c.pe, lambda g4=g4, tt=tt: nc.tensor.transpose(tp[0:64, tt, g4, :], qn[:, tt, g4, :], K.ident[:, :]),
                             r=[qn.b, K.ident.b], w=[tp.b])
                c.op(c.act, lambda: nc.scalar.copy(QK[0:64, :, ch * 512:(ch + 1) * 512].rearrange("p g (t c) -> p g t c", t=4),
                                                   tp[0:64, :, :, :].rearrange("p t g c -> p g t c")), r=[tp.b], w=[QK.bs[ch]])
                if LV >= 3:
                    sl = ch % 2
                    f_p = fp_[sl]
                    fb, y, sc, cy, r1, r2, spl = [F[k][sl] for k in ("fb", "y", "sc", "cy", "r1", "r2")] + [SPLT[sl]]
                    c.op(c.dve, lambda: nc.vector.tensor_scalar(fb[:, :], f_p[0:2, :], bfb2[:, 0:1], None, ALU.add), r=[f_p.b, bfb2.b], w=[fb.b])
                    if LIM["sub"] < 2:
                        continue
                    c.op(c.act, lambda: nc.scalar.activation(y[:, :], fb[:, :], AF.Exp, scale=-1.0), r=[fb.b], w=[y.b])
                    c.op(c.act, lambda: nc.scalar.activation(y[:, :], y[:, :], AF.Ln, bias=K.one[0:2, 0:1], scale=1.0), r=[y.b, K.one.b], w=[y.b])
                    if LIM["sub"] < 3:
                        continue
                    c.op(c.dve, lambda: nc.vector.tensor_tensor_scan(sc[:, :], onesf[:, :], y[:, :], 0.0, ALU.mult, ALU.add), r=[onesf.b, y.b], w=[sc.b])
                    if ch == 0:
                        c.op(c.dve, lambda: nc.vector.tensor_copy(cy[:, :], sc[:, :]), r=[sc.b], w=[cy.b])
                    else:
                        car = CAR[(ch - 1) % 2]
                        c.op(c.dve, lambda: nc.vector.tensor_scalar(cy[:, :], sc[:, :], car[:, 0:1], None, ALU.add), r=[sc.b, car.b], w=[cy.b])
                    c.op(c.dve, lambda: nc.vector.tensor_copy(CAR[ch % 2][:, :], cy[:, 511:512]), r=[cy.b], w=[CAR[ch % 2].b])
                    if LIM["sub"] < 4:
                        continue
                    c.op(c.dve, lambda: nc.vector.tensor_copy(spl[:, 3, :], cy[:, :]), r=[cy.b], w=[spl.b])
                    c.op(c.dve, lambda: nc.vector.tensor_copy(sc[:, :], spl[:, 3, :]), r=[spl.b], w=[sc.b])
                    c.op(c.dve, lambda: nc.vector.tensor_tensor(r1[:, :], cy[:, :], sc[:, :], ALU.subtract), r=[cy.b, sc.b], w=[r1.b])
                    c.op(c.dve, lambda: nc.vector.tensor_copy(spl[:, 4, :], r1[:, :]), r=[r1.b], w=[spl.b])
                    c.op(c.dve, lambda: nc.vector.tensor_copy(sc[:, :], spl[:, 4, :]), r=[spl.b], w=[sc.b])
                    c.op(c.dve, lambda: nc.vector.tensor_tensor(r2[:, :], r1[:, :], sc[:, :], ALU.subtract), r=[r1.b, sc.b], w=[r2.b])
                    c.op(c.dve, lambda: nc.vector.tensor_copy(spl[:, 5, :], r2[:, :]), r=[r2.b], w=[spl.b])
                    c.op(c.dve, lambda: nc.vector.tensor_scalar(spl[:, 0:3, :], spl[:, 3:6, :], -1.0, None, ALU.mult), r=[spl.b], w=[spl.b])
                    cs_ = slice(ch * 512, (ch + 1) * 512)
                    c.wait_buf(c.sp, QK.bs[ch])
                    for hh in range(2 if LIM["sub"] >= 5 else 0):
                        c.dma(c.sp, QK[64:67, hh, cs_], spl[hh:hh + 1, 0:3, :], r=[spl.b], w=[augb], waw=False)
                        c.dma(c.sp, QK[67:70, 2 + hh, cs_], spl[hh:hh + 1, 3:6, :], r=[spl.b], w=[augb], waw=False)
            c.barrier()
        if not LIM["attn"]:
            return
        with ExitStack() as s2:
            NS, NP = 3, 4
            Sb = [ps(s2, nc, f"Sb{i}", [128, 2, 512], F32) for i in range(NS)]
            Ob = [ps(s2, nc, f"Ob{i}", [128, 512], F32) for i in range(2)]
            P = [sb(s2, nc, f"P{i}", [128, 2, 512], BF16) for i in range(NP)]
            W2 = {k: [sb(s2, nc, f"f{k}{i}", [128, 512], dt) for i in range(2)] for k, dt in
                  [("rl", F32), ("os", F32), ("gf", F32), ("ot", BF16)]}
            units = []
            for h in range(2):
                for I in range(LIM["nch"]):
                    for j in range(0, 4 * I, 2):
                        units.append((h, I, j, 2))
                    for j in range(4 * I, 4 * I + 4):
                        units.append((h, I, j, 1))
            LA = 3

            def emit_S(n):
                h, I, j, cnt_ = units[n]
                Sx, Px = Sb[n % NS], P[n % NP]
                q0 = I * 512
                if cnt_ == 2:
                    for u in range(2):
                        jj = j + u
                        c.op(c.pe, lambda jj=jj, u=u: nc.tensor.matmul(Sx[:, u, :], QK[0:70, 2 + h, jj * 128:(jj + 1) * 128],
                                                                       QK[0:70, h, q0:q0 + 512], start=True, stop=True),
                             r=[QK.bs[jj // 4], QK.bs[I]], w=[Sx.b])
                    c.op(c.act, lambda: nc.scalar.activation(Px[:, :, :], Sx[:, :, :], AF.Exp), r=[Sx.b], w=[Px.b])
                    return
                r_ = j - 4 * I
                lo = r_ * 128
                kT = QK[0:70, 2 + h, j * 128:(j + 1) * 128]
                rdeps = [QK.bs[j // 4], QK.bs[I]]
                c.op(c.pe, lambda: nc.tensor.matmul(Sx[:, 0, lo:lo + 128], kT, QK[0:70, h, q0 + lo:q0 + lo + 128], start=True, stop=False),
                     r=rdeps, w=[Sx.b])
                c.op(c.pe, lambda: nc.tensor.matmul(Sx[:, 0, lo:lo + 128], K.ident[:, :], K.mge[:, :], start=False, stop=True),
                     r=[K.ident.b, K.mge.b], w=[Sx.b])
                if lo + 128 < 512:
                    c.op(c.pe, lambda: nc.tensor.matmul(Sx[:, 0, lo + 128:512], kT, QK[0:70, h, q0 + lo + 128:q0 + 512], start=True, stop=True),
                         r=rdeps, w=[Sx.b])
                c.op(c.act, lambda: nc.scalar.activation(Px[:, 0, lo:512], Sx[:, 0, lo:512], AF.Exp), r=[Sx.b], w=[Px.b])

            def emit_PV(n):
                h, I, j, cnt_ = units[n]
                nj = 4 * I + 4
                it = h * LIM["nch"] + I
                O, Px = Ob[it % 2], P[n % NP]
                if cnt_ == 2:
                    for u in range(2):
                        jj = j + u
                        c.op(c.pe, lambda jj=jj, u=u: nc.tensor.matmul(O[:, 0:512], V[:, jj, h * 128:(h + 1) * 128], Px[:, u, :],
                                                                       start=(jj == 0), stop=False, skip_group_check=True),
                             r=[V.bs[jj // 4], Px.b], w=[O.b])
                    return
                lo = (j - 4 * I) * 128
                c.op(c.pe, lambda: nc.tensor.matmul(O[:, lo:512], V[:, j, h * 128:(h + 1) * 128], Px[:, 0, lo:512],
                                                    start=(j == 0), stop=(j == nj - 1), skip_group_check=True),
                     r=[V.bs[j // 4], Px.b], w=[O.b])
                if j == nj - 1:
                    finalize_out(c, nc, W2, O, h, 512, Gt, I * 512, 0, G, l, I * 512, it % 2, it % 2)

            for n in range(len(units) + LA):
                if n < len(units):
                    emit_S(n)
                if n - LA >= 0:
                    emit_PV(n - LA)
            c.barrier()


def phase_dil(c, nc, K, G, l, ROPE):
    with ExitStack() as st:
        Wb = sb(st, nc, "Wb", [128, 8, 512], BF16)
        c.dma(c.sp, Wb[:, :, :], G["wb16"][l].ap().rearrange("(k p) n -> p k n", p=128), r=[G["wb16_b"][l]], w=[Wb.b])
        gq = sb(st, nc, "gqb", [128, 256], F32)
        c.dma(c.sp, gq[:, :], G["smalls"][:, l * 2560 + 2304:l * 2560 + 2560], r=[], w=[gq.b])
        c.op(c.dve, lambda: nc.vector.tensor_scalar(gq[:, 0:128], gq[:, 0:128], 0.125, None, ALU.mult), r=[gq.b], w=[gq.b])
        QK = sb(st, nc, "QKd", [128, 4, S], BF16, nb=16)
        VT = sb(st, nc, "VTd", [128, S], BF16, nb=16)
        Gt = sb(st, nc, "Gtd", [128, S], BF16)
        with ExitStack() as s1:
            UT = [sb(s1, nc, f"UTd{i}", [128, 8, 512], BF16) for i in range(2)]
            sq = sb(s1, nc, "bsq", [128, 1024], F32)
            qf = sb(s1, nc, "bqf", [128, 1024], F32)
            qg = sb(s1, nc, "bqg", [128, 4, 4, 64], F32)
            ra = sb(s1, nc, "bra", [128, 4, 4, 16], F32)
            rb = sb(s1, nc, "brb", [128, 4, 4, 16], F32)
            ss = sb(s1, nc, "bss", [128, 16], F32)
            rs = sb(s1, nc, "brs", [128, 16], F32)
            QN = [sb(s1, nc, f"QNd{i}", [128, 4, 4, 64], BF16) for i in range(2)]
            E = sb(s1, nc, "Egd", [128, 512], F32)
            gp = ps(s1, nc, "gpd", [128, 512], F32)
            vp = ps(s1, nc, "vpd", [128, 512], F32)
            zqk = ps(s1, nc, "zqkd", [128, 4, 256], F32)
            tp = ps(s1, nc, "tpd", [128, 4, 4, 128], BF16)
            uts = {0: load_uT_chunk(c, nc, G, l, UT, 0)}
            for ch in range(LIM["nch"]):
                if ch + 1 < LIM["nch"]:
                    uts[ch + 1] = load_uT_chunk(c, nc, G, l, UT, ch + 1)
                ut = uts.pop(ch)
                for k in range(8):
                    c.op(c.pe, lambda k=k: nc.tensor.matmul(vp[:, :], Wb[:, k, 256:384], ut[:, k, :], start=(k == 0), stop=(k == 7)),
                         r=[Wb.b, ut.b], w=[vp.b])
                c.op(c.act, lambda: nc.scalar.copy(VT[:, ch * 512:(ch + 1) * 512], vp[:, :]), r=[vp.b], w=[VT.bs[ch]])
                for k in range(8):
                    c.op(c.pe, lambda k=k: nc.tensor.matmul(gp[:, :], Wb[:, k, 384:512], ut[:, k, :], start=(k == 0), stop=(k == 7)),
                         r=[Wb.b, ut.b], w=[gp.b])
                silu_gate(c, nc, K, gp, E, Gt[:, ch * 512:(ch + 1) * 512], Gt.b)
                for tt in range(4):
                    for k in range(8):
                        c.op(c.pe, lambda k=k, tt=tt: nc.tensor.matmul(zqk[:, tt, :], ut[:, k, tt * 128:(tt + 1) * 128], Wb[:, k, 0:256],
                                                                       start=(k == 0), stop=(k == 7)),
                             r=[Wb.b, ut.b], w=[zqk.b])
                qn = QN[ch % 2]
                qk_norm_chunk(c, nc, K, zqk, sq, ss, rs, qf)
                c.op(c.dve, lambda: nc.vector.tensor_tensor(qg[:, :, :, :].rearrange("p t g d -> p t (g d)"),
                                                             qf[:, :].rearrange("p (t n) -> p t n", t=4),
                                                             gq[:, :].unsqueeze(1).to_broadcast([128, 4, 256]), ALU.mult),
                     r=[qf.b, gq.b], w=[qg.b])
                c.op(c.act, lambda: nc.scalar.copy(qn[:, :, :, 16:64], qg[:, :, :, 16:64]), r=[qg.b], w=[qn.b])
                t0 = 4 * ch
                cs = ROPE["cs2"][:, t0:t0 + 4, :].unsqueeze(2).to_broadcast([128, 4, 4, 16])
                sn_lo = ROPE["sn2"][:, t0:t0 + 4, 0:8].unsqueeze(2).to_broadcast([128, 4, 4, 8])
                sn_hi = ROPE["sn2"][:, t0:t0 + 4, 8:16].unsqueeze(2).to_broadcast([128, 4, 4, 8])
                c.op(c.dve, lambda: nc.vector.tensor_tensor(ra[:, :, :, :], qg[:, :, :, 0:16], cs, ALU.mult), r=[qg.b, ROPE["b"]], w=[ra.b])
                c.op(c.dve, lambda: nc.vector.tensor_tensor(rb[:, :, :, 0:8], qg[:, :, :, 8:16], sn_lo, ALU.mult), r=[qg.b, ROPE["b"]], w=[rb.b])
                c.op(c.dve, lambda: nc.vector.tensor_tensor(rb[:, :, :, 8:16], qg[:, :, :, 0:8], sn_hi, ALU.mult), r=[qg.b, ROPE["b"]], w=[rb.b])
                c.op(c.dve, lambda: nc.vector.tensor_tensor(qn[:, :, :, 0:16], ra[:, :, :, :], rb[:, :, :, :], ALU.add), r=[ra.b, rb.b], w=[qn.b])
                for tt in range(4):
                    for g4 in range(4):
                        c.op(c.pe, lambda g4=g4, tt=tt: nc.tensor.transpose(tp[0:64, tt, g4, :], qn[:, tt, g4, :], K.ident[:, :]),
                             r=[qn.b, K.ident.b], w=[tp.b])
                c.op(c.act, lambda: nc.scalar.copy(QK[0:64, :, ch * 512:(ch + 1) * 512].rearrange("p g (t c) -> p g t c", t=4),
                                                   tp[0:64, :, :, :].rearrange("p t g c -> p g t c")), r=[tp.b], w=[QK.bs[ch]])
            c.barrier()
        with ExitStack() as s2:
            NS, NP, NV, NVT = 3, 5, 32, 3
            Sb = [ps(s2, nc, f"Sd{i}", [128, 256], F32) for i in range(NS)]
            Op = [ps(s2, nc, f"Od{i}", [128, 128], F32) for i in range(2)]
            vtp = [ps(s2, nc, f"vtp{i}", [128, 64], BF16) for i in range(NVT)]
            P = [sb(s2, nc, f"Pd{i}", [128, 256], BF16) for i in range(NP)]
            Vd = [sb(s2, nc, f"Vd{i}", [128, 128], BF16) for i in range(NV)]
            ACC = [sb(s2, nc, f"ACC{i}", [128, 2048], F32) for i in range(2)]
            W2 = {k: [sb(s2, nc, f"g{k}{i}", [128, 2048], dt) for i in range(1)] for k, dt in
                  [("rl", F32), ("os", F32), ("gf", F32), ("ot", BF16)]}
            cnt = {"nv": [0, 0], "nvt": 0}
            LA = 3
            NVH = NV // 2
            for k_, v in enumerate(Vd):
                c.op(c.dve, lambda v=v: nc.vector.memset(v[:, :], 1.0), w=[v.b])

            def make_vd(h, start, d):
                hr = slice(h * 64, (h + 1) * 64)
                vcol = slice(0, 64) if h == 0 else slice(64, 128)
                vd = Vd[h * NVH + cnt["nv"][h] % NVH]
                cnt["nv"][h] += 1
                vt = vtp[cnt["nvt"] % NVT]
                cnt["nvt"] += 1
                chs = sorted(set([start // 512, (start + 127 * d) // 512]))
                c.op(c.pe, lambda: nc.tensor.transpose(vt[:, :], VT[hr, start:start + 127 * d + 1:d], K.ident[hr, hr]),
                     r=[VT.bs[x] for x in range(chs[0], chs[-1] + 1)] + [K.ident.b], w=[vt.b])
                c.op(c.dve, lambda: nc.vector.tensor_copy(vd[:, vcol], vt[:, :]), r=[vt.b], w=[vd.b])
                return vd

            blocks = []
            for SBk in range(LIM["nsb"]):
                for h in range(2):
                    for d in (1, 4, 16):
                        nbq = 16 // d
                        for r_ in range(d):
                            for ii in range(nbq):
                                blocks.append((SBk, d, r_, ii, nbq, h))
            state = {"vd_prev": None}
            info = {}

            def emit_A(n):
                SBk, d, r_, ii, nbq, h = blocks[n]
                if ii == 0:
                    state["vd_prev"] = None
                i = nbq * SBk + ii
                start = 128 * i * d + r_
                pstart = start - 128 * d
                Sx, Px = Sb[n % NS], P[n % NP]
                c0, c1 = start // 512, (start + 127 * d) // 512
                rd = [QK.bs[x] for x in range(c0, c1 + 1)]
                qT = QK[0:64, h, start:start + 127 * d + 1:d]
                c.op(c.pe, lambda: nc.tensor.matmul(Sx[:, 128:256], QK[0:64, 2 + h, start:start + 127 * d + 1:d], qT, start=True, stop=True),
                     r=rd, w=[Sx.b])
                lo = 128
                vd_prev = state["vd_prev"]
                if i > 0:
                    lo = 0
                    p0, p1 = pstart // 512, (pstart + 127 * d) // 512
                    rdp = rd + [QK.bs[x] for x in range(p0, p1 + 1)]
                    c.op(c.pe, lambda: nc.tensor.matmul(Sx[:, 0:128], QK[0:64, 2 + h, pstart:pstart + 127 * d + 1:d], qT, start=True, stop=True),
                         r=rdp, w=[Sx.b])
                    if vd_prev is None:
                        vd_prev = make_vd(h, pstart, d)
                vd_own = make_vd(h, start, d)
                c.op(c.act, lambda: nc.scalar.activation(Px[:, lo:256], Sx[:, lo:256], AF.Exp), r=[Sx.b], w=[Px.b])
                c.op(c.dve, lambda: nc.vector.tensor_tensor(Px[:, lo:256], Px[:, lo:256], K.m01[:, lo:256], ALU.mult), r=[Px.b, K.m01.b], w=[Px.b])
                info[n] = (i, start, vd_prev, vd_own)
                state["vd_prev"] = vd_own

            def emit_B(n):
                SBk, d, r_, ii, nbq, h = blocks[n]
                i, start, vd_prev, vd_own = info.pop(n)
                Px = P[n % NP]
                O = Op[n % 2]
                acc = ACC[h]
                if i > 0:
                    c.op(c.pe, lambda: nc.tensor.matmul(O[:, :], vd_prev[:, :], Px[:, 0:128], start=True, stop=False),
                         r=[vd_prev.b, Px.b], w=[O.b])
                c.op(c.pe, lambda: nc.tensor.matmul(O[:, :], vd_own[:, :], Px[:, 128:256], start=(i == 0), stop=True),
                     r=[vd_own.b, Px.b], w=[O.b])
                off = start - 2048 * SBk
                av = acc[:, off:off + 127 * d + 1:d]
                if d == 1:
                    c.op(c.dve, lambda: nc.vector.tensor_copy(av, O[:, :]), r=[O.b], w=[acc.b])
                else:
                    c.op(c.dve, lambda: nc.vector.tensor_tensor(av, O[:, :], av, ALU.add), r=[O.b, acc.b], w=[acc.b])
                last = (n + 1 == len(blocks)) or (blocks[n + 1][0] != SBk) or (blocks[n + 1][5] != h)
                if last:
                    finalize_out(c, nc, W2, acc, h, 2048, Gt, SBk * 2048, 128, G, l, SBk * 2048, 0, 2)
                    if h == 1 and LIM["gather"]:
                        gather_o(c, G, l, SBk, 1)

            for n in range(len(blocks) + LA):
                if n < len(blocks):
                    emit_A(n)
                if n - LA >= 0:
                    emit_B(n - LA)
            c.barrier()


def build_rope(c, nc, st, K, G):
    cs2 = sb(st, nc, "cs2", [128, 64, 16], F32)
    sn2 = sb(st, nc, "sn2", [128, 64, 16], F32)
    rb = Buf("rope")
    with ExitStack() as s1:
        pi_ = sb(s1, nc, "posi", [128, 64], I32)
        pf = sb(s1, nc, "posf", [128, 64], F32)
        ang = sb(s1, nc, "ang", [128, 64, 8], F32)
        a2 = sb(s1, nc, "ang2", [128, 64, 8], F32)
        kf = sb(s1, nc, "kf", [128, 64, 8], F32)
        ki = sb(s1, nc, "ki", [128, 64, 8], I32)
        c.dma(c.sp, pi_[:, :], G["pos"][:, :], r=[], w=[pi_.b])
        c.op(c.dve, lambda: nc.vector.tensor_copy(pf[:, :], pi_[:, :]), r=[pi_.b], w=[pf.b])
        c.op(c.dve, lambda: nc.vector.tensor_tensor(ang[:, :, :], pf[:, :].unsqueeze(2).to_broadcast([128, 64, 8]),
                                                     K.invf[:, :].unsqueeze(1).to_broadcast([128, 64, 8]), ALU.mult),
             r=[pf.b, K.invf.b], w=[ang.b])

        def sin_of(dst_ap, shift, scale):
            c.op(c.dve, lambda: nc.vector.tensor_scalar(a2[:, :, :], ang[:, :, :], shift + float(np.pi), None, ALU.add), r=[ang.b], w=[a2.b])
            c.op(c.dve, lambda: nc.vector.tensor_scalar(kf[:, :, :], a2[:, :, :], 1.0 / TWO_PI, None, ALU.mult), r=[a2.b], w=[kf.b])
            c.op(c.dve, lambda: nc.vector.tensor_copy(ki[:, :, :], kf[:, :, :]), r=[kf.b], w=[ki.b])
            c.op(c.dve, lambda: nc.vector.tensor_copy(kf[:, :, :], ki[:, :, :]), r=[ki.b], w=[kf.b])
            c.op(c.dve, lambda: nc.vector.scalar_tensor_tensor(a2[:, :, :], kf[:, :, :], -TWO_PI, a2[:, :, :], ALU.mult, ALU.add), r=[kf.b, a2.b], w=[a2.b])
            c.op(c.dve, lambda: nc.vector.tensor_scalar(kf[:, :, :], a2[:, :, :], 0.0, TWO_PI, ALU.is_lt, ALU.mult), r=[a2.b], w=[kf.b])
            c.op(c.dve, lambda: nc.vector.tensor_tensor(a2[:, :, :], a2[:, :, :], kf[:, :, :], ALU.add), r=[a2.b, kf.b], w=[a2.b])
            c.op(c.dve, lambda: nc.vector.tensor_scalar(kf[:, :, :], a2[:, :, :], TWO_PI, -TWO_PI, ALU.is_ge, ALU.mult), r=[a2.b], w=[kf.b])
            c.op(c.dve, lambda: nc.vector.tensor_tensor(a2[:, :, :], a2[:, :, :], kf[:, :, :], ALU.add), r=[a2.b, kf.b], w=[a2.b])
            c.op(c.act, lambda: nc.scalar.activation(a2[:, :, :], a2[:, :, :], AF.Sin, bias=K.negpi[:, 0:1], scale=1.0), r=[a2.b, K.negpi.b], w=[a2.b])
            c.op(c.dve, lambda: nc.vector.tensor_scalar(dst_ap, a2[:, :, :], scale, None, ALU.mult), r=[a2.b], w=[rb])

        sin_of(cs2[:, :, 0:8], float(np.pi / 2), 1.0)
        sin_of(cs2[:, :, 8:16], float(np.pi / 2), 1.0)
        sin_of(sn2[:, :, 0:8], 0.0, -1.0)
        sin_of(sn2[:, :, 8:16], 0.0, 1.0)
        c.barrier()
    return {"cs2": cs2, "sn2": sn2, "b": rb}


def phase_b(c, nc, K, G, l, last):
    with ExitStack() as st:
        Wo = sb(st, nc, "Wo", [128, 8, 1024], BF16)
        Wg = sb(st, nc, "Wg", [128, 8, 1024], BF16)
        Wp = sb(st, nc, "Wp", [128, 2, 1024], BF16)
        PT = sb(st, nc, "PT", [128, 2, TSH], BF16)
        c.dma(c.sp, Wo[:, :, :], G["wo16"][l].ap().rearrange("(k p) n -> p k n", p=128), r=[G["wo16_b"][l]], w=[Wo.b])
        c.dma(c.sp, Wg[:, :, :], G["wg16"][l].ap().rearrange("(k p) n -> p k n", p=128), r=[G["wg16_b"][l]], w=[Wg.b])
        c.dma(c.sp, Wp[:, :, :], G["wp16"][l].ap().rearrange("(k p) n -> p k n", p=128), r=[G["wp16_b"][l]], w=[Wp.b])
        c.dma(c.sp, PT[:, :, :], G["pT16"][l].ap().rearrange("(k p) t -> p k t", p=128), r=[G["pT16_b"][l]], w=[PT.b])
        gple = sb(st, nc, "gple", [128, 1024], F32)
        c.dma(c.sp, gple[:, :], G["smalls"][:, l * 2560 + 1024:l * 2560 + 2048], r=[], w=[gple.b])
        gnext = None
        if not last:
            gnext = sb(st, nc, "gnext", [128, 1024], F32)
            c.dma(c.sp, gnext[:, :], G["smalls"][:, (l + 1) * 2560:(l + 1) * 2560 + 1024], r=[], w=[gnext.b])
            nt = NormT(c, nc, st, K, "nb")
        OT = [sb(st, nc, f"OTi{i}", [128, 8, 512], BF16) for i in range(2)]
        H = [sb(st, nc, f"H{i}", [128, 1024], F32) for i in range(2)]
        H2 = [sb(st, nc, f"H2{i}", [128, 1024], F32) for i in range(2)]
        sq = sb(st, nc, "bsq", [128, 1024], BF16)
        ss = [sb(st, nc, f"bss{i}", [128, 1], F32) for i in range(2)]
        rs = [sb(st, nc, f"brs{i}", [128, 1], F32) for i in range(2)]
        Vn = [sb(st, nc, f"Vn{i}", [128, 1024], BF16) for i in range(2)]
        VTs = [sb(st, nc, f"VTs{i}", [128, 1024], BF16) for i in range(2)]
        Eg = [sb(st, nc, f"Egb{i}", [128, 1024], F32) for i in range(2)]
        TM = [sb(st, nc, f"TM{i}", [128, 1024], F32) for i in range(2)]
        X = [[ps(st, nc, f"X{i}{hf}", [128, 512], F32) for hf in range(2)] for i in range(2)]
        Gp = [ps(st, nc, f"Gp{i}", [128, 512], F32) for i in range(2)]
        vtp = ps(st, nc, "vtpb", [128, 1024], BF16)
        hsrc = G["x_sh"] if l == 0 else G["h_loc"].ap()
        hsrc_b = [Buf("xin")] if l == 0 else G["h_loc_b"]

        def load_ot(ch):
            o_ = OT[ch % 2]
            src = G["oT_mine"][l].ap().rearrange("(q p) t -> p q t", p=128)[:, :, ch * 512:(ch + 1) * 512]
            c.dma(c.sp, o_[:, :, :], src, r=[G["oT_mine_b"][l]], w=[o_.b])
            return o_

        def load_h(t_):
            h_ = H[t_ % 2]
            c.dma(c.sp, h_[:, :], hsrc[t_ * 128:(t_ + 1) * 128, :], r=hsrc_b, w=[h_.b])
            return h_

        ots = {0: load_ot(0)}

        def stage_a(t):
            i = t % 2
            ot = ots[t // 4]
            if t % 4 == 1 and t // 4 + 1 < 4:
                ots[t // 4 + 1] = load_ot(t // 4 + 1)
            h = load_h(t)
            tt = t % 4
            for half in range(2):
                a = X[i][half]
                for kc in range(8):
                    c.op(c.pe, lambda kc=kc: nc.tensor.matmul(a[:, :], ot[:, kc, tt * 128:(tt + 1) * 128],
                                                              Wo[:, kc, half * 512:(half + 1) * 512],
                                                              start=(kc == 0), stop=(kc == 7)),
                         r=[ot.b, Wo.b], w=[a.b])
                c.op(c.dve, lambda a=a, half=half: nc.vector.tensor_tensor(h[:, half * 512:(half + 1) * 512], a[:, :], h[:, half * 512:(half + 1) * 512], ALU.add),
                     r=[a.b, h.b], w=[h.b])
            c.op(c.act, lambda: nc.scalar.activation(sq[:, :], h[:, :], AF.Square, accum_out=ss[i][:, 0:1]), r=[h.b], w=[sq.b, ss[i].b])
            rstd_from_ss(c, nc, ss[i], rs[i], 1, K, 1.0 / D)
            vn, vts = Vn[i], VTs[i]
            c.op(c.dve, lambda: nc.vector.scalar_tensor_tensor(vn[:, :], h[:, :], rs[i][:, 0:1], gple[:, :], ALU.mult, ALU.mult),
                 r=[h.b, rs[i].b, gple.b], w=[vn.b])
            for k in range(8):
                c.op(c.pe, lambda k=k: nc.tensor.transpose(vtp[:, k * 128:(k + 1) * 128], vn[:, k * 128:(k + 1) * 128], K.ident[:, :]),
                     r=[vn.b, K.ident.b], w=[vtp.b])
            c.op(c.act, lambda: nc.scalar.copy(vts[:, :], vtp[:, :]), r=[vtp.b], w=[vts.b])

        def stage_b(t):
            i = t % 2
            h, vts, eg, tm = H[i], VTs[i], Eg[i], TM[i]
            for half in range(2):
                g_ = Gp[half]
                for kc in range(8):
                    c.op(c.pe, lambda kc=kc: nc.tensor.matmul(g_[:, :], vts[:, kc * 128:(kc + 1) * 128], Wg[:, kc, half * 512:(half + 1) * 512],
                                                              start=(kc == 0), stop=(kc == 7)),
                         r=[vts.b, Wg.b], w=[g_.b])
                c.op(c.act, lambda g_=g_, half=half: nc.scalar.activation(eg[:, half * 512:(half + 1) * 512], g_[:, :], AF.Exp, scale=-1.0),
                     r=[g_.b], w=[eg.b])
            c.op(c.act, lambda: nc.scalar.activation(eg[:, :], eg[:, :], AF.Ln, bias=K.one[:, 0:1], scale=1.0), r=[eg.b, K.one.b], w=[eg.b])
            c.op(c.act, lambda: nc.scalar.activation(eg[:, :], eg[:, :], AF.Exp, scale=-1.0), r=[eg.b], w=[eg.b])
            for half in range(2):
                p_ = X[i][half]
                for kc in range(2):
                    c.op(c.pe, lambda kc=kc: nc.tensor.matmul(p_[:, :], PT[:, kc, t * 128:(t + 1) * 128], Wp[:, kc, half * 512:(half + 1) * 512],
                                                              start=(kc == 0), stop=(kc == 1)),
                         r=[PT.b, Wp.b], w=[p_.b])
                c.op(c.dve, lambda p_=p_, half=half: nc.vector.tensor_tensor(tm[:, half * 512:(half + 1) * 512], p_[:, :], eg[:, half * 512:(half + 1) * 512], ALU.mult),
                     r=[p_.b, eg.b], w=[tm.b])
            h2 = H2[i]
            c.op(c.dve, lambda: nc.vector.tensor_tensor(h2[:, :], h[:, :], tm[:, :], ALU.add), r=[h.b, tm.b], w=[h2.b])
            if last:
                c.dma(c.sp, G["out"][t * 128:(t + 1) * 128, :], h2[:, :], r=[h2.b], w=[G["out_b"][i]], waw=False)
            else:
                c.dma(c.sp, G["h_loc"].ap()[t * 128:(t + 1) * 128, :], h2[:, :], r=[h2.b], w=[G["h_loc_b"][i]], waw=False)
                nt.emit(h2[:, :], h2.b, gnext, t, G["uT_loc"][l + 1], G["uT_loc_b"][l + 1], G["uT_all"][l + 1], G["uT_all_b"][l + 1])

        stage_a(0)
        for t in range(16):
            if t + 1 < 16:
                stage_a(t + 1)
            stage_b(t)
        c.barrier()


def build_program(n_layers=2, debug=False, stop=None):
    _UID[0] = 0
    nc = bass.Bass("TRN2", target_bir_lowering=False)
    G = {}
    ei = lambda n, s_, d: nc.dram_tensor(n, s_, d, kind="ExternalInput").ap()
    G["x_sh"] = ei("x_sh", [TSH, D], F32)
    G["pT"] = ei("pT", [2, 256, TSH], F32)
    G["pos"] = ei("pos", [128, 64], I32)
    G["w_a"] = ei("w_a", [2, D, 514], F32)
    G["w_b"] = ei("w_b", [2, D, 512], F32)
    G["w_out"] = ei("w_out", [2, D, D], F32)
    G["w_gate"] = ei("w_gate", [2, D, D], F32)
    G["w_ple"] = ei("w_ple", [2, 256, D], F32)
    G["smalls"] = ei("smalls", [128, 5120], F32)
    G["consts"] = ei("consts", [128, 392], F32)
    G["bf"] = ei("bf", [2, 128, 2], F32)
    G["out"] = nc.dram_tensor("out", [TSH, D], F32, kind="ExternalOutput").ap()
    G["out_b"] = [Buf("out0", True), Buf("out1", True)]
    if debug:
        G["dbg_o"] = nc.dram_tensor("dbg_o", [256, S], BF16, kind="ExternalOutput").ap()
        G["dbg_u"] = nc.dram_tensor("dbg_u", [8 * 128, TSH], BF16, kind="ExternalOutput").ap()
    for nm, shp, src in (("wa16", [D, 514], "w_a"), ("wb16", [D, 512], "w_b"), ("wo16", [D, D], "w_out"),
                         ("wg16", [D, D], "w_gate"), ("wp16", [256, D], "w_ple"), ("pT16", [256, TSH], "pT")):
        G[nm] = [nc.dram_tensor(f"{nm}_{l}", shp, BF16) for l in range(2)]
        G[nm + "_b"] = [Buf(f"{nm}{l}", True) for l in range(2)]
        G[nm + "_src"] = src
    uT_loc = [nc.dram_tensor(f"uT_loc_{ch}", [8 * 128, 512], BF16) for ch in range(4)]
    uT_all = [nc.dram_tensor(f"uT_all_{ch}", [4 * 8 * 128, 512], BF16) for ch in range(4)]
    oT_loc = nc.dram_tensor("oT_loc", [4 * 256, TSH], BF16)
    oT_all = nc.dram_tensor("oT_all", [4 * 4 * 256, TSH], BF16)
    oT_mine = nc.dram_tensor("oT_mine", [4 * 256, TSH], BF16)
    G["uT_loc"] = [uT_loc, uT_loc]
    G["uT_all"] = [uT_all, uT_all]
    G["oT_loc"] = [oT_loc, oT_loc]
    G["oT_all"] = [oT_all, oT_all]
    G["oT_mine"] = [oT_mine, oT_mine]
    G["h_loc"] = nc.dram_tensor("h_loc", [TSH, D], F32)
    G["h_loc_b"] = [Buf("h_loc0", True), Buf("h_loc1", True)]
    b1 = [Buf(f"uTl{i}", True) for i in range(4)]
    G["uT_loc_b"] = [b1, b1]
    b2 = [Buf(f"uTa{i}", True) for i in range(4)]
    G["uT_all_b"] = [b2, b2]
    b3 = [[Buf(f"oTl{q}_{i}", True) for i in range(3)] for q in range(4)]
    G["oT_loc_b"] = [b3, b3]
    b4 = [[Buf(f"oTa{q}_{y}", True) for y in range(2)] for q in range(4)]
    G["oT_all_b"] = [b4, b4]
    G["oT_mine_b"] = [Buf("oTmine0", True), Buf("oTmine1", True)]
    with ExitStack() as st:
        c = Ctx(nc, st)
        def precast(names, l):
            for nm in names:
                c.dma(c.pool, G[nm][l].ap()[:, :], G[G[nm + "_src"]][l], r=[], w=[G[nm + "_b"][l]])

        precast(["wa16"], 0)
        K = load_consts(c, nc, st, G)
        ROPE = build_rope(c, nc, st, K, G)
        if stop not in ("foxsim", "dilsim", "bsim"):
            phase_norm0(c, nc, K, G)
            c.recycle()
        for l in range(n_layers):
            if stop == "norm0":
                break
            if stop in ("foxsim", "dilsim", "bsim"):
                precast(["wb16", "wo16", "wg16", "wp16", "pT16"], 0)
            if stop == "foxsim":
                phase_fox(c, nc, K, G, l)
                break
            if stop == "dilsim":
                phase_dil(c, nc, K, G, l, ROPE)
                break
            if stop == "bsim":
                phase_b(c, nc, K, G, l, last=False)
                break
            if stop == "ag":
                break
            if l == 0:
                precast(["wb16", "wo16", "wg16", "wp16", "pT16"], 0)
                if n_layers > 1:
                    precast(["wa16", "wb16", "wo16", "wg16", "wp16", "pT16"], 1)
            phase_fox(c, nc, K, G, l)
            c.recycle()
            for q in range(4):
                gather_o(c, G, l, q, 0)
            if stop == "fox":
                break
            phase_dil(c, nc, K, G, l, ROPE)
            c.recycle()
            if stop == "dil":
                break
            rank = nc.gpsimd.partition_id() % 4
            src = G["oT_all"][l].ap()[bass.ds(rank * 1024, 1024), :]
            c.dma(c.pool, G["oT_mine"][l].ap()[:, :], src, r=[b_ for qq in G["oT_all_b"][l] for b_ in qq], w=[G["oT_mine_b"][l]])
            phase_b(c, nc, K, G, l, last=(l == 1))
            c.recycle()
        if n_layers == 1 and stop is None:
            c.dma(c.pool, G["out"][:, :], G["h_loc"].ap()[:, :], r=G["h_loc_b"], w=[G["out_b"][0]])
        if debug:
            db = Buf("dbg", True)
            if stop == "dil":
                for q in range(4):
                    c.dma(c.pool, G["dbg_o"][:, q * TSH:(q + 1) * TSH], G["oT_loc"][0].ap()[q * 256:(q + 1) * 256, :], r=G["oT_loc_b"][0][q], w=[db])
            if stop is not None:
                for ch in range(4):
                    c.dma(c.pool, G["dbg_u"][:, ch * 512:(ch + 1) * 512], G["uT_loc"][0][ch].ap()[:, :], r=[G["uT_loc_b"][0][ch]], w=[db])
            c.wait_buf(c.sp, db)
        G["nsem"] = c.nsem
        for ob in G["out_b"]:
            c.wait_buf(c.sp, ob)
            c.wait_buf(c.pool, ob)
    return nc


def make_consts():
    cst = np.zeros((128, 392), np.float32)
    s_ = np.arange(128)[:, None]
    t_ = np.arange(128)[None, :]
    cst[:, 0:128] = (s_ == t_)
    cst[:, 128:256] = np.where(t_ >= s_, 0.0, NEG)
    cst[:, 256:384] = np.where(t_ <= s_, 0.0, NEG)
    cst[:, 384:392] = (500000.0 ** (-np.arange(8, dtype=np.float32) / 8.0))[None, :]
    return cst


def make_in_maps(x, p, positions, norm_g, w_in, b_f, qk_norm_g, w_out, w_ple, ple_norm_g, w_ple_gate):
    f = lambda a: np.ascontiguousarray(np.asarray(a, dtype=np.float32))
    x, p, norm_g, w_in, b_f, qk_norm_g, w_out, w_ple, ple_norm_g, w_ple_gate = map(
        f, (x, p, norm_g, w_in, b_f, qk_norm_g, w_out, w_ple, ple_norm_g, w_ple_gate))
    positions = np.asarray(positions).astype(np.int32)
    cst = make_consts()
    smalls = np.zeros((128, 5120), np.float32)
    for l in range(2):
        o = l * 2560
        smalls[:, o:o + 1024] = norm_g[l][None]
        smalls[:, o + 1024:o + 2048] = ple_norm_g[l][None]
        g = qk_norm_g[l]
        smalls[:, o + 2048:o + 2304] = np.concatenate([g[0], g[0], g[1], g[1]])[None]
        smalls[:, o + 2304:o + 2560] = np.concatenate([g[2], g[2], g[3], g[3]])[None]
    maps = []
    for core in range(NCORES):
        b, r = core // 4, core % 4
        hs = slice(128 * r, 128 * r + 128)
        cols_a = np.concatenate([np.arange(0, 512)[hs], np.arange(512, 1024)[hs], np.arange(1024, 1536)[hs],
                                 np.array([2048 + 2 * r, 2048 + 2 * r + 1]), np.arange(1536, 2048)[hs]])
        base = 2056
        cols_b = np.concatenate([base + np.arange(0, 512)[hs], base + np.arange(512, 1024)[hs],
                                 base + np.arange(1024, 1536)[hs], base + np.arange(1536, 2048)[hs]])
        m = {
            "x_sh": np.ascontiguousarray(x[b, r * TSH:(r + 1) * TSH]),
            "pT": np.ascontiguousarray(p[:, b, r * TSH:(r + 1) * TSH, :].transpose(0, 2, 1)),
            "pos": np.ascontiguousarray(positions[b].reshape(64, 128).T),
            "w_a": np.ascontiguousarray(w_in[:, :, cols_a]),
            "w_b": np.ascontiguousarray(w_in[:, :, cols_b]),
            "w_out": w_out, "w_gate": w_ple_gate, "w_ple": w_ple,
            "smalls": smalls, "consts": cst,
            "bf": np.ascontiguousarray(np.broadcast_to(b_f[:, None, 2 * r:2 * r + 2], (2, 128, 2))),
        }
        maps.append(m)
    return maps


_NC_CACHE = {}


def kernel(x, p, positions, norm_g, w_in, b_f, qk_norm_g, w_out, w_ple, ple_norm_g, w_ple_gate):
    maps = make_in_maps(x, p, positions, norm_g, w_in, b_f, qk_norm_g, w_out, w_ple, ple_norm_g, w_ple_gate)
    nc = build_program()
    res = run_bass_kernel_spmd(nc, maps, core_ids=list(range(NCORES)))
    out = np.empty((2, S, D), np.float32)
    for core in range(NCORES):
        b, r = core // 4, core % 4
        out[b, r * TSH:(r + 1) * TSH] = res.results[core]["out"]
    return out
```

```python
import numpy as np
from contextlib import ExitStack
import concourse.bass as bass
import concourse.mybir as mybir
from concourse.bass_utils import run_bass_kernel_spmd

F32 = mybir.dt.float32
BF16 = mybir.dt.bfloat16
I32 = mybir.dt.int32
AF = mybir.ActivationFunctionType
ALU = mybir.AluOpType
AX = mybir.AxisListType

NCORES = 8
S = 8192
D = 1024
TSH = 2048
NEG = -30000.0
EPS = 1e-6
TWO_PI = float(2 * np.pi)
LIM = {"nch": 16, "nsb": 4, "attn": True, "stage1": 99, "sub": 99, "gather": True}


class Buf:
    __slots__ = ("name", "writer", "readers", "dsem", "dcount", "persist")

    def __init__(self, name="", persist=False):
        self.name = name
        self.persist = persist
        self.writer = None
        self.readers = {}
        self.dsem = None
        self.dcount = 0


class Eng:
    def __init__(self, ctx, name, raw, is_pe=False):
        self.name = name
        self.raw = raw
        self.sem = ctx.new_sem("e_" + name)
        self.count = 0
        self.seen = {}
        self.is_pe = is_pe


class Ctx:
    def __init__(self, nc, stack):
        self.nc = nc
        self.stack = stack
        self.sems = {}
        self.nsem = 0
        self.pe = Eng(self, "pe", nc.tensor, is_pe=True)
        self.act = Eng(self, "act", nc.scalar)
        self.dve = Eng(self, "dve", nc.vector)
        self.pool = Eng(self, "pool", nc.gpsimd)
        self.sp = Eng(self, "sp", nc.sync)
        self.dbufs = []
        self.free_sems = []
        self.scope_bufs = []

    def new_sem(self, name):
        self.nsem += 1
        s = self.stack.enter_context(self.nc.semaphore(f"{name}_{self.nsem}"))
        self.sems[id(s)] = s
        return s

    def _wait(self, eng, deps):
        for sem, val in deps:
            k = id(sem)
            if eng.seen.get(k, 0) < val:
                eng.raw.wait_ge(sem, val)
                eng.seen[k] = val

    def _deps(self, r, w, waw=True):
        deps = []
        for b in r:
            if b.writer is not None:
                deps.append(b.writer)
        for b in w:
            if b.writer is not None and waw:
                deps.append(b.writer)
            for k, v in b.readers.items():
                deps.append((self.sems[k], v))
        return deps

    def op(self, eng, fn, r=(), w=()):
        own = id(eng.sem)
        if eng.is_pe:
            deps = [d for d in self._deps(r, w) if id(d[0]) != own]
        else:
            deps = self._deps(r, w)
        self._wait(eng, deps)
        ins = fn()
        ins.then_inc(eng.sem, 1)
        eng.count += 1
        for b in r:
            b.readers[own] = eng.count
        for b in w:
            b.writer = (eng.sem, eng.count)
            b.readers = {}
        return ins

    def dma(self, q, out_ap, in_ap, r=(), w=(), waw=True, **kw):
        deps = self._deps(r, w, waw=waw)
        self._wait(q, deps)
        dst = w[0]
        if dst.dsem is None:
            if self.free_sems and q is not self.pool:
                dst.dsem, dst.dcount = self.free_sems.pop()
            else:
                dst.dsem = self.new_sem("d_" + dst.name)
            self.dbufs.append(dst)
            if not dst.persist:
                self.scope_bufs.append(dst)
        ins = q.raw.dma_start(out=out_ap, in_=in_ap, **kw)
        ins.then_inc(dst.dsem, 16)
        dst.dcount += 16
        k = id(dst.dsem)
        for b in r:
            b.readers[k] = dst.dcount
        old_readers = dst.readers if not waw else {}
        dst.writer = (dst.dsem, dst.dcount)
        dst.readers = {}
        return ins

    def allgather(self, src_ap, src_b, dst_ap, dst_b):
        q = self.pool
        self._wait(q, self._deps(src_b, [dst_b]))
        if dst_b.dsem is None:
            dst_b.dsem = self.new_sem("cc_" + dst_b.name)
        ins = self.nc.gpsimd.collective_compute(
            "AllGather", ALU.bypass, replica_groups=[[0, 1, 2, 3], [4, 5, 6, 7]],
            ins=[src_ap.opt()], outs=[dst_ap.opt()])
        ins.then_inc(dst_b.dsem, 1)
        dst_b.dcount += 1
        for sb_ in src_b:
            sb_.readers[id(dst_b.dsem)] = dst_b.dcount
        dst_b.writer = (dst_b.dsem, dst_b.dcount)
        dst_b.readers = {}

    def barrier(self):
        engs = [self.pe, self.act, self.dve, self.pool, self.sp]
        deps = [(e.sem, e.count) for e in engs if e.count > 0]
        deps += [(b.dsem, b.dcount) for b in self.dbufs if b.dcount > 0]
        for e in engs:
            self._wait(e, [d for d in deps if id(d[0]) != id(e.sem) or not e.is_pe])

    def recycle(self):
        for b in self.scope_bufs:
            self.free_sems.append((b.dsem, b.dcount))
            self.dbufs.remove(b)
            b.dsem = None
        self.scope_bufs = []

    def wait_buf(self, eng, b):
        if b.writer is not None:
            self._wait(eng, [b.writer])


class T:
    def __init__(self, t, name, nb=1):
        self.t = t
        self.b = Buf(name)
        self.bs = [Buf(f"{name}{i}") for i in range(nb)]

    def __getitem__(self, k):
        return self.t[k]


_UID = [0]


def _un(name):
    _UID[0] += 1
    return f"{name}_{_UID[0]}"


def sb(st, nc, name, shape, dt, nb=1):
    name = _un(name)
    return T(st.enter_context(nc.sbuf_tensor(name, shape, dt)), name, nb)


def ps(st, nc, name, shape, dt, nb=1):
    name = _un(name)
    return T(st.enter_context(nc.psum_tensor(name, shape, dt)), name, nb)


class Consts:
    pass


def load_consts(c, nc, st, G):
    K = Consts()
    cf = sb(st, nc, "cf", [128, 392], F32)
    c.dma(c.sp, cf[:, :], G["consts"][:, :], r=[], w=[cf.b])
    K.ident = sb(st, nc, "ident", [128, 128], BF16)
    K.mge = sb(st, nc, "mge", [128, 128], BF16)
    K.mle = sb(st, nc, "mle", [128, 128], BF16)
    K.identf = sb(st, nc, "identf", [128, 128], F32)
    K.tri = sb(st, nc, "tri", [128, 128], F32)
    K.l127 = sb(st, nc, "l127", [128, 128], F32)
    K.invf = sb(st, nc, "invf", [128, 8], F32)
    K.m01 = sb(st, nc, "m01", [128, 256], BF16)
    K.eps = sb(st, nc, "epsc", [128, 1], F32)
    K.one = sb(st, nc, "onec", [128, 1], F32)
    K.negpi = sb(st, nc, "negpic", [128, 1], F32)
    c.op(c.dve, lambda: nc.vector.tensor_copy(K.ident[:, :], cf[:, 0:128]), r=[cf.b], w=[K.ident.b])
    c.op(c.dve, lambda: nc.vector.tensor_copy(K.mge[:, :], cf[:, 128:256]), r=[cf.b], w=[K.mge.b])
    c.op(c.dve, lambda: nc.vector.tensor_copy(K.mle[:, :], cf[:, 256:384]), r=[cf.b], w=[K.mle.b])
    c.op(c.dve, lambda: nc.vector.tensor_copy(K.invf[:, :], cf[:, 384:392]), r=[cf.b], w=[K.invf.b])
    c.op(c.dve, lambda: nc.vector.tensor_copy(K.identf[:, :], cf[:, 0:128]), r=[cf.b], w=[K.identf.b])
    c.op(c.dve, lambda: nc.vector.tensor_scalar(K.tri[:, :], cf[:, 128:256], 0.0, None, ALU.is_equal), r=[cf.b], w=[K.tri.b])
    c.op(c.dve, lambda: nc.vector.tensor_scalar(K.m01[:, 0:128], cf[:, 256:384], 0.0, None, ALU.is_equal), r=[cf.b], w=[K.m01.b])
    c.op(c.dve, lambda: nc.vector.tensor_scalar(K.m01[:, 128:256], cf[:, 128:256], 0.0, None, ALU.is_equal), r=[cf.b], w=[K.m01.b])
    c.op(c.dve, lambda: nc.vector.memset(K.l127[:, :], 0.0), w=[K.l127.b])
    c.op(c.dve, lambda: nc.vector.memset(K.l127[96:128, :], 1.0), w=[K.l127.b])
    c.op(c.dve, lambda: nc.vector.tensor_scalar(K.l127[:, :], K.l127[:, :], cf[:, 127:128], None, ALU.mult), r=[K.l127.b, cf.b], w=[K.l127.b])
    c.op(c.dve, lambda: nc.vector.memset(K.eps[:, :], EPS), w=[K.eps.b])
    c.op(c.dve, lambda: nc.vector.memset(K.one[:, :], 1.0), w=[K.one.b])
    c.op(c.dve, lambda: nc.vector.memset(K.negpi[:, :], -float(np.pi)), w=[K.negpi.b])
    return K


def rstd_from_ss(c, nc, ss, rs, n, K, inv_n):
    c.op(c.act, lambda: nc.scalar.activation(rs[:, 0:n], ss[:, 0:n], AF.Ln, bias=K.eps[:, 0:1], scale=inv_n),
         r=[ss.b, K.eps.b], w=[rs.b])
    c.op(c.act, lambda: nc.scalar.activation(rs[:, 0:n], rs[:, 0:n], AF.Exp, scale=-0.5), r=[rs.b], w=[rs.b])


class NormT:
    def __init__(self, c, nc, st, K, name):
        self.c, self.nc, self.K = c, nc, K
        self.sq = sb(st, nc, name + "sq", [128, 1024], BF16)
        self.ss = [sb(st, nc, name + f"ss{i}", [128, 1], F32) for i in range(2)]
        self.rs = [sb(st, nc, name + f"rs{i}", [128, 1], F32) for i in range(2)]
        self.u = [sb(st, nc, name + f"u{i}", [128, 1024], BF16) for i in range(2)]
        self.tp = [ps(st, nc, name + f"tp{i}", [128, 1024], BF16) for i in range(1)]
        self.stg = [sb(st, nc, name + f"stg{i}", [128, 8, 512], BF16) for i in range(2)]
        self.n = 0

    def emit(self, h_ap, h_buf, g_t, t, uT_dram, uT_buf, uT_all=None, uT_all_b=None):
        c, nc, K = self.c, self.nc, self.K
        i = self.n % 2
        self.n += 1
        ss, rs, u, tp = self.ss[i], self.rs[i], self.u[i], self.tp[0]
        stg = self.stg[(t // 4) % 2]
        c.op(c.act, lambda: nc.scalar.activation(self.sq[:, :], h_ap, AF.Square, accum_out=ss[:, 0:1]),
             r=[h_buf], w=[self.sq.b, ss.b])
        rstd_from_ss(c, nc, ss, rs, 1, K, 1.0 / D)
        c.op(c.dve, lambda: nc.vector.scalar_tensor_tensor(u[:, :], h_ap, rs[:, 0:1], g_t[:, :], ALU.mult, ALU.mult),
             r=[h_buf, rs.b, g_t.b], w=[u.b])
        for k in range(8):
            c.op(c.pe, lambda k=k: nc.tensor.transpose(tp[:, k * 128:(k + 1) * 128], u[:, k * 128:(k + 1) * 128], K.ident[:, :]),
                 r=[u.b, K.ident.b], w=[tp.b])
        tt = t % 4
        c.op(c.act, lambda: nc.scalar.copy(stg[:, :, tt * 128:(tt + 1) * 128], tp[:, :].rearrange("p (k t) -> p k t", k=8)),
             r=[tp.b], w=[stg.b])
        if tt == 3:
            ch = t // 4
            dst = uT_dram[ch].ap().rearrange("(k p) t -> p k t", p=128)
            c.dma(c.sp, dst, stg[:, :, :], r=[stg.b], w=[uT_buf[ch]])
            if uT_all is not None and LIM["gather"]:
                c.allgather(uT_dram[ch].ap(), [uT_buf[ch]], uT_all[ch].ap(), uT_all_b[ch])


def phase_norm0(c, nc, K, G):
    with ExitStack() as st:
        g = sb(st, nc, "n0g", [128, 1024], F32)
        c.dma(c.sp, g[:, :], G["smalls"][:, 0:1024], r=[], w=[g.b])
        xin = [sb(st, nc, f"n0x{i}", [128, 1024], F32) for i in range(2)]
        nt = NormT(c, nc, st, K, "n0")
        for t in range(16):
            x = xin[t % 2]
            c.dma(c.sp, x[:, :], G["x_sh"][t * 128:(t + 1) * 128, :], r=[], w=[x.b])
            nt.emit(x[:, :], x.b, g, t, G["uT_loc"][0], G["uT_loc_b"][0], G["uT_all"][0], G["uT_all_b"][0])
        c.barrier()


def qk_norm_tile(c, nc, K, W, zp, ncol, gq, slot):
    sq, ss, rs, qf, qg = W["sq"][slot], W["ss"][slot], W["rs"][slot], W["qf"][slot], W["qg"][slot]
    c.op(c.act, lambda: nc.scalar.activation(sq[:, :], zp[:, 0:256], AF.Square), r=[zp.b], w=[sq.b])
    c.op(c.dve, lambda: nc.vector.tensor_reduce(ss[:, 0:4], sq[:, :].rearrange("p (g d) -> p g d", g=4), AX.X, ALU.add),
         r=[sq.b], w=[ss.b])
    rstd_from_ss(c, nc, ss, rs, 4, K, 1.0 / 64)
    c.op(c.dve, lambda: nc.vector.tensor_tensor(qf[:, :].rearrange("p (g d) -> p g d", g=4),
                                                 zp[:, 0:256].rearrange("p (g d) -> p g d", g=4),
                                                 rs[:, 0:4].unsqueeze(2).to_broadcast([128, 4, 64]), ALU.mult),
         r=[zp.b, rs.b], w=[qf.b])
    c.op(c.dve, lambda: nc.vector.tensor_tensor(qg[:, :], qf[:, :], gq[:, :], ALU.mult), r=[qf.b, gq.b], w=[qg.b])
    return qg


def qk_norm_chunk(c, nc, K, zqk, sq, ss, rs, qf):
    c.op(c.act, lambda: nc.scalar.activation(sq[:, :], zqk[:, :, :].rearrange("p t n -> p (t n)"), AF.Square), r=[zqk.b], w=[sq.b])
    c.op(c.dve, lambda: nc.vector.tensor_reduce(ss[:, 0:16], sq[:, :].rearrange("p (a d) -> p a d", a=16), AX.X, ALU.add),
         r=[sq.b], w=[ss.b])
    rstd_from_ss(c, nc, ss, rs, 16, K, 1.0 / 64)
    c.op(c.dve, lambda: nc.vector.tensor_tensor(qf[:, :].rearrange("p (a d) -> p a d", a=16),
                                                 zqk[:, :, :].rearrange("p t (g d) -> p (t g) d", g=4),
                                                 rs[:, 0:16].unsqueeze(2).to_broadcast([128, 16, 64]), ALU.mult),
         r=[zqk.b, rs.b], w=[qf.b])
    return qf


def silu_gate(c, nc, K, gp, e, gdst_ap, gdst_b):
    c.op(c.act, lambda: nc.scalar.activation(e[:, :], gp[:, :], AF.Exp, scale=-1.0), r=[gp.b], w=[e.b])
    c.op(c.act, lambda: nc.scalar.activation(e[:, :], e[:, :], AF.Ln, bias=K.one[:, 0:1], scale=1.0), r=[e.b, K.one.b], w=[e.b])
    c.op(c.act, lambda: nc.scalar.activation(e[:, :], e[:, :], AF.Exp, scale=-1.0), r=[e.b], w=[e.b])
    c.op(c.dve, lambda: nc.vector.tensor_tensor(gdst_ap, gp[:, :], e[:, :], ALU.mult), r=[gp.b, e.b], w=[gdst_b])


def load_uT_chunk(c, nc, G, l, UT, ch):
    ut = UT[ch % 2]
    src = G["uT_all"][l][ch % 4].ap().rearrange("(r k p) t -> p r k t", r=4, k=8, p=128)[:, ch // 4, :, :]
    c.dma(c.sp, ut[:, :, :], src, r=[G["uT_all_b"][l][ch % 4]], w=[ut.b])
    return ut


def finalize_out(c, nc, W, acc, h, ncols, Gt, gcol0, ot_dram_rows, G, l, tok0, slot, bslot):
    oh = slice(0, 64) if h == 0 else slice(64, 128)
    lh = slice(64, 128) if h == 0 else slice(0, 64)
    rl, osb, ot = W["rl"][slot], W["os"][slot], W["ot"][slot]
    if ncols <= 512:
        c.op(c.dve, lambda: nc.vector.reciprocal(rl[oh, 0:ncols], acc[lh, 0:ncols]), r=[acc.b], w=[rl.b])
    else:
        c.op(c.dve, lambda: nc.vector.tensor_copy(rl[oh, 0:ncols], acc[lh, 0:ncols]), r=[acc.b], w=[rl.b])
        c.op(c.act, lambda: nc.scalar.activation(rl[oh, 0:ncols], rl[oh, 0:ncols], AF.Ln), r=[rl.b], w=[rl.b])
        c.op(c.act, lambda: nc.scalar.activation(rl[oh, 0:ncols], rl[oh, 0:ncols], AF.Exp, scale=-1.0), r=[rl.b], w=[rl.b])
    c.op(c.dve, lambda: nc.vector.tensor_tensor(osb[oh, 0:ncols], acc[oh, 0:ncols], rl[oh, 0:ncols], ALU.mult),
         r=[acc.b, rl.b], w=[osb.b])
    gf = W["gf"][slot]
    if ncols <= 512:
        c.op(c.dve, lambda: nc.vector.tensor_copy(gf[oh, 0:ncols], Gt[oh, gcol0:gcol0 + ncols]), r=[Gt.b], w=[gf.b])
    else:
        c.op(c.act, lambda: nc.scalar.copy(gf[oh, 0:ncols], Gt[oh, gcol0:gcol0 + ncols]), r=[Gt.b], w=[gf.b])
    c.op(c.dve, lambda: nc.vector.tensor_tensor(ot[oh, 0:ncols], osb[oh, 0:ncols], gf[oh, 0:ncols], ALU.mult),
         r=[osb.b, gf.b], w=[ot.b])
    qd = tok0 // TSH
    r0 = qd * 256 + ot_dram_rows + h * 64
    tq = tok0 % TSH
    c.dma(c.sp, G["oT_loc"][l].ap()[r0:r0 + 64, tq:tq + ncols], ot[oh, 0:ncols], r=[ot.b], w=[G["oT_loc_b"][l][qd][bslot]], waw=False)


def gather_o(c, G, l, q, typ):
    r0 = q * 256 + typ * 128
    d0 = (q * 2 + typ) * 512
    srcb = G["oT_loc_b"][l][q][0:2] if typ == 0 else G["oT_loc_b"][l][q][2:3]
    c.allgather(G["oT_loc"][l].ap()[r0:r0 + 128, :], srcb, G["oT_all"][l].ap()[d0:d0 + 512, :], G["oT_all_b"][l][q][typ])


def phase_fox(c, nc, K, G, l):
    with ExitStack() as st:
        Wa = sb(st, nc, "Wa", [128, 8, 514], BF16)
        c.dma(c.sp, Wa[:, :, :], G["wa16"][l].ap().rearrange("(k p) n -> p k n", p=128), r=[G["wa16_b"][l]], w=[Wa.b])
        gq = sb(st, nc, "gqa", [128, 256], F32)
        c.dma(c.sp, gq[:, :], G["smalls"][:, l * 2560 + 2048:l * 2560 + 2304], r=[], w=[gq.b])
        c.op(c.dve, lambda: nc.vector.tensor_scalar(gq[:, 0:128], gq[:, 0:128], 0.125, None, ALU.mult), r=[gq.b], w=[gq.b])
        bfb = sb(st, nc, "bfb", [128, 2], F32)
        c.dma(c.sp, bfb[:, :], G["bf"][l], r=[], w=[bfb.b])
        QK = sb(st, nc, "QK", [128, 4, S], BF16, nb=16)
        V = sb(st, nc, "V", [128, 64, 256], BF16, nb=16)
        Gt = sb(st, nc, "Gt", [128, S], BF16)
        c.op(c.dve, lambda: nc.vector.memset(V[:, :, 64:192], 1.0), w=V.bs)
        c.op(c.dve, lambda: nc.vector.memset(QK[64:70, :, :], 1.0), w=QK.bs)
        with ExitStack() as s1:
            UT = [sb(s1, nc, f"UT{i}", [128, 8, 512], BF16) for i in range(2)]
            sq = sb(s1, nc, "asq", [128, 1024], F32)
            qf = sb(s1, nc, "aqf", [128, 1024], F32)
            ss = sb(s1, nc, "ass", [128, 16], F32)
            rs = sb(s1, nc, "ars", [128, 16], F32)
            F = {k: [sb(s1, nc, f"f{k}", [2, 512], F32)] * 2 for k in ("fb", "y", "sc", "cy", "r1", "r2")}
            SPLT = [sb(s1, nc, "SPLT", [2, 6, 512], BF16)] * 2
            CAR = [sb(s1, nc, f"CAR{i}", [2, 1], F32) for i in range(2)]
            onesf = sb(s1, nc, "onesf", [2, 512], F32)
            bfb2 = sb(s1, nc, "bfb2", [2, 1], F32)
            c.dma(c.sp, bfb2[:, :], G["bf"][l][0, :].rearrange("(p o) -> p o", o=1), r=[], w=[bfb2.b])
            c.op(c.dve, lambda: nc.vector.memset(onesf[:, :], 1.0), w=[onesf.b])
            QN = [sb(s1, nc, f"QN{i}", [128, 4, 4, 64], BF16) for i in range(2)]
            E = [sb(s1, nc, "Eg", [128, 512], F32)] * 2
            gp = [ps(s1, nc, "gp", [128, 512], F32)] * 2
            fp_ = [ps(s1, nc, "fp", [32, 512], F32)] * 2
            zqk = ps(s1, nc, "zqk", [128, 4, 256], F32)
            zv = ps(s1, nc, "zv", [128, 4, 128], F32)
            tp = ps(s1, nc, "tpa", [128, 4, 4, 128], BF16)
            LV = LIM["stage1"]
            augb = Buf("augb")
            nchs = LIM["nch"] if LV >= 1 else 0
            uts = {}
            if nchs:
                uts[0] = load_uT_chunk(c, nc, G, l, UT, 0)
            for ch in range(nchs):
                if ch + 1 < nchs:
                    uts[ch + 1] = load_uT_chunk(c, nc, G, l, UT, ch + 1)
                ut = uts.pop(ch)
                g_p = gp[ch % 2]
                for k in range(8):
                    c.op(c.pe, lambda k=k: nc.tensor.matmul(g_p[:, :], Wa[:, k, 386:514], ut[:, k, :], start=(k == 0), stop=(k == 7)),
                         r=[Wa.b, ut.b], w=[g_p.b])
                silu_gate(c, nc, K, g_p, E[ch % 2], Gt[:, ch * 512:(ch + 1) * 512], Gt.b)
                for tt in range(4):
                    for k in range(8):
                        c.op(c.pe, lambda k=k, tt=tt: nc.tensor.matmul(zqk[:, tt, :], ut[:, k, tt * 128:(tt + 1) * 128], Wa[:, k, 0:256],
                                                                       start=(k == 0), stop=(k == 7)),
                             r=[Wa.b, ut.b], w=[zqk.b])
                    for k in range(8):
                        c.op(c.pe, lambda k=k, tt=tt: nc.tensor.matmul(zv[:, tt, :], ut[:, k, tt * 128:(tt + 1) * 128], Wa[:, k, 256:384],
                                                                       start=(k == 0), stop=(k == 7)),
                             r=[Wa.b, ut.b], w=[zv.b])
                if LV >= 3:
                    sl = ch % 2
                    f_p = fp_[sl]
                    for k in range(8):
                        c.op(c.pe, lambda k=k: nc.tensor.matmul(f_p[:, :], Wa[:, k, 384:416], ut[:, k, :], start=(k == 0), stop=(k == 7)),
                             r=[Wa.b, ut.b], w=[f_p.b])
                c.op(c.act, lambda: nc.scalar.copy(V[:, 4 * ch:4 * ch + 4, 0:64], zv[:, :, 0:64]), r=[zv.b], w=[V.bs[ch]])
                c.op(c.act, lambda: nc.scalar.copy(V[:, 4 * ch:4 * ch + 4, 192:256], zv[:, :, 64:128]), r=[zv.b], w=[V.bs[ch]])
                qn = QN[ch % 2]
                qk_norm_chunk(c, nc, K, zqk, sq, ss, rs, qf)
                c.op(c.dve, lambda: nc.vector.tensor_tensor(qn[:, :, :, :].rearrange("p t g d -> p t (g d)"),
                                                             qf[:, :].rearrange("p (t n) -> p t n", t=4),
                                                             gq[:, :].unsqueeze(1).to_broadcast([128, 4, 256]), ALU.mult),
                     r=[qf.b, gq.b], w=[qn.b])
                for tt in range(4):
                    for g4 in range(4):
                        c.op(c.pe, lambda g4=g4, tt=tt: nc.tensor.transpose(tp[0:64, tt, g4, :], qn[:, tt, g4, :], K.ident[:, :]),
                             r=[qn.b, K.ident.b], w=[tp.b])
                c.op(c.act, lambda: nc.scalar.copy(QK[0:64, :, ch * 512:(ch + 1) * 512].rearrange("p g (t c) -> p g t c", t=4),
                                                   tp[0:64, :, :, :].rearrange("p t g c -> p g t c")), r=[tp.b], w=[QK.bs[ch]])
                if LV >= 3:
                    sl = ch % 2
                    f_p = fp_[sl]
                    fb, y, sc, cy, r1, r2, spl = [F[k][sl] for k in ("fb", "y", "sc", "cy", "r1", "r2")] + [SPLT[sl]]
                    c.op(c.dve, lambda: nc.vector.tensor_scalar(fb[:, :], f_p[0:2, :], bfb2[:, 0:1], None, ALU.add), r=[f_p.b, bfb2.b], w=[fb.b])
                    if LIM["sub"] < 2:
                        continue
                    c.op(c.act, lambda: nc.scalar.activation(y[:, :], fb[:, :], AF.Exp, scale=-1.0), r=[fb.b], w=[y.b])
                    c.op(c.act, lambda: nc.scalar.activation(y[:, :], y[:, :], AF.Ln, bias=K.one[0:2, 0:1], scale=1.0), r=[y.b, K.one.b], w=[y.b])
                    if LIM["sub"] < 3:
                        continue
                    c.op(c.dve, lambda: nc.vector.tensor_tensor_scan(sc[:, :], onesf[:, :], y[:, :], 0.0, ALU.mult, ALU.add), r=[onesf.b, y.b], w=[sc.b])
                    if ch == 0:
                        c.op(c.dve, lambda: nc.vector.tensor_copy(cy[:, :], sc[:, :]), r=[sc.b], w=[cy.b])
                    else:
                        car = CAR[(ch - 1) % 2]
                        c.op(c.dve, lambda: nc.vector.tensor_scalar(cy[:, :], sc[:, :], car[:, 0:1], None, ALU.add), r=[sc.b, car.b], w=[cy.b])
                    c.op(c.dve, lambda: nc.vector.tensor_copy(CAR[ch % 2][:, :], cy[:, 511:512]), r=[cy.b], w=[CAR[ch % 2].b])
                    if LIM["sub"] < 4:
                        continue
                    c.op(c.dve, lambda: nc.vector.tensor_copy(spl[:, 3, :], cy[:, :]), r=[cy.b], w=[spl.b])
                    c.op(c.dve, lambda: nc.vector.tensor_copy(sc[:, :], spl[:, 3, :]), r=[spl.b], w=[sc.b])
                    c.op(c.dve, lambda: nc.vector.tensor_tensor(r1[:, :], cy[:, :], sc[:, :], ALU.subtract), r=[cy.b, sc.b], w=[r1.b])
                    c.op(c.dve, lambda: nc.vector.tensor_copy(spl[:, 4, :], r1[:, :]), r=[r1.b], w=[spl.b])
                    c.op(c.dve, lambda: nc.vector.tensor_copy(sc[:, :], spl[:, 4, :]), r=[spl.b], w=[sc.b])
                    c.op(c.dve, lambda: nc.vector.tensor_tensor(r2[:, :], r1[:, :], sc[:, :], ALU.subtract), r=[r1.b, sc.b], w=[r2.b])
                    c.op(c.dve, lambda: nc.vector.tensor_copy(spl[:, 5, :], r2[:, :]), r=[r2.b], w=[spl.b])
                    c.op(c.dve, lambda: nc.vector.tensor_scalar(spl[:, 0:3, :], spl[:, 3:6, :], -1.0, None, ALU.mult), r=[spl.b], w=[spl.b])
                    cs_ = slice(ch * 512, (ch + 1) * 512)
                    c.wait_buf(c.sp, QK.bs[ch])
                    for hh in range(2 if LIM["sub"] >= 5 else 0):
                        c.dma(c.sp, QK[64:67, hh, cs_], spl[hh:hh + 1, 0:3, :], r=[spl.b], w=[augb], waw=False)
                        c.dma(c.sp, QK[67:70, 2 + hh, cs_], spl[hh:hh + 1, 3:6, :], r=[spl.b], w=[augb], waw=False)
            c.barrier()
        if not LIM["attn"]:
            return
        with ExitStack() as s2:
            NS, NP = 3, 4
            Sb = [ps(s2, nc, f"Sb{i}", [128, 2, 512], F32) for i in range(NS)]
            Ob = [ps(s2, nc, f"Ob{i}", [128, 512], F32) for i in range(2)]
            P = [sb(s2, nc, f"P{i}", [128, 2, 512], BF16) for i in range(NP)]
            W2 = {k: [sb(s2, nc, f"f{k}{i}", [128, 512], dt) for i in range(2)] for k, dt in
                  [("rl", F32), ("os", F32), ("gf", F32), ("ot", BF16)]}
            units = []
            for h in range(2):
                for I in range(LIM["nch"]):
                    for j in range(0, 4 * I, 2):
                        units.append((h, I, j, 2))
                    for j in range(4 * I, 4 * I + 4):
                        units.append((h, I, j, 1))
            LA = 3

            def emit_S(n):
                h, I, j, cnt_ = units[n]
                Sx, Px = Sb[n % NS], P[n % NP]
                q0 = I * 512
                if cnt_ == 2:
                    for u in range(2):
                        jj = j + u
                        c.op(c.pe, lambda jj=jj, u=u: nc.tensor.matmul(Sx[:, u, :], QK[0:70, 2 + h, jj * 128:(jj + 1) * 128],
                                                                       QK[0:70, h, q0:q0 + 512], start=True, stop=True),
                             r=[QK.bs[jj // 4], QK.bs[I]], w=[Sx.b])
                    c.op(c.act, lambda: nc.scalar.activation(Px[:, :, :], Sx[:, :, :], AF.Exp), r=[Sx.b], w=[Px.b])
                    return
                r_ = j - 4 * I
                lo = r_ * 128
                kT = QK[0:70, 2 + h, j * 128:(j + 1) * 128]
                rdeps = [QK.bs[j // 4], QK.bs[I]]
                c.op(c.pe, lambda: nc.tensor.matmul(Sx[:, 0, lo:lo + 128], kT, QK[0:70, h, q0 + lo:q0 + lo + 128], start=True, stop=False),
                     r=rdeps, w=[Sx.b])
                c.op(c.pe, lambda: nc.tensor.matmul(Sx[:, 0, lo:lo + 128], K.ident[:, :], K.mge[:, :], start=False, stop=True),
                     r=[K.ident.b, K.mge.b], w=[Sx.b])
                if lo + 128 < 512:
                    c.op(c.pe, lambda: nc.tensor.matmul(Sx[:, 0, lo + 128:512], kT, QK[0:70, h, q0 + lo + 128:q0 + 512], start=True, stop=True),
                         r=rdeps, w=[Sx.b])
                c.op(c.act, lambda: nc.scalar.activation(Px[:, 0, lo:512], Sx[:, 0, lo:512], AF.Exp), r=[Sx.b], w=[Px.b])

            def emit_PV(n):
                h, I, j, cnt_ = units[n]
                nj = 4 * I + 4
                it = h * LIM["nch"] + I
                O, Px = Ob[it % 2], P[n % NP]
                if cnt_ == 2:
                    for u in range(2):
                        jj = j + u
                        c.op(c.pe, lambda jj=jj, u=u: nc.tensor.matmul(O[:, 0:512], V[:, jj, h * 128:(h + 1) * 128], Px[:, u, :],
                                                                       start=(jj == 0), stop=False, skip_group_check=True),
                             r=[V.bs[jj // 4], Px.b], w=[O.b])
                    return
                lo = (j - 4 * I) * 128
                c.op(c.pe, lambda: nc.tensor.matmul(O[:, lo:512], V[:, j, h * 128:(h + 1) * 128], Px[:, 0, lo:512],
                                                    start=(j == 0), stop=(j == nj - 1), skip_group_check=True),
                     r=[V.bs[j // 4], Px.b], w=[O.b])
                if j == nj - 1:
                    finalize_out(c, nc, W2, O, h, 512, Gt, I * 512, 0, G, l, I * 512, it % 2, it % 2)

            for n in range(len(units) + LA):
                if n < len(units):
                    emit_S(n)
                if n - LA >= 0:
                    emit_PV(n - LA)
            c.barrier()


def phase_dil(c, nc, K, G, l, ROPE):
    with ExitStack() as st:
        Wb = sb(st, nc, "Wb", [128, 8, 512], BF16)
        c.dma(c.sp, Wb[:, :, :], G["wb16"][l].ap().rearrange("(k p) n -> p k n", p=128), r=[G["wb16_b"][l]], w=[Wb.b])
        gq = sb(st, nc, "gqb", [128, 256], F32)
        c.dma(c.sp, gq[:, :], G["smalls"][:, l * 2560 + 2304:l * 2560 + 2560], r=[], w=[gq.b])
        c.op(c.dve, lambda: nc.vector.tensor_scalar(gq[:, 0:128], gq[:, 0:128], 0.125, None, ALU.mult), r=[gq.b], w=[gq.b])
        QK = sb(st, nc, "QKd", [128, 4, S], BF16, nb=16)
        VT = sb(st, nc, "VTd", [128, S], BF16, nb=16)
        Gt = sb(st, nc, "Gtd", [128, S], BF16)
        with ExitStack() as s1:
            UT = [sb(s1, nc, f"UTd{i}", [128, 8, 512], BF16) for i in range(2)]
            sq = sb(s1, nc, "bsq", [128, 1024], F32)
            qf = sb(s1, nc, "bqf", [128, 1024], F32)
            qg = sb(s1, nc, "bqg", [128, 4, 4, 64], F32)
            ra = sb(s1, nc, "bra", [128, 4, 4, 16], F32)
            rb = sb(s1, nc, "brb", [128, 4, 4, 16], F32)
            ss = sb(s1, nc, "bss", [128, 16], F32)
            rs = sb(s1, nc, "brs", [128, 16], F32)
            QN = [sb(s1, nc, f"QNd{i}", [128, 4, 4, 64], BF16) for i in range(2)]
            E = sb(s1, nc, "Egd", [128, 512], F32)
            gp = ps(s1, nc, "gpd", [128, 512], F32)
            vp = ps(s1, nc, "vpd", [128, 512], F32)
            zqk = ps(s1, nc, "zqkd", [128, 4, 256], F32)
            tp = ps(s1, nc, "tpd", [128, 4, 4, 128], BF16)
            uts = {0: load_uT_chunk(c, nc, G, l, UT, 0)}
            for ch in range(LIM["nch"]):
                if ch + 1 < LIM["nch"]:
                    uts[ch + 1] = load_uT_chunk(c, nc, G, l, UT, ch + 1)
                ut = uts.pop(ch)
                for k in range(8):
                    c.op(c.pe, lambda k=k: nc.tensor.matmul(vp[:, :], Wb[:, k, 256:384], ut[:, k, :], start=(k == 0), stop=(k == 7)),
                         r=[Wb.b, ut.b], w=[vp.b])
                c.op(c.act, lambda: nc.scalar.copy(VT[:, ch * 512:(ch + 1) * 512], vp[:, :]), r=[vp.b], w=[VT.bs[ch]])
                for k in range(8):
                    c.op(c.pe, lambda k=k: nc.tensor.matmul(gp[:, :], Wb[:, k, 384:512], ut[:, k, :], start=(k == 0), stop=(k == 7)),
                         r=[Wb.b, ut.b], w=[gp.b])
                silu_gate(c, nc, K, gp, E, Gt[:, ch * 512:(ch + 1) * 512], Gt.b)
                for tt in range(4):
                    for k in range(8):
                        c.op(c.pe, lambda k=k, tt=tt: nc.tensor.matmul(zqk[:, tt, :], ut[:, k, tt * 128:(tt + 1) * 128], Wb[:, k, 0:256],
                                                                       start=(k == 0), stop=(k == 7)),
                             r=[Wb.b, ut.b], w=[zqk.b])
                qn = QN[ch % 2]
                qk_norm_chunk(c, nc, K, zqk, sq, ss, rs, qf)
                c.op(c.dve, lambda: nc.vector.tensor_tensor(qg[:, :, :, :].rearrange("p t g d -> p t (g d)"),
                                                             qf[:, :].rearrange("p (t n) -> p t n", t=4),
                                                             gq[:, :].unsqueeze(1).to_broadcast([128, 4, 256]), ALU.mult),
                     r=[qf.b, gq.b], w=[qg.b])
                c.op(c.act, lambda: nc.scalar.copy(qn[:, :, :, 16:64], qg[:, :, :, 16:64]), r=[qg.b], w=[qn.b])
                t0 = 4 * ch
                cs = ROPE["cs2"][:, t0:t0 + 4, :].unsqueeze(2).to_broadcast([128, 4, 4, 16])
                sn_lo = ROPE["sn2"][:, t0:t0 + 4, 0:8].unsqueeze(2).to_broadcast([128, 4, 4, 8])
                sn_hi = ROPE["sn2"][:, t0:t0 + 4, 8:16].unsqueeze(2).to_broadcast([128, 4, 4, 8])
                c.op(c.dve, lambda: nc.vector.tensor_tensor(ra[:, :, :, :], qg[:, :, :, 0:16], cs, ALU.mult), r=[qg.b, ROPE["b"]], w=[ra.b])
                c.op(c.dve, lambda: nc.vector.tensor_tensor(rb[:, :, :, 0:8], qg[:, :, :, 8:16], sn_lo, ALU.mult), r=[qg.b, ROPE["b"]], w=[rb.b])
                c.op(c.dve, lambda: nc.vector.tensor_tensor(rb[:, :, :, 8:16], qg[:, :, :, 0:8], sn_hi, ALU.mult), r=[qg.b, ROPE["b"]], w=[rb.b])
                c.op(c.dve, lambda: nc.vector.tensor_tensor(qn[:, :, :, 0:16], ra[:, :, :, :], rb[:, :, :, :], ALU.add), r=[ra.b, rb.b], w=[qn.b])
                for tt in range(4):
                    for g4 in range(4):
                        c.op(c.pe, lambda g4=g4, tt=tt: nc.tensor.transpose(tp[0:64, tt, g4, :], qn[:, tt, g4, :], K.ident[:, :]),
                             r=[qn.b, K.ident.b], w=[tp.b])
                c.op(c.act, lambda: nc.scalar.copy(QK[0:64, :, ch * 512:(ch + 1) * 512].rearrange("p g (t c) -> p g t c", t=4),
                                                   tp[0:64, :, :, :].rearrange("p t g c -> p g t c")), r=[tp.b], w=[QK.bs[ch]])
            c.barrier()
        with ExitStack() as s2:
            NS, NP, NV, NVT = 3, 5, 32, 3
            Sb = [ps(s2, nc, f"Sd{i}", [128, 256], F32) for i in range(NS)]
            Op = [ps(s2, nc, f"Od{i}", [128, 128], F32) for i in range(2)]
            vtp = [ps(s2, nc, f"vtp{i}", [128, 64], BF16) for i in range(NVT)]
            P = [sb(s2, nc, f"Pd{i}", [128, 256], BF16) for i in range(NP)]
            Vd = [sb(s2, nc, f"Vd{i}", [128, 128], BF16) for i in range(NV)]
            ACC = [sb(s2, nc, f"ACC{i}", [128, 2048], F32) for i in range(2)]
            W2 = {k: [sb(s2, nc, f"g{k}{i}", [128, 2048], dt) for i in range(1)] for k, dt in
                  [("rl", F32), ("os", F32), ("gf", F32), ("ot", BF16)]}
            cnt = {"nv": [0, 0], "nvt": 0}
            LA = 3
            NVH = NV // 2
            for k_, v in enumerate(Vd):
                c.op(c.dve, lambda v=v: nc.vector.memset(v[:, :], 1.0), w=[v.b])

            def make_vd(h, start, d):
                hr = slice(h * 64, (h + 1) * 64)
                vcol = slice(0, 64) if h == 0 else slice(64, 128)
                vd = Vd[h * NVH + cnt["nv"][h] % NVH]
                cnt["nv"][h] += 1
                vt = vtp[cnt["nvt"] % NVT]
                cnt["nvt"] += 1
                chs = sorted(set([start // 512, (start + 127 * d) // 512]))
                c.op(c.pe, lambda: nc.tensor.transpose(vt[:, :], VT[hr, start:start + 127 * d + 1:d], K.ident[hr, hr]),
                     r=[VT.bs[x] for x in range(chs[0], chs[-1] + 1)] + [K.ident.b], w=[vt.b])
                c.op(c.act, lambda: nc.scalar.copy(vd[:, vcol], vt[:, :]), r=[vt.b], w=[vd.b])
                return vd

            blocks = []
            for SBk in range(LIM["nsb"]):
                for h in range(2):
                    for d in (1, 4, 16):
                        nbq = 16 // d
                        for r_ in range(d):
                            for ii in range(nbq):
                                blocks.append((SBk, d, r_, ii, nbq, h))
            state = {"vd_prev": None}
            info = {}

            def emit_A(n):
                SBk, d, r_, ii, nbq, h = blocks[n]
                if ii == 0:
                    state["vd_prev"] = None
                i = nbq * SBk + ii
                start = 128 * i * d + r_
                pstart = start - 128 * d
                Sx, Px = Sb[n % NS], P[n % NP]
                c0, c1 = start // 512, (start + 127 * d) // 512
                rd = [QK.bs[x] for x in range(c0, c1 + 1)]
                qT = QK[0:64, h, start:start + 127 * d + 1:d]
                c.op(c.pe, lambda: nc.tensor.matmul(Sx[:, 128:256], QK[0:64, 2 + h, start:start + 127 * d + 1:d], qT, start=True, stop=True),
                     r=rd, w=[Sx.b])
                lo = 128
                vd_prev = state["vd_prev"]
                if i > 0:
                    lo = 0
                    p0, p1 = pstart // 512, (pstart + 127 * d) // 512
                    rdp = rd + [QK.bs[x] for x in range(p0, p1 + 1)]
                    c.op(c.pe, lambda: nc.tensor.matmul(Sx[:, 0:128], QK[0:64, 2 + h, pstart:pstart + 127 * d + 1:d], qT, start=True, stop=True),
                         r=rdp, w=[Sx.b])
                    if vd_prev is None:
                        vd_prev = make_vd(h, pstart, d)
                vd_own = make_vd(h, start, d)
                c.op(c.act, lambda: nc.scalar.activation(Px[:, lo:256], Sx[:, lo:256], AF.Exp), r=[Sx.b], w=[Px.b])
                c.op(c.dve, lambda: nc.vector.tensor_tensor(Px[:, lo:256], Px[:, lo:256], K.m01[:, lo:256], ALU.mult), r=[Px.b, K.m01.b], w=[Px.b])
                info[n] = (i, start, vd_prev, vd_own)
                state["vd_prev"] = vd_own

            def emit_B(n):
                SBk, d, r_, ii, nbq, h = blocks[n]
                i, start, vd_prev, vd_own = info.pop(n)
                Px = P[n % NP]
                O = Op[n % 2]
                acc = ACC[h]
                if i > 0:
                    c.op(c.pe, lambda: nc.tensor.matmul(O[:, :], vd_prev[:, :], Px[:, 0:128], start=True, stop=False),
                         r=[vd_prev.b, Px.b], w=[O.b])
                c.op(c.pe, lambda: nc.tensor.matmul(O[:, :], vd_own[:, :], Px[:, 128:256], start=(i == 0), stop=True),
                     r=[vd_own.b, Px.b], w=[O.b])
                off = start - 2048 * SBk
                av = acc[:, off:off + 127 * d + 1:d]
                if d == 1:
                    c.op(c.dve, lambda: nc.vector.tensor_copy(av, O[:, :]), r=[O.b], w=[acc.b])
                else:
                    c.op(c.dve, lambda: nc.vector.tensor_tensor(av, O[:, :], av, ALU.add), r=[O.b, acc.b], w=[acc.b])
                last = (n + 1 == len(blocks)) or (blocks[n + 1][0] != SBk) or (blocks[n + 1][5] != h)
                if last:
                    finalize_out(c, nc, W2, acc, h, 2048, Gt, SBk * 2048, 128, G, l, SBk * 2048, 0, 2)
                    if h == 1 and LIM["gather"]:
                        gather_o(c, G, l, SBk, 1)

            for n in range(len(blocks) + LA):
                if n < len(blocks):
                    emit_A(n)
                if n - LA >= 0:
                    emit_B(n - LA)
            c.barrier()


def build_rope(c, nc, st, K, G):
    cs2 = sb(st, nc, "cs2", [128, 64, 16], F32)
    sn2 = sb(st, nc, "sn2", [128, 64, 16], F32)
    rb = Buf("rope")
    with ExitStack() as s1:
        pi_ = sb(s1, nc, "posi", [128, 64], I32)
        pf = sb(s1, nc, "posf", [128, 64], F32)
        ang = sb(s1, nc, "ang", [128, 64, 8], F32)
        a2 = sb(s1, nc, "ang2", [128, 64, 8], F32)
        kf = sb(s1, nc, "kf", [128, 64, 8], F32)
        ki = sb(s1, nc, "ki", [128, 64, 8], I32)
        c.dma(c.sp, pi_[:, :], G["pos"][:, :], r=[], w=[pi_.b])
        c.op(c.dve, lambda: nc.vector.tensor_copy(pf[:, :], pi_[:, :]), r=[pi_.b], w=[pf.b])
        c.op(c.dve, lambda: nc.vector.tensor_tensor(ang[:, :, :], pf[:, :].unsqueeze(2).to_broadcast([128, 64, 8]),
                                                     K.invf[:, :].unsqueeze(1).to_broadcast([128, 64, 8]), ALU.mult),
             r=[pf.b, K.invf.b], w=[ang.b])

        def sin_of(dst_ap, shift, scale):
            c.op(c.dve, lambda: nc.vector.tensor_scalar(a2[:, :, :], ang[:, :, :], shift + float(np.pi), None, ALU.add), r=[ang.b], w=[a2.b])
            c.op(c.dve, lambda: nc.vector.tensor_scalar(kf[:, :, :], a2[:, :, :], 1.0 / TWO_PI, None, ALU.mult), r=[a2.b], w=[kf.b])
            c.op(c.dve, lambda: nc.vector.tensor_copy(ki[:, :, :], kf[:, :, :]), r=[kf.b], w=[ki.b])
            c.op(c.dve, lambda: nc.vector.tensor_copy(kf[:, :, :], ki[:, :, :]), r=[ki.b], w=[kf.b])
            c.op(c.dve, lambda: nc.vector.scalar_tensor_tensor(a2[:, :, :], kf[:, :, :], -TWO_PI, a2[:, :, :], ALU.mult, ALU.add), r=[kf.b, a2.b], w=[a2.b])
            c.op(c.dve, lambda: nc.vector.tensor_scalar(kf[:, :, :], a2[:, :, :], 0.0, TWO_PI, ALU.is_lt, ALU.mult), r=[a2.b], w=[kf.b])
            c.op(c.dve, lambda: nc.vector.tensor_tensor(a2[:, :, :], a2[:, :, :], kf[:, :, :], ALU.add), r=[a2.b, kf.b], w=[a2.b])
            c.op(c.dve, lambda: nc.vector.tensor_scalar(kf[:, :, :], a2[:, :, :], TWO_PI, -TWO_PI, ALU.is_ge, ALU.mult), r=[a2.b], w=[kf.b])
            c.op(c.dve, lambda: nc.vector.tensor_tensor(a2[:, :, :], a2[:, :, :], kf[:, :, :], ALU.add), r=[a2.b, kf.b], w=[a2.b])
            c.op(c.act, lambda: nc.scalar.activation(a2[:, :, :], a2[:, :, :], AF.Sin, bias=K.negpi[:, 0:1], scale=1.0), r=[a2.b, K.negpi.b], w=[a2.b])
            c.op(c.dve, lambda: nc.vector.tensor_scalar(dst_ap, a2[:, :, :], scale, None, ALU.mult), r=[a2.b], w=[rb])

        sin_of(cs2[:, :, 0:8], float(np.pi / 2), 1.0)
        sin_of(cs2[:, :, 8:16], float(np.pi / 2), 1.0)
        sin_of(sn2[:, :, 0:8], 0.0, -1.0)
        sin_of(sn2[:, :, 8:16], 0.0, 1.0)
        c.barrier()
    return {"cs2": cs2, "sn2": sn2, "b": rb}


def phase_b(c, nc, K, G, l, last):
    with ExitStack() as st:
        Wo = sb(st, nc, "Wo", [128, 8, 1024], BF16)
        Wg = sb(st, nc, "Wg", [128, 8, 1024], BF16)
        Wp = sb(st, nc, "Wp", [128, 2, 1024], BF16)
        PT = sb(st, nc, "PT", [128, 2, TSH], BF16)
        c.dma(c.sp, Wo[:, :, :], G["wo16"][l].ap().rearrange("(k p) n -> p k n", p=128), r=[G["wo16_b"][l]], w=[Wo.b])
        c.dma(c.sp, Wg[:, :, :], G["wg16"][l].ap().rearrange("(k p) n -> p k n", p=128), r=[G["wg16_b"][l]], w=[Wg.b])
        c.dma(c.sp, Wp[:, :, :], G["wp16"][l].ap().rearrange("(k p) n -> p k n", p=128), r=[G["wp16_b"][l]], w=[Wp.b])
        c.dma(c.sp, PT[:, :, :], G["pT16"][l].ap().rearrange("(k p) t -> p k t", p=128), r=[G["pT16_b"][l]], w=[PT.b])
        gple = sb(st, nc, "gple", [128, 1024], F32)
        c.dma(c.sp, gple[:, :], G["smalls"][:, l * 2560 + 1024:l * 2560 + 2048], r=[], w=[gple.b])
        gnext = None
        if not last:
            gnext = sb(st, nc, "gnext", [128, 1024], F32)
            c.dma(c.sp, gnext[:, :], G["smalls"][:, (l + 1) * 2560:(l + 1) * 2560 + 1024], r=[], w=[gnext.b])
            nt = NormT(c, nc, st, K, "nb")
        OT = [sb(st, nc, f"OTi{i}", [128, 8, 512], BF16) for i in range(2)]
        H = [sb(st, nc, f"H{i}", [128, 1024], F32) for i in range(2)]
        H2 = [sb(st, nc, f"H2{i}", [128, 1024], F32) for i in range(2)]
        sq = sb(st, nc, "bsq", [128, 1024], BF16)
        ss = [sb(st, nc, f"bss{i}", [128, 1], F32) for i in range(2)]
        rs = [sb(st, nc, f"brs{i}", [128, 1], F32) for i in range(2)]
        Vn = [sb(st, nc, f"Vn{i}", [128, 1024], BF16) for i in range(2)]
        VTs = [sb(st, nc, f"VTs{i}", [128, 1024], BF16) for i in range(2)]
        Eg = [sb(st, nc, f"Egb{i}", [128, 1024], F32) for i in range(2)]
        TM = [sb(st, nc, f"TM{i}", [128, 1024], F32) for i in range(2)]
        X = [[ps(st, nc, f"X{i}{hf}", [128, 512], F32) for hf in range(2)] for i in range(2)]
        Gp = [ps(st, nc, f"Gp{i}", [128, 512], F32) for i in range(2)]
        vtp = ps(st, nc, "vtpb", [128, 1024], BF16)
        hsrc = G["x_sh"] if l == 0 else G["h_loc"].ap()
        hsrc_b = [Buf("xin")] if l == 0 else G["h_loc_b"]

        def load_ot(ch):
            o_ = OT[ch % 2]
            src = G["oT_mine"][l].ap().rearrange("(q p) t -> p q t", p=128)[:, :, ch * 512:(ch + 1) * 512]
            c.dma(c.sp, o_[:, :, :], src, r=[G["oT_mine_b"][l]], w=[o_.b])
            return o_

        def load_h(t_):
            h_ = H[t_ % 2]
            c.dma(c.sp, h_[:, :], hsrc[t_ * 128:(t_ + 1) * 128, :], r=hsrc_b, w=[h_.b])
            return h_

        ots = {0: load_ot(0)}

        def stage_a(t):
            i = t % 2
            ot = ots[t // 4]
            if t % 4 == 1 and t // 4 + 1 < 4:
                ots[t // 4 + 1] = load_ot(t // 4 + 1)
            h = load_h(t)
            tt = t % 4
            for half in range(2):
                a = X[i][half]
                for kc in range(8):
                    c.op(c.pe, lambda kc=kc: nc.tensor.matmul(a[:, :], ot[:, kc, tt * 128:(tt + 1) * 128],
                                                              Wo[:, kc, half * 512:(half + 1) * 512],
                                                              start=(kc == 0), stop=(kc == 7)),
                         r=[ot.b, Wo.b], w=[a.b])
                c.op(c.dve, lambda a=a, half=half: nc.vector.tensor_tensor(h[:, half * 512:(half + 1) * 512], a[:, :], h[:, half * 512:(half + 1) * 512], ALU.add),
                     r=[a.b, h.b], w=[h.b])
            c.op(c.act, lambda: nc.scalar.activation(sq[:, :], h[:, :], AF.Square, accum_out=ss[i][:, 0:1]), r=[h.b], w=[sq.b, ss[i].b])
            rstd_from_ss(c, nc, ss[i], rs[i], 1, K, 1.0 / D)
            vn, vts = Vn[i], VTs[i]
            c.op(c.dve, lambda: nc.vector.scalar_tensor_tensor(vn[:, :], h[:, :], rs[i][:, 0:1], gple[:, :], ALU.mult, ALU.mult),
                 r=[h.b, rs[i].b, gple.b], w=[vn.b])
            for k in range(8):
                c.op(c.pe, lambda k=k: nc.tensor.transpose(vtp[:, k * 128:(k + 1) * 128], vn[:, k * 128:(k + 1) * 128], K.ident[:, :]),
                     r=[vn.b, K.ident.b], w=[vtp.b])
            c.op(c.act, lambda: nc.scalar.copy(vts[:, :], vtp[:, :]), r=[vtp.b], w=[vts.b])

        def stage_b(t):
            i = t % 2
            h, vts, eg, tm = H[i], VTs[i], Eg[i], TM[i]
            for half in range(2):
                g_ = Gp[half]
                for kc in range(8):
                    c.op(c.pe, lambda kc=kc: nc.tensor.matmul(g_[:, :], vts[:, kc * 128:(kc + 1) * 128], Wg[:, kc, half * 512:(half + 1) * 512],
                                                              start=(kc == 0), stop=(kc == 7)),
                         r=[vts.b, Wg.b], w=[g_.b])
                c.op(c.act, lambda g_=g_, half=half: nc.scalar.activation(eg[:, half * 512:(half + 1) * 512], g_[:, :], AF.Exp, scale=-1.0),
                     r=[g_.b], w=[eg.b])
            c.op(c.act, lambda: nc.scalar.activation(eg[:, :], eg[:, :], AF.Ln, bias=K.one[:, 0:1], scale=1.0), r=[eg.b, K.one.b], w=[eg.b])
            c.op(c.act, lambda: nc.scalar.activation(eg[:, :], eg[:, :], AF.Exp, scale=-1.0), r=[eg.b], w=[eg.b])
            for half in range(2):
                p_ = X[i][half]
                for kc in range(2):
                    c.op(c.pe, lambda kc=kc: nc.tensor.matmul(p_[:, :], PT[:, kc, t * 128:(t + 1) * 128], Wp[:, kc, half * 512:(half + 1) * 512],
                                                              start=(kc == 0), stop=(kc == 1)),
                         r=[PT.b, Wp.b], w=[p_.b])
                c.op(c.dve, lambda p_=p_, half=half: nc.vector.tensor_tensor(tm[:, half * 512:(half + 1) * 512], p_[:, :], eg[:, half * 512:(half + 1) * 512], ALU.mult),
                     r=[p_.b, eg.b], w=[tm.b])
            h2 = H2[i]
            c.op(c.dve, lambda: nc.vector.tensor_tensor(h2[:, :], h[:, :], tm[:, :], ALU.add), r=[h.b, tm.b], w=[h2.b])
            if last:
                c.dma(c.sp, G["out"][t * 128:(t + 1) * 128, :], h2[:, :], r=[h2.b], w=[G["out_b"][i]], waw=False)
            else:
                c.dma(c.sp, G["h_loc"].ap()[t * 128:(t + 1) * 128, :], h2[:, :], r=[h2.b], w=[G["h_loc_b"][i]], waw=False)
                nt.emit(h2[:, :], h2.b, gnext, t, G["uT_loc"][l + 1], G["uT_loc_b"][l + 1], G["uT_all"][l + 1], G["uT_all_b"][l + 1])

        stage_a(0)
        for t in range(16):
            if t + 1 < 16:
                stage_a(t + 1)
            stage_b(t)
        c.barrier()


def build_program(n_layers=2, debug=False, stop=None):
    _UID[0] = 0
    nc = bass.Bass("TRN2", target_bir_lowering=False)
    G = {}
    ei = lambda n, s_, d: nc.dram_tensor(n, s_, d, kind="ExternalInput").ap()
    G["x_sh"] = ei("x_sh", [TSH, D], F32)
    G["pT"] = ei("pT", [2, 256, TSH], F32)
    G["pos"] = ei("pos", [128, 64], I32)
    G["w_a"] = ei("w_a", [2, D, 514], F32)
    G["w_b"] = ei("w_b", [2, D, 512], F32)
    G["w_out"] = ei("w_out", [2, D, D], F32)
    G["w_gate"] = ei("w_gate", [2, D, D], F32)
    G["w_ple"] = ei("w_ple", [2, 256, D], F32)
    G["smalls"] = ei("smalls", [128, 5120], F32)
    G["consts"] = ei("consts", [128, 392], F32)
    G["bf"] = ei("bf", [2, 128, 2], F32)
    G["out"] = nc.dram_tensor("out", [TSH, D], F32, kind="ExternalOutput").ap()
    G["out_b"] = [Buf("out0", True), Buf("out1", True)]
    if debug:
        G["dbg_o"] = nc.dram_tensor("dbg_o", [256, S], BF16, kind="ExternalOutput").ap()
        G["dbg_u"] = nc.dram_tensor("dbg_u", [8 * 128, TSH], BF16, kind="ExternalOutput").ap()
    for nm, shp, src in (("wa16", [D, 514], "w_a"), ("wb16", [D, 512], "w_b"), ("wo16", [D, D], "w_out"),
                         ("wg16", [D, D], "w_gate"), ("wp16", [256, D], "w_ple"), ("pT16", [256, TSH], "pT")):
        G[nm] = [nc.dram_tensor(f"{nm}_{l}", shp, BF16) for l in range(2)]
        G[nm + "_b"] = [Buf(f"{nm}{l}", True) for l in range(2)]
        G[nm + "_src"] = src
    uT_loc = [nc.dram_tensor(f"uT_loc_{ch}", [8 * 128, 512], BF16) for ch in range(4)]
    uT_all = [nc.dram_tensor(f"uT_all_{ch}", [4 * 8 * 128, 512], BF16) for ch in range(4)]
    oT_loc = nc.dram_tensor("oT_loc", [4 * 256, TSH], BF16)
    oT_all = nc.dram_tensor("oT_all", [4 * 4 * 256, TSH], BF16)
    oT_mine = nc.dram_tensor("oT_mine", [4 * 256, TSH], BF16)
    G["uT_loc"] = [uT_loc, uT_loc]
    G["uT_all"] = [uT_all, uT_all]
    G["oT_loc"] = [oT_loc, oT_loc]
    G["oT_all"] = [oT_all, oT_all]
    G["oT_mine"] = [oT_mine, oT_mine]
    G["h_loc"] = nc.dram_tensor("h_loc", [TSH, D], F32)
    G["h_loc_b"] = [Buf("h_loc0", True), Buf("h_loc1", True)]
    b1 = [Buf(f"uTl{i}", True) for i in range(4)]
    G["uT_loc_b"] = [b1, b1]
    b2 = [Buf(f"uTa{i}", True) for i in range(4)]
    G["uT_all_b"] = [b2, b2]
    b3 = [[Buf(f"oTl{q}_{i}", True) for i in range(3)] for q in range(4)]
    G["oT_loc_b"] = [b3, b3]
    b4 = [[Buf(f"oTa{q}_{y}", True) for y in range(2)] for q in range(4)]
    G["oT_all_b"] = [b4, b4]
    G["oT_mine_b"] = [Buf("oTmine0", True), Buf("oTmine1", True)]
    with ExitStack() as st:
        c = Ctx(nc, st)
        def precast(names, l):
            for nm in names:
                c.dma(c.pool, G[nm][l].ap()[:, :], G[G[nm + "_src"]][l], r=[], w=[G[nm + "_b"][l]])

        precast(["wa16"], 0)
        K = load_consts(c, nc, st, G)
        ROPE = build_rope(c, nc, st, K, G)
        if stop not in ("foxsim", "dilsim", "bsim"):
            phase_norm0(c, nc, K, G)
            c.recycle()
        for l in range(n_layers):
            if stop == "norm0":
                break
            if stop in ("foxsim", "dilsim", "bsim"):
                precast(["wb16", "wo16", "wg16", "wp16", "pT16"], 0)
            if stop == "foxsim":
                phase_fox(c, nc, K, G, l)
                break
            if stop == "dilsim":
                phase_dil(c, nc, K, G, l, ROPE)
                break
            if stop == "bsim":
                phase_b(c, nc, K, G, l, last=False)
                break
            if stop == "ag":
                break
            if l == 0:
                precast(["wb16", "wo16", "wg16", "wp16", "pT16"], 0)
                if n_layers > 1:
                    precast(["wa16", "wb16", "wo16", "wg16", "wp16", "pT16"], 1)
            phase_fox(c, nc, K, G, l)
            c.recycle()
            for q in range(4):
                gather_o(c, G, l, q, 0)
            if stop == "fox":
                break
            phase_dil(c, nc, K, G, l, ROPE)
            c.recycle()
            if stop == "dil":
                break
            rank = nc.gpsimd.partition_id() % 4
            src = G["oT_all"][l].ap()[bass.ds(rank * 1024, 1024), :]
            c.dma(c.pool, G["oT_mine"][l].ap()[:, :], src, r=[b_ for qq in G["oT_all_b"][l] for b_ in qq], w=[G["oT_mine_b"][l]])
            phase_b(c, nc, K, G, l, last=(l == 1))
            c.recycle()
        if n_layers == 1 and stop is None:
            c.dma(c.pool, G["out"][:, :], G["h_loc"].ap()[:, :], r=G["h_loc_b"], w=[G["out_b"][0]])
        if debug:
            db = Buf("dbg", True)
            if stop == "dil":
                for q in range(4):
                    c.dma(c.pool, G["dbg_o"][:, q * TSH:(q + 1) * TSH], G["oT_loc"][0].ap()[q * 256:(q + 1) * 256, :], r=G["oT_loc_b"][0][q], w=[db])
            if stop is not None:
                for ch in range(4):
                    c.dma(c.pool, G["dbg_u"][:, ch * 512:(ch + 1) * 512], G["uT_loc"][0][ch].ap()[:, :], r=[G["uT_loc_b"][0][ch]], w=[db])
            c.wait_buf(c.sp, db)
        G["nsem"] = c.nsem
        for ob in G["out_b"]:
            c.wait_buf(c.sp, ob)
            c.wait_buf(c.pool, ob)
    return nc


def make_consts():
    cst = np.zeros((128, 392), np.float32)
    s_ = np.arange(128)[:, None]
    t_ = np.arange(128)[None, :]
    cst[:, 0:128] = (s_ == t_)
    cst[:, 128:256] = np.where(t_ >= s_, 0.0, NEG)
    cst[:, 256:384] = np.where(t_ <= s_, 0.0, NEG)
    cst[:, 384:392] = (500000.0 ** (-np.arange(8, dtype=np.float32) / 8.0))[None, :]
    return cst


def make_in_maps(x, p, positions, norm_g, w_in, b_f, qk_norm_g, w_out, w_ple, ple_norm_g, w_ple_gate):
    f = lambda a: np.ascontiguousarray(np.asarray(a, dtype=np.float32))
    x, p, norm_g, w_in, b_f, qk_norm_g, w_out, w_ple, ple_norm_g, w_ple_gate = map(
        f, (x, p, norm_g, w_in, b_f, qk_norm_g, w_out, w_ple, ple_norm_g, w_ple_gate))
    positions = np.asarray(positions).astype(np.int32)
    cst = make_consts()
    smalls = np.zeros((128, 5120), np.float32)
    for l in range(2):
        o = l * 2560
        smalls[:, o:o + 1024] = norm_g[l][None]
        smalls[:, o + 1024:o + 2048] = ple_norm_g[l][None]
        g = qk_norm_g[l]
        smalls[:, o + 2048:o + 2304] = np.concatenate([g[0], g[0], g[1], g[1]])[None]
        smalls[:, o + 2304:o + 2560] = np.concatenate([g[2], g[2], g[3], g[3]])[None]
    maps = []
    for core in range(NCORES):
        b, r = core // 4, core % 4
        hs = slice(128 * r, 128 * r + 128)
        cols_a = np.concatenate([np.arange(0, 512)[hs], np.arange(512, 1024)[hs], np.arange(1024, 1536)[hs],
                                 np.array([2048 + 2 * r, 2048 + 2 * r + 1]), np.arange(1536, 2048)[hs]])
        base = 2056
        cols_b = np.concatenate([base + np.arange(0, 512)[hs], base + np.arange(512, 1024)[hs],
                                 base + np.arange(1024, 1536)[hs], base + np.arange(1536, 2048)[hs]])
        m = {
            "x_sh": np.ascontiguousarray(x[b, r * TSH:(r + 1) * TSH]),
            "pT": np.ascontiguousarray(p[:, b, r * TSH:(r + 1) * TSH, :].transpose(0, 2, 1)),
            "pos": np.ascontiguousarray(positions[b].reshape(64, 128).T),
            "w_a": np.ascontiguousarray(w_in[:, :, cols_a]),
            "w_b": np.ascontiguousarray(w_in[:, :, cols_b]),
            "w_out": w_out, "w_gate": w_ple_gate, "w_ple": w_ple,
            "smalls": smalls, "consts": cst,
            "bf": np.ascontiguousarray(np.broadcast_to(b_f[:, None, 2 * r:2 * r + 2], (2, 128, 2))),
        }
        maps.append(m)
    return maps


_NC_CACHE = {}


def kernel(x, p, positions, norm_g, w_in, b_f, qk_norm_g, w_out, w_ple, ple_norm_g, w_ple_gate):
    maps = make_in_maps(x, p, positions, norm_g, w_in, b_f, qk_norm_g, w_out, w_ple, ple_norm_g, w_ple_gate)
    nc = build_program()
    res = run_bass_kernel_spmd(nc, maps, core_ids=list(range(NCORES)))
    out = np.empty((2, S, D), np.float32)
    for core in range(NCORES):
        b, r = core // 4, core % 4
        out[b, r * TSH:(r + 1) * TSH] = res.results[core]["out"]
    return out
```
